# Optimizing a Trainium2 kernel written in Bass

```python
import jax, jax.numpy as jnp
from jax import lax
import numpy as np

D_MODEL = 1024
BATCH = 32
SEQ = 2048
DEPTH = 1

CHUNK = 64
HEAD_DIM = 64
MIX_WIDTH = D_MODEL
A_WIDTH = MIX_WIDTH // 2
B_WIDTH = MIX_WIDTH - A_WIDTH
A_HEADS = A_WIDTH // HEAD_DIM
A_KV_HEADS = max(1, A_HEADS // 4)
B_HEADS = B_WIDTH // HEAD_DIM
WINDOW = 128
A_PREV_CHUNKS = WINDOW // CHUNK
B_PREV_CHUNKS = 8
MAX_REL = 128
PLE_DIM = 256
RMS_EPS = 1e-6
NEG_BIG = -1e30

A_Q = A_HEADS * HEAD_DIM
A_KV = A_KV_HEADS * HEAD_DIM
PROJ_SIZES = (A_Q, A_KV, A_KV, A_WIDTH, B_WIDTH, B_WIDTH, B_WIDTH, B_WIDTH)
D_IN_PROJ = sum(PROJ_SIZES)
SPLIT_POINTS = [int(v) for v in np.cumsum(PROJ_SIZES)[:-1]]

kernel_name = "hybrid_chunk_swa_sink_relbias_ple"


def rmsnorm(x, g):
    xf = x.astype(jnp.float32)
    var = jnp.mean(xf * xf, axis=-1, keepdims=True)
    return (xf * lax.rsqrt(var + RMS_EPS)).astype(x.dtype) * g


def band_rel(n_prev):
    band = (n_prev + 1) * CHUNK
    qi = jnp.arange(CHUNK, dtype=jnp.int32)[:, None]
    kj = jnp.arange(band, dtype=jnp.int32)[None, :]
    return qi - kj + n_prev * CHUNK


def alibi_bias(n_heads, n_prev):
    slopes = jnp.asarray(2.0 ** (-8.0 * np.arange(1, n_heads + 1) / n_heads), dtype=jnp.float32)
    dist = jnp.abs(band_rel(n_prev)).astype(jnp.float32)
    return -slopes[:, None, None] * dist[None]


def rel_position_bias(table, n_prev):
    idx = jnp.clip(band_rel(n_prev), -MAX_REL, MAX_REL) + MAX_REL
    return table.astype(jnp.float32)[:, idx]


def chunk_band_attention(q, k, v, n_prev, bias, sink):
    b, s, hq, d = q.shape
    hkv = k.shape[2]
    grp = hq // hkv
    nc = s // CHUNK
    band = (n_prev + 1) * CHUNK
    pad = n_prev * CHUNK
    kp = jnp.pad(k, ((0, 0), (pad, 0), (0, 0), (0, 0)))
    vp = jnp.pad(v, ((0, 0), (pad, 0), (0, 0), (0, 0)))
    qc = q.reshape(b, nc, CHUNK, hkv, grp, d)
    bias_g = bias.reshape(hkv, grp, CHUNK, band)
    scale = HEAD_DIM ** -0.5
    if sink is not None:
        sink_g = sink.astype(jnp.float32).reshape(hkv, grp, 1, 1)

    def one_chunk(c):
        qb = lax.dynamic_index_in_dim(qc, c, axis=1, keepdims=False)
        kb = lax.dynamic_slice_in_dim(kp, c * CHUNK, band, axis=1)
        vb = lax.dynamic_slice_in_dim(vp, c * CHUNK, band, axis=1)
        sc = jnp.einsum('bqkgd,bskd->bkgqs', qb, kb).astype(jnp.float32) * scale + bias_g
        valid = jnp.arange(band) >= pad - c * CHUNK
        sc = jnp.where(valid, sc, NEG_BIG)
        m = jnp.max(sc, axis=-1, keepdims=True)
        if sink is not None:
            m = jnp.maximum(m, sink_g)
            e = jnp.exp(sc - m)
            denom = jnp.sum(e, axis=-1, keepdims=True) + jnp.exp(sink_g - m)
        else:
            e = jnp.exp(sc - m)
            denom = jnp.sum(e, axis=-1, keepdims=True)
        w = (e / denom).astype(vb.dtype)
        out = jnp.einsum('bkgqs,bskd->bqkgd', w, vb)
        return out.reshape(b, CHUNK, hq * d)

    out = lax.map(one_chunk, jnp.arange(nc))
    return jnp.transpose(out, (1, 0, 2, 3)).reshape(b, s, hq * d)


def setup_inputs(seed: int = 0) -> dict:
    key = jax.random.key(seed)
    ks = jax.random.split(key, 12)
    f32 = jnp.float32
    x = jax.random.normal(ks[0], (BATCH, SEQ, D_MODEL), f32)
    p = jax.random.normal(ks[1], (DEPTH, BATCH, SEQ, PLE_DIM), f32)
    norm_g = 1.0 + 0.02 * jax.random.normal(ks[2], (DEPTH, D_MODEL), f32)
    w_in = jax.random.normal(ks[3], (DEPTH, D_MODEL, D_IN_PROJ), f32) * D_MODEL ** -0.5
    sink_a = 0.5 * jax.random.normal(ks[4], (DEPTH, A_HEADS), f32)
    rel_bias_b = 0.1 * jax.random.normal(ks[5], (DEPTH, B_HEADS, 2 * MAX_REL + 1), f32)
    w_out = jax.random.normal(ks[6], (DEPTH, MIX_WIDTH, D_MODEL), f32) * MIX_WIDTH ** -0.5
    ple_norm_g = 1.0 + 0.02 * jax.random.normal(ks[7], (DEPTH, D_MODEL), f32)
    w_ple_proj = jax.random.normal(ks[8], (DEPTH, PLE_DIM, D_MODEL), f32) * PLE_DIM ** -0.5
    w_ple_gate = jax.random.normal(ks[9], (DEPTH, D_MODEL, D_MODEL), f32) * D_MODEL ** -0.5
    final_norm_g = 1.0 + 0.02 * jax.random.normal(ks[10], (D_MODEL,), f32)
    return {"x": x, "p": p, "norm_g": norm_g, "w_in": w_in, "sink_a": sink_a,
            "rel_bias_b": rel_bias_b, "w_out": w_out, "ple_norm_g": ple_norm_g,
            "w_ple_proj": w_ple_proj, "w_ple_gate": w_ple_gate, "final_norm_g": final_norm_g}


def reference(x, p, norm_g, w_in, sink_a, rel_bias_b, w_out, ple_norm_g, w_ple_proj, w_ple_gate, final_norm_g):
    b, s, _ = x.shape
    bias_a = alibi_bias(A_HEADS, A_PREV_CHUNKS)
    h = x
    for i in range(DEPTH):
        u = rmsnorm(h, norm_g[i])
        z = u @ w_in[i]
        qa, ka, va, ga, qb, kb, vb, gb = jnp.split(z, SPLIT_POINTS, axis=-1)
        ya = chunk_band_attention(
            qa.reshape(b, s, A_HEADS, HEAD_DIM),
            ka.reshape(b, s, A_KV_HEADS, HEAD_DIM),
            va.reshape(b, s, A_KV_HEADS, HEAD_DIM),
            A_PREV_CHUNKS, bias_a, sink_a[i])
        bias_b = rel_position_bias(rel_bias_b[i], B_PREV_CHUNKS)
        yb = chunk_band_attention(
            qb.reshape(b, s, B_HEADS, HEAD_DIM),
            kb.reshape(b, s, B_HEADS, HEAD_DIM),
            vb.reshape(b, s, B_HEADS, HEAD_DIM),
            B_PREV_CHUNKS, bias_b, None)
        y = jnp.concatenate([ya * jax.nn.silu(ga), yb * jax.nn.silu(gb)], axis=-1)
        h = h + y @ w_out[i]
        gate = jax.nn.sigmoid(rmsnorm(h, ple_norm_g[i]) @ w_ple_gate[i])
        h = h + (p[i] @ w_ple_proj[i]) * gate
    return rmsnorm(h, final_norm_g)
```

```python
import numpy as np
from contextlib import ExitStack
import concourse.bass as bass
import concourse.mybir as mybir
from concourse.bass_utils import run_bass_kernel_spmd

F32 = mybir.dt.float32
BF16 = mybir.dt.bfloat16
ALU = mybir.AluOpType
AF = mybir.ActivationFunctionType

NCORES = 8
D = 1024
SEQ = 2048
SEQ_PER_CORE = 4
TOK = SEQ_PER_CORE * SEQ
TB = 256
NT = TB // 128
NBLK = TOK // TB
BLK_PER_SEQ = SEQ // TB
XR = 8
PLE = 256
EPS = 1e-6
NFM = 22
WALL = NFM * 128 + 640
STG = 3072
MASK_NEG = -200.0
RESCHED = True
EMUL_ENG = "pool"


class _Op:
    __slots__ = ("eng", "fn", "deps", "dma", "idx", "signal", "sigval", "dma_val", "need_eng", "need_dma", "force")


class Prog:
    def __init__(self):
        self.ops = []
        self.last_writer = {}
        self.readers = {}

    def add(self, eng, fn, reads=(), writes=(), dma=None, after=()):
        idx = len(self.ops)
        deps = {}
        for k in reads:
            w = self.last_writer.get(k)
            if w is not None:
                deps[w] = True
        for k in writes:
            w = self.last_writer.get(k)
            if w is not None:
                deps.setdefault(w, False)
            for r in self.readers.get(k, ()):
                deps.setdefault(r, False)
        op = _Op()
        op.eng, op.fn, op.deps, op.dma, op.idx = eng, fn, deps, dma, idx
        op.signal = False
        op.sigval = 0
        op.dma_val = 0
        op.force = tuple(after)
        self.ops.append(op)
        for k in reads:
            self.readers.setdefault(k, []).append(idx)
        for k in writes:
            self.last_writer[k] = idx
            self.readers[k] = []
        return idx

    def finalize(self):
        ops = self.ops
        dma_count = {}
        for op in ops:
            if op.dma is not None:
                dma_count[op.dma] = dma_count.get(op.dma, 0) + 1
                op.dma_val = 16 * dma_count[op.dma]
        for op in ops:
            need_eng, need_dma = {}, {}
            for d, raw in op.deps.items():
                p = ops[d]
                if p.dma is not None:
                    if p.dma_val > need_dma.get(p.dma, 0):
                        need_dma[p.dma] = p.dma_val
                else:
                    if p.eng == op.eng and (op.eng == "pe" or not raw):
                        continue
                    if d > need_eng.get(p.eng, -1):
                        need_eng[p.eng] = d
            for d in op.force:
                if d > need_eng.get(ops[d].eng, -1):
                    need_eng[ops[d].eng] = d
            op.need_eng, op.need_dma = need_eng, need_dma
            for d in need_eng.values():
                ops[d].signal = True
        cnt = {}
        for op in ops:
            if op.dma is None and op.signal:
                cnt[op.eng] = cnt.get(op.eng, 0) + 1
                op.sigval = cnt[op.eng]
        self.dma_total = {k: 16 * v for k, v in dma_count.items()}

    def emit_engine(self, eng_name, e, eng_sems, dma_sems, final_dma_waits=()):
        ops = self.ops
        waited = {}
        for op in ops:
            if op.eng != eng_name:
                continue
            for pe_name, d in op.need_eng.items():
                v = ops[d].sigval
                key = ("e", pe_name)
                if waited.get(key, 0) < v:
                    e.wait_ge(eng_sems[pe_name], v)
                    waited[key] = v
            for k, v in op.need_dma.items():
                key = ("d", k)
                if waited.get(key, 0) < v:
                    e.wait_ge(dma_sems[k], v)
                    waited[key] = v
            ins = op.fn(e)
            if op.dma is not None:
                ins.then_inc(dma_sems[op.dma], 16)
            elif op.signal:
                ins.then_inc(eng_sems[eng_name], 1)
        for k in final_dma_waits:
            e.wait_ge(dma_sems[k], self.dma_total[k])


def _free_elems(ap):
    n = 1
    for d in ap.shape[1:]:
        n *= d
    return n


class _Rec:
    __slots__ = ("kind", "F", "info")

    def then_inc(self, *a, **k):
        return self


class _CostEng:
    def __init__(self):
        self.last = None

    def _r(self, kind, F, **info):
        r = _Rec()
        r.kind, r.F, r.info = kind, F, info
        self.last = r
        return r

    def matmul(self, out, lhsT, rhs, start=None, stop=None, skip_group_check=False, tile_position=None, **kw):
        return self._r("mm", _free_elems(rhs), K=lhsT.shape[0], M=_free_elems(lhsT),
                       rowbase=lhsT.base_partition(), colbase=out.base_partition())

    def transpose(self, out, in_, ident):
        return self._r("tr", 128)

    def activation(self, out, in_, func, bias=0.0, scale=1.0, accum_out=None, **kw):
        return self._r("act", _free_elems(in_), func=str(func), accum=accum_out is not None)

    def tensor_tensor(self, out, in0, in1, op, **kw):
        ps = ("bank" in in0.name) or ("bank" in in1.name)
        return self._r("tt", _free_elems(out), bf16=("bfloat16" in str(out.dtype)) and not ps)

    def tensor_copy(self, out, in_, **kw):
        return self._r("cp", _free_elems(out))

    def tensor_single_scalar(self, out, in_, scalar, op, **kw):
        return self._r("ts", _free_elems(out))

    def scalar_tensor_tensor(self, out, in0, scalar, in1, op0, op1, **kw):
        return self._r("stt", _free_elems(out))

    def tensor_scalar(self, out, in0, scalar1, scalar2, op0, op1=None, **kw):
        return self._r("ts", _free_elems(out))

    def reciprocal(self, out, in_, **kw):
        return self._r("rcp", _free_elems(out))

    def memset(self, ap, c):
        return self._r("ms", _free_elems(ap))

    def dma_start(self, out, in_, **kw):
        n = 1
        for d in out.shape:
            n *= d
        return self._r("dma", n)


def _op_dur(eng, r, state):
    k = r.kind
    if eng == "pe":
        if k == "tr":
            cfg, d = (128, 128), 56.0
        else:
            K_, M_, N_ = r.info["K"], r.info["M"], r.F
            cfg = (64 if 2 < K_ <= 64 else (32 if K_ <= 2 else 128), 64 if M_ <= 64 else 128)
            d = max(N_, 64) * 0.455 + 8
            prev = state.get("prevmm")
            if (prev is not None and prev["cfg"] == cfg and cfg != (128, 128) and cfg[0] != 32
                    and (prev["rb"], prev["cb"]) != (r.info["rowbase"], r.info["colbase"]) and not prev.get("paired")):
                state["prevmm"] = dict(cfg=cfg, rb=r.info["rowbase"], cb=r.info["colbase"], paired=True)
                return 4.0
        sw = 100.0 if state.get("cfg") not in (None, cfg) else 0.0
        state["cfg"] = cfg
        state["prevmm"] = dict(cfg=cfg, rb=r.info.get("rowbase", 0), cb=r.info.get("colbase", 0)) if k == "mm" else None
        return d + sw
    if eng == "act":
        if k == "dma":
            return 60.0
        return 0.83 * r.F + 100 + (93 if r.info.get("accum") else 0)
    if eng == "dve":
        if k == "rcp":
            return 5.0 * r.F + 20
        if k == "tt" and r.info.get("bf16"):
            return 60 + 0.6 * r.F
        if k in ("tt", "stt", "cp"):
            return 70 + 1.2 * r.F
        return 60 + 0.6 * r.F
    if eng == "pool":
        return 900.0 if k == "dma" else 2.45 * r.F
    return 60.0


def _list_schedule(P, fixed=("sp", "pool"), SEM=70.0):
    ops = P.ops
    n = len(ops)
    engs_c = {}
    recs = []
    for op in ops:
        ce = engs_c.setdefault(op.eng, _CostEng())
        op.fn(ce)
        recs.append(ce.last)
    st = {}
    dur1 = [_op_dur(op.eng, recs[i], st.setdefault(op.eng, {})) for i, op in enumerate(ops)]
    node_of = [0] * n
    nodes = []
    for i, op in enumerate(ops):
        if op.eng == "pe" and nodes and ops[nodes[-1][-1]].eng == "pe" and nodes[-1][-1] == i - 1:
            nodes[-1].append(i)
        else:
            nodes.append([i])
        node_of[i] = len(nodes) - 1
    m = len(nodes)
    neng = [ops[nd[0]].eng for nd in nodes]
    ndur = [sum(dur1[i] for i in nd) for nd in nodes]
    nlat = [(2000.0 + recs[nd[0]].F * 4 / 200.0) if ops[nd[0]].dma is not None else 0.0 for nd in nodes]
    preds = [set() for _ in range(m)]
    for i, op in enumerate(ops):
        a = node_of[i]
        for d in list(op.deps.keys()) + list(op.force):
            b = node_of[d]
            if b != a:
                preds[a].add(b)
    last = {}
    for a in range(m):
        if neng[a] in fixed:
            if neng[a] in last:
                preds[a].add(last[neng[a]])
            last[neng[a]] = a
    succs = [[] for _ in range(m)]
    for a in range(m):
        for p in preds[a]:
            succs[p].append(a)
    prio = [0.0] * m
    for a in range(m - 1, -1, -1):
        mx = 0.0
        for s in succs[a]:
            if prio[s] > mx:
                mx = prio[s]
        prio[a] = ndur[a] + nlat[a] + mx
    npred = [len(p) for p in preds]
    ready_t = [0.0] * m
    avail = {}
    for a in range(m):
        if npred[a] == 0:
            avail.setdefault(neng[a], []).append(a)
    free = {}
    engs = sorted(set(neng))
    order_nodes = []
    while len(order_nodes) < m:
        best = None
        for e in engs:
            h = avail.get(e)
            if not h:
                continue
            t = free.get(e, 0.0)
            rdy = [a for a in h if ready_t[a] <= t]
            if rdy:
                a = max(rdy, key=lambda a: (prio[a], -a))
                s = t
            else:
                a = min(h, key=lambda a: (ready_t[a], -prio[a]))
                s = ready_t[a]
            if best is None or s < best[0] or (s == best[0] and prio[a] > prio[best[2]]):
                best = (s, e, a)
        s, e, a = best
        avail[e].remove(a)
        free[e] = s + ndur[a]
        fin = s + ndur[a] + nlat[a]
        order_nodes.append(a)
        for sc in succs[a]:
            npred[sc] -= 1
            same = neng[sc] == e and nlat[a] == 0.0
            r = fin + (0.0 if same else SEM)
            if r > ready_t[sc]:
                ready_t[sc] = r
            if npred[sc] == 0:
                avail.setdefault(neng[sc], []).append(sc)
    order = [i for a in order_nodes for i in nodes[a]]
    newpos = {old: new for new, old in enumerate(order)}
    new_ops = [ops[i] for i in order]
    for new, op in enumerate(new_ops):
        op.idx = new
        op.deps = {newpos[d]: raw for d, raw in op.deps.items()}
        op.force = tuple(newpos[d] for d in op.force)
    P.ops = new_ops


def build_program(nblk=NBLK, stages=("head", "attn", "tail"), dbg=False):
    nc = bass.Bass("TRN2", target_bir_lowering=False)

    def din(name, shape):
        return nc.dram_tensor(name, shape, F32, kind="ExternalInput").ap()

    x_d = din("x", [TOK, D])
    p_d = din("p", [TOK, PLE])
    w_in_d = din("w_in_r", [D, WALL])
    g1_d = din("g1", [128, 8])
    w_out_d = din("w_out", [D, D])
    w_gate_d = din("w_gate", [D, D])
    g2_d = din("g2", [128, 8])
    w_pp_d = din("w_pp", [PLE, D])
    gfin_d = din("gfin", [128, D])
    biasA_d = din("biasA", [128, 8 * 256])
    biasB_d = din("biasB", [128, 8 * 640])
    sink_d = din("sinkrow", [1, 512])
    ident_d = din("ident", [128, 128])
    out_d = nc.dram_tensor("out", [TOK, D], F32, kind="ExternalOutput").ap()

    P = Prog()
    es = ExitStack()
    with es:
        def sb(name, shape, dt):
            return es.enter_context(nc.sbuf_tensor(name, shape, dt))

        def ps(name, shape, dt):
            return es.enter_context(nc.psum_tensor(name, shape, dt))

        Wall = sb("Wall", [128, 8, WALL], BF16)
        Wout = sb("Wout", [128, 8, D], BF16)
        Wg = sb("Wg", [128, 8, D], BF16)
        Wp = sb("Wp", [128, 2, D], BF16)
        gfin = sb("gfin_t", [128, D], F32)
        EA = sb("EA", [128, 8, 256], BF16)
        EB = sb("EB", [128, 8, 640], BF16)
        g1 = sb("g1_t", [128, 8], F32)
        g2 = sb("g2_t", [128, 8], F32)
        ident = sb("ident_t", [128, 128], BF16)
        ones2 = sb("ones2", [128, 64], BF16)
        onesrow = sb("onesrow", [2, 256], BF16)
        skhl = sb("skhl", [2, 512], BF16)
        xring = sb("xring", [128, XR * D], F32)
        junk = sb("junk", [128, D], mybir.dt.float8e4)
        ubuf = [sb(f"ubuf{i}", [128, D], BF16) for i in range(2)]
        uT = sb("uT", [128, 8, TB], BF16)
        Qa = [sb(f"Qa{i}", [128, 4, TB], BF16) for i in range(2)]
        Qb = [sb(f"Qb{i}", [128, 4, TB], BF16) for i in range(2)]
        GT = [sb(f"GT{i}", [128, 8, TB], BF16) for i in range(2)]
        KaT = sb("KaT", [128, 2, 1024], BF16)
        KbT = sb("KbT", [128, 4, 1024], BF16)
        Vr = sb("Vr", [128, 8, 640], BF16)
        pbf = [sb(f"pbf{i}", [128, NT, PLE], BF16) for i in range(2)]
        pT = sb("pT", [128, 2, TB], BF16)
        Pt = [sb(f"Pt{i}", [128, 2, 2, 256], BF16) for i in range(3)]
        Rn = [sb("Rn0", [128, 256], F32)] * 2
        tgs = [sb("tgs0", [128, 512], F32)] * 2
        tnh = [sb(f"tnh{i}", [128, 512], BF16) for i in range(2)]
        skhi, sklo = tnh[0][0:1, :], tnh[1][0:1, :]
        ss = sb("ss", [128, 3 * NT], F32)
        sd = sb("sd", [128, 3 * NT], F32)
        rs = sb("rs", [128, 3 * NT], F32)

        banks = [ps(f"bank{i}", [128, 512], F32) for i in range(8)]
        Sb = banks[0:4]
        Ab = banks[4:6]
        Gb = banks[6:8]
        Tb = banks[7][:, :].bitcast(BF16).rearrange("p (a b) -> p a b", a=8)
        NGB = 2

        eng_names = ["pe", "act", "dve", "pool"]
        eng_sems = {n: es.enter_context(nc.semaphore("sem_" + n)) for n in eng_names}
        dma_keys = ([f"xl{i}" for i in range(XR)] + [f"xs{i}" for i in range(XR)]
                    + ["p0", "p1", "stg0", "stg1", "wout", "wp", "gfin", "ident", "g1", "g2", "sink"])
        dma_sems = {k: es.enter_context(nc.semaphore("dsem_" + k)) for k in dma_keys}

        gb_rot = [0]

        def next_gb():
            b = gb_rot[0] % NGB
            gb_rot[0] += 1
            return b

        def dma(eng, key, out, in_, reads, writes):
            P.add(eng, lambda e, o=out, i=in_: e.dma_start(out=o, in_=i), reads, writes, dma=key)

        dma("pool", "ident", ident[:], ident_d, [], [("ident",)])
        dma("sp", "g1", g1[:], g1_d, [], [("g1",)])
        dma("sp", "g2", g2[:], g2_d, [], [("g2",)])
        sk32 = tgs[0][0:1, :]
        dma("sp", "sink", sk32, sink_d, [], [("sk32",), ("tgs", 0)])
        dma("sp", "gfin", gfin[:], gfin_d, [], [("gfin",)])
        P.add("dve", lambda e: e.memset(ones2[:], 2.0), [], [("ones2",)])
        P.add("dve", lambda e: e.memset(onesrow[:], 1.0), [], [("onesrow",)])

        stg_i = [0]

        def staged(src_ap, ncols, consume):
            s = stg_i[0] % 2
            stg_i[0] += 1
            st_ap = xring[:, s * STG: s * STG + ncols]
            dma("sp", f"stg{s}", st_ap, src_ap, [], [("stg", s)])
            consume(st_ap, s)

        flip = [0]

        def scale_cast(out_ap, in_ap, scal_ap, s, extra_reads, writes):
            if flip[0] % 2 == 0:
                P.add("dve", lambda e: e.tensor_single_scalar(out=out_ap, in_=in_ap, scalar=scal_ap, op=ALU.mult),
                      [("stg", s)] + extra_reads, writes)
            else:
                P.add("act", lambda e: e.activation(out=out_ap, in_=in_ap, func=AF.Copy, scale=scal_ap),
                      [("stg", s)] + extra_reads, writes)
            flip[0] += 1

        for c in range(8):
            for hp in range(2):
                c0 = hp * (WALL // 2)
                n = WALL // 2
                staged(w_in_d[c * 128:(c + 1) * 128, c0:c0 + n], n,
                       lambda st, s, c=c, c0=c0, n=n: scale_cast(Wall[:, c, c0:c0 + n], st, g1[:, c:c + 1], s,
                                                                 [("g1",)], [("Wall", c, c0)]))
        for c in range(8):
            staged(w_gate_d[c * 128:(c + 1) * 128, :], D,
                   lambda st, s, c=c: scale_cast(Wg[:, c, :], st, g2[:, c:c + 1], s, [("g2",)], [("Wg", c)]))
        for hh in range(2):
            def consA(st, s, hh=hh):
                for q in range(4):
                    h = hh * 4 + q
                    P.add("act", lambda e, h=h, q=q: e.activation(out=EA[:, h, :], in_=st[:, q * 256:(q + 1) * 256], func=AF.Exp),
                          [("stg", s)], [("EA", h)])
            staged(biasA_d[:, hh * 1024:(hh + 1) * 1024], 1024, consA)
        for hh in range(2):
            def consB(st, s, hh=hh):
                for q in range(4):
                    h = hh * 4 + q
                    P.add("act", lambda e, h=h, q=q: e.activation(out=EB[:, h, :], in_=st[:, q * 640:(q + 1) * 640], func=AF.Exp),
                          [("stg", s)], [("EB", h)])
            staged(biasB_d[:, hh * 2560:(hh + 1) * 2560], 2560, consB)
        dma("pool", "wout", Wout[:], w_out_d.rearrange("(c p) n -> p c n", p=128), [], [("Wout",)])
        dma("pool", "wp", Wp[:], w_pp_d.rearrange("(c p) n -> p c n", p=128), [], [("Wp",)])
        P.add("act", lambda e: e.activation(out=sk32, in_=sk32, func=AF.Exp), [("sk32",)], [("sk32",), ("tgs", 0)])
        P.add("dve", lambda e: e.tensor_single_scalar(out=sk32, in_=sk32, scalar=2.0, op=ALU.mult), [("sk32",)], [("sk32",), ("tgs", 0)])
        P.add("dve", lambda e: e.tensor_copy(out=skhi, in_=sk32), [("sk32",)], [("skhi",), ("tnh", 0)])
        P.add("dve", lambda e: e.tensor_tensor(out=sklo, in0=sk32, in1=skhi, op=ALU.subtract), [("sk32",), ("skhi",), ("tgs", 0)], [("sklo",), ("tnh", 1)])
        dma_sems["skc"] = es.enter_context(nc.semaphore("dsem_skc"))
        dma("sp", "skc", skhl[0:1, :], skhi, [("skhi",), ("tnh", 0)], [("skhl", 0)])
        dma("sp", "skc", skhl[1:2, :], sklo, [("sklo",), ("tnh", 1)], [("skhl", 1)])

        wall_keys = [("Wall", c, c0) for c in range(8) for c0 in (0, WALL // 2)]
        wg_keys = [("Wg", c) for c in range(8)]

        def xslot_keys(slot):
            ks = [("x", slot)]
            lo, hi = slot * D, (slot + 1) * D
            for s in range(2):
                if lo < (s + 1) * STG and hi > s * STG:
                    ks.append(("stg", s))
            return ks

        def load_x_tile(J, t):
            g = J * NT + t
            slot = g % XR
            dma("sp", f"xl{slot}", xring[:, slot * D:(slot + 1) * D], x_d[g * 128:(g + 1) * 128, :],
                [], xslot_keys(slot))

        def load_x(J):
            for t in range(NT):
                load_x_tile(J, t)

        EPS_AP = sb("eps_t", [128, 1], F32)
        P.add("dve", lambda e: e.memset(EPS_AP[:], EPS), [], [("eps",)])
        u2T = sb("u2T", [128, 8, TB], BF16)

        def xtile(J, t):
            g = J * NT + t
            slot = g % XR
            return xring[:, slot * D:(slot + 1) * D], slot, g

        def sq_stats(J, t, col0):
            xs, slot, g = xtile(J, t)
            P.add("act", lambda e: e.activation(out=junk[:], in_=xs, func=AF.Square, accum_out=ss[:, col0 + t:col0 + t + 1], saturate=False),
                  [("x", slot)], [("junk",), ("ss", col0 + t)])

        nit = sb("nit", [128, 3 * NT], mybir.dt.int32)
        nta = sb("nta", [128, 3 * NT], F32)
        ntb = sb("ntb", [128, 3 * NT], F32)

        def sqrt_recip(col0, ncol):
            groups = sorted({(cc // NT) * NT for cc in range(col0, col0 + ncol)})
            sl = slice(col0, col0 + ncol)
            kk = lambda nm: [(nm, gq) for gq in groups]
            P.add("dve", lambda e: e.tensor_scalar(out=sd[:, sl], in0=ss[:, sl], scalar1=1.0 / D, scalar2=EPS, op0=ALU.mult, op1=ALU.add),
                  [("ss", cc) for cc in range(col0, col0 + ncol)], kk("sd"))
            P.add("dve", lambda e: e.tensor_single_scalar(out=nit[:, sl], in_=sd[:, sl].bitcast(mybir.dt.int32), scalar=1,
                                                         op=ALU.logical_shift_right), kk("sd"), kk("nit"))
            P.add("dve", lambda e: e.tensor_scalar(out=nit[:, sl], in0=nit[:, sl], scalar1=-1, scalar2=0x5f3759df,
                                                  op0=ALU.mult, op1=ALU.add), kk("nit"), kk("nit"))
            P.add("dve", lambda e: e.tensor_copy(out=rs[:, sl], in_=nit[:, sl].bitcast(F32)), kk("nit"), kk("rs"))
            for _ in range(3):
                P.add("dve", lambda e: e.tensor_tensor(out=nta[:, sl], in0=rs[:, sl], in1=rs[:, sl], op=ALU.mult), kk("rs"), kk("nta"))
                P.add("dve", lambda e: e.scalar_tensor_tensor(out=ntb[:, sl], in0=nta[:, sl], scalar=-0.5, in1=sd[:, sl],
                                                             op0=ALU.mult, op1=ALU.mult), kk("nta") + kk("sd"), kk("ntb"))
                P.add("dve", lambda e: e.scalar_tensor_tensor(out=rs[:, sl], in0=ntb[:, sl], scalar=1.5, in1=rs[:, sl],
                                                             op0=ALU.add, op1=ALU.mult), kk("ntb") + kk("rs"), kk("rs"))

        def scale_u(J, t, col0):
            xs, slot, g = xtile(J, t)
            us = t % 2
            P.add("act", lambda e: e.activation(out=ubuf[us][:], in_=xs, func=AF.Copy, scale=rs[:, col0 + t:col0 + t + 1]),
                  [("x", slot), ("rs", col0)], [("u", us)])

        def transposes_u(t, dstT, dkey):
            us = t % 2
            for cc in range(8):
                P.add("pe", lambda e, cc=cc: e.transpose(Tb[:, cc, :], ubuf[us][:, cc * 128:(cc + 1) * 128], ident[:]),
                      [("u", us), ("ident",)], [("psG", 1)])
            P.add("dve", lambda e: e.tensor_copy(out=dstT[:, :, t * 128:(t + 1) * 128], in_=Tb[:, 0:8, :]),
                  [("psG", 1)], [(dkey, t)])

        def transposes_p(J, t):
            par = J % 2
            for cc in range(2):
                P.add("pe", lambda e, cc=cc: e.transpose(Tb[:, cc, :], pbf[par][:, t, cc * 128:(cc + 1) * 128], ident[:]),
                      [("pbf", par), ("ident",)], [("psG", 1)])
            P.add("dve", lambda e: e.tensor_copy(out=pT[:, :, t * 128:(t + 1) * 128], in_=Tb[:, 0:2, :]),
                  [("psG", 1)], [("pT", t)])

        def head_io(J, do_x=True, do_p=True):
            par = J % 2
            if do_x and J + 1 < nblk:
                load_x(J + 1)
            if do_p:
                dma("pool", f"p{par}", pbf[par][:], p_d[J * TB:(J + 1) * TB, :].rearrange("(t q) f -> q t f", q=128),
                    [], [("pbf", par)])

        FM = [("Qa", 0, 0), ("Qa", 2, 2), ("Ka", 0, 4), ("Ga", 0, 6), ("Ga", 2, 8), ("Qb", 0, 10), ("Qb", 2, 12),
              ("Kb", 0, 14), ("Kb", 2, 16), ("Gb", 0, 18), ("Gb", 2, 20)]
        uT_keys = [("uT", t) for t in range(NT)]

        def proj_fm(J, n_ev):
            par = J % 2
            ringcol = (J * TB) % 1024
            rt0 = (J * NT) % 8
            kind, ci, f0 = FM[n_ev]
            gb = next_gb()
            for i in range(2):
                for cc in range(8):
                    P.add("pe", lambda e, i=i, cc=cc: e.matmul(
                        Gb[gb][:, i * TB:(i + 1) * TB], Wall[:, cc, (f0 + i) * 128:(f0 + i + 1) * 128], uT[:, cc, :],
                        start=(cc == 0), stop=(cc == 7)), wall_keys + uT_keys, [("psG", gb)])
            src3 = Gb[gb][:, :].rearrange("p (a b) -> p a b", a=2)
            if kind in ("Qa", "Qb", "Ka", "Kb"):
                if kind in ("Qa", "Qb"):
                    dst = (Qa if kind == "Qa" else Qb)[par][:, ci:ci + 2, :]
                    wk = [(kind, par, ci), (kind, par, ci + 1)]
                else:
                    dst = (KaT if kind == "Ka" else KbT)[:, ci:ci + 2, ringcol:ringcol + TB]
                    wk = [(kind, ci, (rt0 + t) % 8) for t in range(NT)]
                if n_ev % 2 == 0:
                    P.add("dve", lambda e: e.tensor_copy(out=dst, in_=src3), [("psG", gb)], wk)
                else:
                    P.add("act", lambda e: e.activation(out=dst, in_=src3, func=AF.Copy), [("psG", gb)], wk)
            else:
                gc = ci + (0 if kind == "Ga" else 4)
                ts_ = n_ev % 2
                tn3 = tnh[ts_][:, :].rearrange("p (a b) -> p a b", a=2)
                P.add("act", lambda e: e.activation(out=tn3, in_=src3, func=AF.Tanh, scale=0.5), [("psG", gb)], [("tnh", ts_)])
                dst = GT[par][:, gc:gc + 2, :]
                P.add("dve", lambda e: e.scalar_tensor_tensor(out=dst, in0=tn3, scalar=1.0, in1=src3, op0=ALU.add, op1=ALU.mult),
                      [("psG", gb), ("tnh", ts_)], [("GT", par, gc), ("GT", par, gc + 1)])

        def proj_v(J, t, which):
            rt = ((J * NT) % 8 + t) % 8
            gb = next_gb()
            if which == 0:
                for cc in range(8):
                    P.add("pe", lambda e, cc=cc: e.matmul(
                        Gb[gb][:, :], uT[:, cc, t * 128:(t + 1) * 128], Wall[:, cc, NFM * 128:NFM * 128 + 512],
                        start=(cc == 0), stop=(cc == 7)), wall_keys + [("uT", t)], [("psG", gb)])
                P.add("act", lambda e: e.activation(out=Vr[:, rt, 0:512], in_=Gb[gb][:, :], func=AF.Copy), [("psG", gb)], [("Vb", rt)])
            else:
                for cc in range(8):
                    P.add("pe", lambda e, cc=cc: e.matmul(
                        Gb[gb][:, 0:128], uT[:, cc, t * 128:(t + 1) * 128], Wall[:, cc, NFM * 128 + 512:NFM * 128 + 640],
                        start=(cc == 0), stop=(cc == 7)), wall_keys + [("uT", t)], [("psG", gb)])
                P.add("dve", lambda e: e.tensor_copy(out=Vr[:, rt, 512:640], in_=Gb[gb][:, 0:128]), [("psG", gb)], [("Va", rt)])

        def wout(J, t, half):
            par = J % 2
            xs, slot, g = xtile(J, t)
            gb = next_gb()
            for cc in range(8):
                ysrc = (Qa if cc < 4 else Qb)[par][:, cc % 4, t * 128:(t + 1) * 128]
                yk = [("Qa" if cc < 4 else "Qb", par, cc % 4)]
                P.add("pe", lambda e, cc=cc, ysrc=ysrc: e.matmul(
                    Gb[gb][:, :], ysrc, Wout[:, cc, half * 512:(half + 1) * 512], start=(cc == 0), stop=(cc == 7)),
                    yk + [("Wout",)], [("psG", gb)])
            P.add("dve", lambda e: e.tensor_tensor(out=xs[:, half * 512:(half + 1) * 512], in0=Gb[gb][:, :],
                                                  in1=xs[:, half * 512:(half + 1) * 512], op=ALU.add),
                  [("psG", gb), ("x", slot)], [("x", slot)])

        def gate(J, t, half):
            xs, slot, g = xtile(J, t)
            gbg = next_gb()
            for cc in range(8):
                P.add("pe", lambda e, cc=cc: e.matmul(
                    Gb[gbg][:, :], u2T[:, cc, t * 128:(t + 1) * 128], Wg[:, cc, half * 512:(half + 1) * 512],
                    start=(cc == 0), stop=(cc == 7)), [("u2T", t)] + wg_keys, [("psG", gbg)])
            ti = 0
            P.add("act", lambda e: e.activation(out=tgs[ti][:], in_=Gb[gbg][:, :], func=AF.Tanh, scale=0.5),
                  [("psG", gbg)], [("tgs", ti)])
            gbp = next_gb()
            for cc in range(2):
                P.add("pe", lambda e, cc=cc: e.matmul(
                    Gb[gbp][:, :], pT[:, cc, t * 128:(t + 1) * 128], Wp[:, cc, half * 512:(half + 1) * 512],
                    start=(cc == 0), stop=(cc == 1)), [("pT", t), ("Wp",)], [("psG", gbp)])
            P.add("dve", lambda e: e.scalar_tensor_tensor(out=tgs[ti][:], in0=tgs[ti][:], scalar=1.0, in1=Gb[gbp][:, :],
                                                         op0=ALU.add, op1=ALU.mult),
                  [("tgs", ti), ("psG", gbp)], [("tgs", ti)])
            P.add("dve", lambda e: e.scalar_tensor_tensor(out=xs[:, half * 512:(half + 1) * 512], in0=tgs[ti][:], scalar=0.5,
                                                         in1=xs[:, half * 512:(half + 1) * 512], op0=ALU.mult, op1=ALU.add),
                  [("tgs", ti), ("x", slot)], [("x", slot)])

        def final(J, t):
            xs, slot, g = xtile(J, t)
            P.add("dve", lambda e: e.scalar_tensor_tensor(out=xs, in0=xs, scalar=rs[:, 2 * NT + t:2 * NT + t + 1], in1=gfin[:],
                                                         op0=ALU.mult, op1=ALU.mult),
                  [("x", slot), ("rs", 2 * NT), ("gfin",)], [("x", slot)])
            dma("sp", f"xs{slot}", out_d[g * 128:(g + 1) * 128, :], xs, [("x", slot)], [])

        def head(J):
            head_io(J)
            for t in range(NT):
                sq_stats(J, t, 0)
            sqrt_recip(0, NT)
            yield
            for t in range(NT):
                scale_u(J, t, 0)
                transposes_u(t, uT, "uT")
                yield
            for n_ev in range(len(FM)):
                proj_fm(J, n_ev)
                yield
            for t in range(NT):
                for which in range(2):
                    proj_v(J, t, which)
                    yield

        def tail(J):
            for t in range(NT):
                for half in range(2):
                    wout(J, t, half)
                    yield
                sq_stats(J, t, NT)
            sqrt_recip(NT, NT)
            yield
            for t in range(NT):
                scale_u(J, t, NT)
                transposes_u(t, u2T, "u2T")
                yield
                transposes_p(J, t)
                yield
            for t in range(NT):
                for half in range(2):
                    gate(J, t, half)
                    yield
                sq_stats(J, t, 2 * NT)
            sqrt_recip(2 * NT, NT)
            yield
            for t in range(NT):
                final(J, t)
                yield

        def stream2(J):
            H = J + 2
            if J + 3 < nblk:
                load_x(J + 3)
            for t in range(NT):
                for half in range(2):
                    wout(J, t, half)
                    yield
                sq_stats(J, t, NT)
            for t in range(NT):
                sq_stats(H, t, 0)
            sqrt_recip(0, 2 * NT)
            yield
            for t in range(NT):
                scale_u(H, t, 0)
            for t in range(NT):
                transposes_p(J, t)
                yield "switch"
            head_io(H, do_x=False)
            for t in range(NT):
                transposes_u(t, uT, "uT")
                yield "switch"
            for t in range(NT):
                scale_u(J, t, NT)
            for n_ev in range(0, 3):
                proj_fm(H, n_ev)
                yield
            for t in range(NT):
                transposes_u(t, u2T, "u2T")
                yield "switch"
            for n_ev in range(3, 6):
                proj_fm(H, n_ev)
                yield
            rest_h = [lambda n_ev=n_ev: proj_fm(H, n_ev) for n_ev in range(6, len(FM))] + \
                     [lambda t=t, which=which: proj_v(H, t, which) for t in range(NT) for which in range(2)]
            gates = [(t, half) for t in range(NT) for half in range(2)]
            gi = 0
            for i, fn in enumerate(rest_h):
                fn()
                yield
                if i % 2 == 1 and gi < len(gates):
                    t, half = gates[gi]
                    gate(J, t, half)
                    if half == 1:
                        sq_stats(J, t, 2 * NT)
                    gi += 1
                    yield
            while gi < len(gates):
                t, half = gates[gi]
                gate(J, t, half)
                if half == 1:
                    sq_stats(J, t, 2 * NT)
                gi += 1
                yield
            sqrt_recip(2 * NT, NT)
            yield
            for t in range(NT):
                final(J, t)
                yield

        pair_ctr = [0]
        unit_ctr = [0]

        def attn(J):
            par = J % 2
            s = J // BLK_PER_SEQ
            jl = J % BLK_PER_SEQ
            units = []
            for mixer in ("A", "B"):
                for j in range(4):
                    if mixer == "A":
                        kps, span = range(2 * jl - 1, 2 * jl + 2), 256
                    else:
                        kps, span = range(2 * jl - 4, 2 * jl + 2), 640
                    ul = []
                    for kp in kps:
                        if kp < 0:
                            continue
                        q_lo = max(128 * kp, TB * jl)
                        q_hi = min(128 * kp + span, TB * jl + TB)
                        if q_hi <= q_lo:
                            continue
                        ul.append((kp, q_lo - 128 * kp, q_hi - q_lo, q_lo - TB * jl))
                    pid = pair_ctr[0]
                    pair_ctr[0] += 1
                    for i, u in enumerate(ul):
                        uid = unit_ctr[0]
                        unit_ctr[0] += 1
                        units.append(dict(mixer=mixer, j=j, kp=u[0], r0=u[1], w=u[2], c0=u[3],
                                          first=(i == 0), last=(i == len(ul) - 1), pid=pid, uid=uid))

            sus = [units[i:i + 2] for i in range(0, len(units), 2)]

            def qk(m):
                X, Y = Sb[2 * (m % 2)], Sb[2 * (m % 2) + 1]
                for i, u in enumerate(sus[m]):
                    mixer, j, w, c0 = u["mixer"], u["j"], u["w"], u["c0"]
                    rt = (s * 16 + u["kp"]) % 8
                    if mixer == "A":
                        kv = j // 2
                        Kt = KaT[:, kv, rt * 128:(rt + 1) * 128]
                        Qt = Qa[par][:, j, c0:c0 + w]
                        rk = [("Ka", 0, rt), ("Qa", par, j)]
                    else:
                        Kt = KbT[:, j, rt * 128:(rt + 1) * 128]
                        Qt = Qb[par][:, j, c0:c0 + w]
                        rk = [("Kb", (j // 2) * 2, rt), ("Qb", par, j)]
                    for hh, bank in ((0, X), (1, Y)):
                        P.add("pe", lambda e, hh=hh, bank=bank, Kt=Kt, Qt=Qt, i=i, w=w: e.matmul(
                            bank[:, i * 256:i * 256 + w], Kt[hh * 64:(hh + 1) * 64, :], Qt[hh * 64:(hh + 1) * 64, :],
                            start=True, stop=True), rk, [("psS", 2 * (m % 2) + hh)])

            def front(m):
                su = sus[m]
                pt = m % 3
                nu = len(su)
                wmax = max(u["w"] for u in su)
                pk = [("Pt", pt, i) for i in range(nu)]
                for hh in range(2):
                    bank3 = Sb[2 * (m % 2) + hh][:, :].rearrange("p (a b) -> p a b", a=2)
                    bk = [("psS", 2 * (m % 2) + hh)]
                    P.add("act", lambda e, hh=hh, bank3=bank3: e.activation(
                        out=Pt[pt][:, 0:nu, hh, 0:wmax], in_=bank3[:, 0:nu, 0:wmax], func=AF.Exp, scale=0.125), bk, pk)
                for i, u in enumerate(su):
                    mixer, j, w, r0 = u["mixer"], u["j"], u["w"], u["r0"]
                    Et = EA if mixer == "A" else EB
                    ek = [("EA" if mixer == "A" else "EB", 2 * j), ("EA" if mixer == "A" else "EB", 2 * j + 1)]
                    P.add(EMUL_ENG, lambda e, i=i, j=j, w=w, r0=r0, Et=Et: e.tensor_tensor(
                        out=Pt[pt][:, i, :, 0:w], in0=Pt[pt][:, i, :, 0:w], in1=Et[:, 2 * j:2 * j + 2, r0:r0 + w], op=ALU.mult),
                        [("Pt", pt, i)] + ek, [("Pt", pt, i)])

            def back(m):
                pt = m % 3
                for i, u in enumerate(sus[m]):
                    back_unit(u, Pt[pt], i, pt)

            def back_unit(u, Ptile, i, pt):
                mixer, j, w, c0 = u["mixer"], u["j"], u["w"], u["c0"]
                rt = (s * 16 + u["kp"]) % 8
                ab = u["pid"] % 2
                if mixer == "A":
                    kv = j // 2
                    V0 = Vr[:, rt, 512 + kv * 64:512 + kv * 64 + 64]
                    V1 = V0
                    vk = [("Va", rt)]
                else:
                    V0 = Vr[:, rt, (2 * j) * 64:(2 * j + 1) * 64]
                    V1 = Vr[:, rt, (2 * j + 1) * 64:(2 * j + 2) * 64]
                    vk = [("Vb", rt)]
                first = u["first"]
                acc = Ab[ab]
                P.add("pe", lambda e: e.matmul(acc[0:64, c0:c0 + w], V0, Ptile[:, i, 0, 0:w], start=first, stop=True,
                                               skip_group_check=True, tile_position=(0, 0)),
                      [("Pt", pt, i)] + vk, [("psA", ab)])
                P.add("pe", lambda e: e.matmul(acc[64:128, c0:c0 + w], V1, Ptile[:, i, 1, 0:w], start=first, stop=True,
                                               skip_group_check=True, tile_position=(0, 64)),
                      [("Pt", pt, i)] + vk, [("psA", ab)])
                P.add("pe", lambda e: e.matmul(acc[0:64, 256 + c0:256 + c0 + w], ones2[:, 0:64], Ptile[:, i, 0, 0:w],
                                               start=False, stop=True, skip_group_check=True, tile_position=(0, 0)),
                      [("Pt", pt, i), ("ones2",)], [("psA", ab)])
                P.add("pe", lambda e: e.matmul(acc[64:128, 256 + c0:256 + c0 + w], ones2[:, 0:64], Ptile[:, i, 1, 0:w],
                                               start=False, stop=True, skip_group_check=True, tile_position=(0, 64)),
                      [("Pt", pt, i), ("ones2",)], [("psA", ab)])
                if not u["last"]:
                    return
                if mixer == "A":
                    P.add("pe", lambda e: e.matmul(acc[:, 256:512], skhl[0:2, j * 128:(j + 1) * 128], onesrow[0:2, :],
                                                   start=False, stop=True, skip_group_check=True),
                          [("skhl", 0), ("skhl", 1), ("onesrow",)], [("psA", ab)])
                rn = 0
                gc = j if mixer == "A" else 4 + j
                ydst = (Qa if mixer == "A" else Qb)[par][:, j, :]
                yk = [("Qa" if mixer == "A" else "Qb", par, j)]
                P.add("dve", lambda e: e.reciprocal(out=Rn[rn][:], in_=acc[:, 256:512]), [("psA", ab)], [("Rn", rn)])
                P.add("dve", lambda e: e.tensor_tensor(out=Rn[rn][:], in0=Rn[rn][:], in1=GT[par][:, gc, :], op=ALU.mult),
                      [("Rn", rn), ("GT", par, gc)], [("Rn", rn)])
                P.add("dve", lambda e: e.tensor_tensor(out=ydst, in0=acc[:, 0:256], in1=Rn[rn][:], op=ALU.mult),
                      [("psA", ab), ("Rn", rn)], yk)

            M = len(sus)
            LAG = 2
            qk(0)
            for m in range(M + LAG):
                if m < M:
                    front(m)
                if LAG <= m:
                    back(m - LAG)
                if m + 1 < M:
                    qk(m + 1)
                yield

        def drain(g):
            for _ in g:
                pass

        def interleave(ga, gb, nb=1):
            da = db = False
            while not (da and db):
                if not da:
                    try:
                        next(ga)
                    except StopIteration:
                        da = True
                for _ in range(nb):
                    if not db:
                        try:
                            if next(gb) == "switch" and not da:
                                break
                        except StopIteration:
                            db = True

        def chain(*gens):
            for g in gens:
                yield from g

        def empty():
            return
            yield

        load_x(0)
        drain(head(0))
        if nblk > 1:
            interleave(attn(0), head(1))
        else:
            drain(attn(0))
        for J in range(nblk):
            s2 = stream2(J) if J + 2 < nblk else tail(J)
            if J + 1 < nblk:
                interleave(attn(J + 1), s2, nb=2)
            else:
                drain(s2)
        if dbg:
            allk = list(P.last_writer.keys())
            dumps = dict(uT=uT, Qa0=Qa[0], Qb0=Qb[0], GT0=GT[0], KaT=KaT, KbT=KbT, Vr=Vr, EA=EA, EB=EB, xring=xring,
                         Wg=Wg, Wp=Wp, Wout=Wout, skhi=skhi, sklo=sklo, rs=rs, pT=pT)
            for nm, tl in dumps.items():
                shp = list(tl[:].shape)
                dd = nc.dram_tensor("dbg_" + nm, shp, tl[:].dtype, kind="ExternalOutput").ap()
                key = "dbg_" + nm
                dma_sems[key] = es.enter_context(nc.semaphore("dsem_" + key))
                dma("sp", key, dd, tl[:], allk, [])

        if RESCHED:
            _list_schedule(P)
        P.finalize()

        with nc.Block() as block:
            @block.sync
            def _(e):
                fw = [k for k in P.dma_total if k.startswith("xs") or k.startswith("dbg_")]
                P.emit_engine("sp", e, eng_sems, dma_sems, final_dma_waits=fw)

            @block.gpsimd
            def _(e):
                P.emit_engine("pool", e, eng_sems, dma_sems)

            @block.tensor
            def _(e):
                P.emit_engine("pe", e, eng_sems, dma_sems)

            @block.scalar
            def _(e):
                P.emit_engine("act", e, eng_sems, dma_sems)

            @block.vector
            def _(e):
                P.emit_engine("dve", e, eng_sems, dma_sems)
    return nc


def _host_layout(x, p, norm_g, w_in, sink_a, rel_bias_b, w_out, ple_norm_g, w_ple_proj, w_ple_gate, final_norm_g):
    f = np.float32
    w = np.asarray(w_in, f)[0]
    qa, ka, va, ga = w[:, 0:512], w[:, 512:640], w[:, 640:768], w[:, 768:1280]
    qb, kb, vb, gb = w[:, 1280:1792], w[:, 1792:2304], w[:, 2304:2816], w[:, 2816:3328]
    ka_dup = np.concatenate([ka[:, 0:64], ka[:, 0:64], ka[:, 64:128], ka[:, 64:128]], axis=1)
    w_in_r = np.ascontiguousarray(np.concatenate([qa, ka_dup, ga, qb, kb, gb, vb, va], axis=1))
    assert w_in_r.shape == (D, WALL)
    g1 = np.ascontiguousarray(np.asarray(norm_g, f)[0].reshape(8, 128).T)
    g2 = np.ascontiguousarray(np.asarray(ple_norm_g, f)[0].reshape(8, 128).T)
    gfin = np.ascontiguousarray(np.tile(np.asarray(final_norm_g, f)[None, :], (128, 1)))
    ki = np.arange(128)[:, None]
    rA = np.arange(256)[None, :]
    slopes = np.asarray(2.0 ** (-8.0 * np.arange(1, 9) / 8), dtype=f)
    distA = np.abs(rA - ki).astype(f)
    maskA = ((ki >= 64) & (rA < 64)) | ((ki < 64) & (rA >= 192))
    biasA = np.empty((128, 8, 256), f)
    for h in range(8):
        biasA[:, h, :] = np.where(maskA, f(MASK_NEG), -slopes[h] * distA)
    rB = np.arange(640)[None, :]
    idx = np.clip(rB - ki, -128, 128) + 128
    maskB = ((ki >= 64) & (rB < 64)) | ((ki < 64) & (rB >= 576))
    tab = np.asarray(rel_bias_b, f)[0]
    biasB = np.empty((128, 8, 640), f)
    for h in range(8):
        biasB[:, h, :] = np.where(maskB, f(MASK_NEG), tab[h][idx])
    sk = np.asarray(sink_a, f)[0]
    sinkrow = np.empty((1, 512), f)
    for j in range(4):
        sinkrow[0, j * 128:j * 128 + 64] = sk[2 * j]
        sinkrow[0, j * 128 + 64:(j + 1) * 128] = sk[2 * j + 1]
    shared = dict(
        w_in_r=w_in_r, g1=g1, w_out=np.ascontiguousarray(np.asarray(w_out, f)[0]),
        w_gate=np.ascontiguousarray(np.asarray(w_ple_gate, f)[0]), g2=g2,
        w_pp=np.ascontiguousarray(np.asarray(w_ple_proj, f)[0]), gfin=gfin,
        biasA=np.ascontiguousarray(biasA.reshape(128, 8 * 256)), biasB=np.ascontiguousarray(biasB.reshape(128, 8 * 640)),
        sinkrow=sinkrow, ident=np.eye(128, dtype=f),
    )
    xf = np.asarray(x, f).reshape(NCORES, TOK, D)
    pf = np.asarray(p, f)[0].reshape(NCORES, TOK, PLE)
    return [dict(shared, x=np.ascontiguousarray(xf[c]), p=np.ascontiguousarray(pf[c])) for c in range(NCORES)]


_NC_CACHE = {}


def kernel(x, p, norm_g, w_in, sink_a, rel_bias_b, w_out, ple_norm_g, w_ple_proj, w_ple_gate, final_norm_g):
    in_maps = _host_layout(x, p, norm_g, w_in, sink_a, rel_bias_b, w_out, ple_norm_g, w_ple_proj, w_ple_gate, final_norm_g)
    if "nc" not in _NC_CACHE:
        _NC_CACHE["nc"] = build_program()
    res = run_bass_kernel_spmd(_NC_CACHE["nc"], in_maps, core_ids=list(range(NCORES)))
    outs = [np.asarray(r["out"], np.float32).reshape(SEQ_PER_CORE, SEQ, D) for r in res.results]
    return np.concatenate(outs, axis=0)
```

```python
import numpy as np
from contextlib import ExitStack
import concourse.bass as bass
import concourse.mybir as mybir
from concourse.bass_utils import run_bass_kernel_spmd

F32 = mybir.dt.float32
BF16 = mybir.dt.bfloat16
ALU = mybir.AluOpType
AF = mybir.ActivationFunctionType

NCORES = 8
D = 1024
SEQ = 2048
SEQ_PER_CORE = 4
TOK = SEQ_PER_CORE * SEQ
TB = 256
NT = TB // 128
NBLK = TOK // TB
BLK_PER_SEQ = SEQ // TB
XR = 8
PLE = 256
EPS = 1e-6
NFM = 22
WALL = NFM * 128 + 640
STG = 3072
MASK_NEG = -200.0
RESCHED = True
EMUL_ENG = "pool"


class _Op:
    __slots__ = ("eng", "fn", "deps", "dma", "idx", "signal", "sigval", "dma_val", "need_eng", "need_dma", "force")


class Prog:
    def __init__(self):
        self.ops = []
        self.last_writer = {}
        self.readers = {}

    def add(self, eng, fn, reads=(), writes=(), dma=None, after=()):
        idx = len(self.ops)
        deps = {}
        for k in reads:
            w = self.last_writer.get(k)
            if w is not None:
                deps[w] = True
        for k in writes:
            w = self.last_writer.get(k)
            if w is not None:
                deps.setdefault(w, False)
            for r in self.readers.get(k, ()):
                deps.setdefault(r, False)
        op = _Op()
        op.eng, op.fn, op.deps, op.dma, op.idx = eng, fn, deps, dma, idx
        op.signal = False
        op.sigval = 0
        op.dma_val = 0
        op.force = tuple(after)
        self.ops.append(op)
        for k in reads:
            self.readers.setdefault(k, []).append(idx)
        for k in writes:
            self.last_writer[k] = idx
            self.readers[k] = []
        return idx

    def finalize(self):
        ops = self.ops
        dma_count = {}
        for op in ops:
            if op.dma is not None:
                dma_count[op.dma] = dma_count.get(op.dma, 0) + 1
                op.dma_val = 16 * dma_count[op.dma]
        for op in ops:
            need_eng, need_dma = {}, {}
            for d, raw in op.deps.items():
                p = ops[d]
                if p.dma is not None:
                    if p.dma_val > need_dma.get(p.dma, 0):
                        need_dma[p.dma] = p.dma_val
                else:
                    if p.eng == op.eng and (op.eng == "pe" or not raw):
                        continue
                    if d > need_eng.get(p.eng, -1):
                        need_eng[p.eng] = d
            for d in op.force:
                if d > need_eng.get(ops[d].eng, -1):
                    need_eng[ops[d].eng] = d
            op.need_eng, op.need_dma = need_eng, need_dma
            for d in need_eng.values():
                ops[d].signal = True
        cnt = {}
        for op in ops:
            if op.dma is None and op.signal:
                cnt[op.eng] = cnt.get(op.eng, 0) + 1
                op.sigval = cnt[op.eng]
        self.dma_total = {k: 16 * v for k, v in dma_count.items()}

    def emit_engine(self, eng_name, e, eng_sems, dma_sems, final_dma_waits=()):
        ops = self.ops
        waited = {}
        for op in ops:
            if op.eng != eng_name:
                continue
            for pe_name, d in op.need_eng.items():
                v = ops[d].sigval
                key = ("e", pe_name)
                if waited.get(key, 0) < v:
                    e.wait_ge(eng_sems[pe_name], v)
                    waited[key] = v
            for k, v in op.need_dma.items():
                key = ("d", k)
                if waited.get(key, 0) < v:
                    e.wait_ge(dma_sems[k], v)
                    waited[key] = v
            ins = op.fn(e)
            if op.dma is not None:
                ins.then_inc(dma_sems[op.dma], 16)
            elif op.signal:
                ins.then_inc(eng_sems[eng_name], 1)
        for k in final_dma_waits:
            e.wait_ge(dma_sems[k], self.dma_total[k])


def _free_elems(ap):
    n = 1
    for d in ap.shape[1:]:
        n *= d
    return n


class _Rec:
    __slots__ = ("kind", "F", "info")

    def then_inc(self, *a, **k):
        return self


class _CostEng:
    def __init__(self):
        self.last = None

    def _r(self, kind, F, **info):
        r = _Rec()
        r.kind, r.F, r.info = kind, F, info
        self.last = r
        return r

    def matmul(self, out, lhsT, rhs, start=None, stop=None, skip_group_check=False, tile_position=None, **kw):
        return self._r("mm", _free_elems(rhs), K=lhsT.shape[0], M=_free_elems(lhsT),
                       rowbase=lhsT.base_partition(), colbase=out.base_partition())

    def transpose(self, out, in_, ident):
        return self._r("tr", 128)

    def activation(self, out, in_, func, bias=0.0, scale=1.0, accum_out=None, **kw):
        return self._r("act", _free_elems(in_), func=str(func), accum=accum_out is not None)

    def tensor_tensor(self, out, in0, in1, op, **kw):
        ps = ("bank" in in0.name) or ("bank" in in1.name)
        return self._r("tt", _free_elems(out), bf16=("bfloat16" in str(out.dtype)) and not ps)

    def tensor_copy(self, out, in_, **kw):
        return self._r("cp", _free_elems(out))

    def tensor_single_scalar(self, out, in_, scalar, op, **kw):
        return self._r("ts", _free_elems(out))

    def scalar_tensor_tensor(self, out, in0, scalar, in1, op0, op1, **kw):
        return self._r("stt", _free_elems(out))

    def reciprocal(self, out, in_, **kw):
        return self._r("rcp", _free_elems(out))

    def memset(self, ap, c):
        return self._r("ms", _free_elems(ap))

    def dma_start(self, out, in_, **kw):
        n = 1
        for d in out.shape:
            n *= d
        return self._r("dma", n)


def _op_dur(eng, r, state):
    k = r.kind
    if eng == "pe":
        if k == "tr":
            cfg, d = (128, 128), 56.0
        else:
            K_, M_, N_ = r.info["K"], r.info["M"], r.F
            cfg = (64 if 2 < K_ <= 64 else (32 if K_ <= 2 else 128), 64 if M_ <= 64 else 128)
            d = max(N_, 64) * 0.455 + 8
            prev = state.get("prevmm")
            if (prev is not None and prev["cfg"] == cfg and cfg != (128, 128) and cfg[0] != 32
                    and (prev["rb"], prev["cb"]) != (r.info["rowbase"], r.info["colbase"]) and not prev.get("paired")):
                state["prevmm"] = dict(cfg=cfg, rb=r.info["rowbase"], cb=r.info["colbase"], paired=True)
                return 4.0
        sw = 100.0 if state.get("cfg") not in (None, cfg) else 0.0
        state["cfg"] = cfg
        state["prevmm"] = dict(cfg=cfg, rb=r.info.get("rowbase", 0), cb=r.info.get("colbase", 0)) if k == "mm" else None
        return d + sw
    if eng == "act":
        if k == "dma":
            return 60.0
        return 0.83 * r.F + 100 + (93 if r.info.get("accum") else 0)
    if eng == "dve":
        if k == "rcp":
            return 5.0 * r.F + 20
        if k == "tt" and r.info.get("bf16"):
            return 60 + 0.6 * r.F
        if k in ("tt", "stt", "cp"):
            return 70 + 1.2 * r.F
        return 60 + 0.6 * r.F
    if eng == "pool":
        return 900.0 if k == "dma" else 2.45 * r.F
    return 60.0


def _list_schedule(P, fixed=("sp", "pool"), SEM=70.0):
    ops = P.ops
    n = len(ops)
    engs_c = {}
    recs = []
    for op in ops:
        ce = engs_c.setdefault(op.eng, _CostEng())
        op.fn(ce)
        recs.append(ce.last)
    st = {}
    dur1 = [_op_dur(op.eng, recs[i], st.setdefault(op.eng, {})) for i, op in enumerate(ops)]
    node_of = [0] * n
    nodes = []
    for i, op in enumerate(ops):
        if op.eng == "pe" and nodes and ops[nodes[-1][-1]].eng == "pe" and nodes[-1][-1] == i - 1:
            nodes[-1].append(i)
        else:
            nodes.append([i])
        node_of[i] = len(nodes) - 1
    m = len(nodes)
    neng = [ops[nd[0]].eng for nd in nodes]
    ndur = [sum(dur1[i] for i in nd) for nd in nodes]
    nlat = [(2000.0 + recs[nd[0]].F * 4 / 200.0) if ops[nd[0]].dma is not None else 0.0 for nd in nodes]
    preds = [set() for _ in range(m)]
    for i, op in enumerate(ops):
        a = node_of[i]
        for d in list(op.deps.keys()) + list(op.force):
            b = node_of[d]
            if b != a:
                preds[a].add(b)
    last = {}
    for a in range(m):
        if neng[a] in fixed:
            if neng[a] in last:
                preds[a].add(last[neng[a]])
            last[neng[a]] = a
    succs = [[] for _ in range(m)]
    for a in range(m):
        for p in preds[a]:
            succs[p].append(a)
    prio = [0.0] * m
    for a in range(m - 1, -1, -1):
        mx = 0.0
        for s in succs[a]:
            if prio[s] > mx:
                mx = prio[s]
        prio[a] = ndur[a] + nlat[a] + mx
    npred = [len(p) for p in preds]
    ready_t = [0.0] * m
    avail = {}
    for a in range(m):
        if npred[a] == 0:
            avail.setdefault(neng[a], []).append(a)
    free = {}
    engs = sorted(set(neng))
    order_nodes = []
    while len(order_nodes) < m:
        best = None
        for e in engs:
            h = avail.get(e)
            if not h:
                continue
            t = free.get(e, 0.0)
            rdy = [a for a in h if ready_t[a] <= t]
            if rdy:
                a = max(rdy, key=lambda a: (prio[a], -a))
                s = t
            else:
                a = min(h, key=lambda a: (ready_t[a], -prio[a]))
                s = ready_t[a]
            if best is None or s < best[0] or (s == best[0] and prio[a] > prio[best[2]]):
                best = (s, e, a)
        s, e, a = best
        avail[e].remove(a)
        free[e] = s + ndur[a]
        fin = s + ndur[a] + nlat[a]
        order_nodes.append(a)
        for sc in succs[a]:
            npred[sc] -= 1
            same = neng[sc] == e and nlat[a] == 0.0
            r = fin + (0.0 if same else SEM)
            if r > ready_t[sc]:
                ready_t[sc] = r
            if npred[sc] == 0:
                avail.setdefault(neng[sc], []).append(sc)
    order = [i for a in order_nodes for i in nodes[a]]
    newpos = {old: new for new, old in enumerate(order)}
    new_ops = [ops[i] for i in order]
    for new, op in enumerate(new_ops):
        op.idx = new
        op.deps = {newpos[d]: raw for d, raw in op.deps.items()}
        op.force = tuple(newpos[d] for d in op.force)
    P.ops = new_ops


def build_program(nblk=NBLK, stages=("head", "attn", "tail"), dbg=False):
    nc = bass.Bass("TRN2", target_bir_lowering=False)

    def din(name, shape):
        return nc.dram_tensor(name, shape, F32, kind="ExternalInput").ap()

    x_d = din("x", [TOK, D])
    p_d = din("p", [TOK, PLE])
    w_in_d = din("w_in_r", [D, WALL])
    g1_d = din("g1", [128, 8])
    w_out_d = din("w_out", [D, D])
    w_gate_d = din("w_gate", [D, D])
    g2_d = din("g2", [128, 8])
    w_pp_d = din("w_pp", [PLE, D])
    gfin_d = din("gfin", [128, D])
    biasA_d = din("biasA", [128, 8 * 256])
    biasB_d = din("biasB", [128, 8 * 640])
    sink_d = din("sinkrow", [1, 512])
    ident_d = din("ident", [128, 128])
    out_d = nc.dram_tensor("out", [TOK, D], F32, kind="ExternalOutput").ap()

    P = Prog()
    es = ExitStack()
    with es:
        def sb(name, shape, dt):
            return es.enter_context(nc.sbuf_tensor(name, shape, dt))

        def ps(name, shape, dt):
            return es.enter_context(nc.psum_tensor(name, shape, dt))

        Wall = sb("Wall", [128, 8, WALL], BF16)
        Wout = sb("Wout", [128, 8, D], BF16)
        Wg = sb("Wg", [128, 8, D], BF16)
        Wp = sb("Wp", [128, 2, D], BF16)
        gfin = sb("gfin_t", [128, D], F32)
        EA = sb("EA", [128, 8, 256], BF16)
        EB = sb("EB", [128, 8, 640], BF16)
        g1 = sb("g1_t", [128, 8], F32)
        g2 = sb("g2_t", [128, 8], F32)
        ident = sb("ident_t", [128, 128], BF16)
        ones2 = sb("ones2", [128, 64], BF16)
        onesrow = sb("onesrow", [2, 256], BF16)
        skhl = sb("skhl", [2, 512], BF16)
        xring = sb("xring", [128, XR * D], F32)
        junk = sb("junk", [128, D], mybir.dt.float8e4)
        ubuf = [sb(f"ubuf{i}", [128, D], BF16) for i in range(2)]
        uT = sb("uT", [128, 8, TB], BF16)
        Qa = [sb(f"Qa{i}", [128, 4, TB], BF16) for i in range(2)]
        Qb = [sb(f"Qb{i}", [128, 4, TB], BF16) for i in range(2)]
        GT = [sb(f"GT{i}", [128, 8, TB], BF16) for i in range(2)]
        KaT = sb("KaT", [128, 2, 1024], BF16)
        KbT = sb("KbT", [128, 4, 1024], BF16)
        Vr = sb("Vr", [128, 8, 640], BF16)
        pbf = [sb(f"pbf{i}", [128, NT, PLE], BF16) for i in range(2)]
        pT = sb("pT", [128, 2, TB], BF16)
        Pt = [sb(f"Pt{i}", [128, 2, 2, 256], BF16) for i in range(3)]
        Rn = [sb("Rn0", [128, 256], F32)] * 2
        tgs = [sb("tgs0", [128, 512], F32)] * 2
        tnh = [sb(f"tnh{i}", [128, 512], BF16) for i in range(2)]
        skhi, sklo = tnh[0][0:1, :], tnh[1][0:1, :]
        ss = sb("ss", [128, 3 * NT], F32)
        sd = sb("sd", [128, 3 * NT], F32)
        rs = sb("rs", [128, 3 * NT], F32)

        banks = [ps(f"bank{i}", [128, 512], F32) for i in range(8)]
        Sb = banks[0:4]
        Ab = banks[4:6]
        Gb = banks[6:8]
        Tb = banks[7][:, :].bitcast(BF16).rearrange("p (a b) -> p a b", a=8)
        NGB = 2

        eng_names = ["pe", "act", "dve", "pool"]
        eng_sems = {n: es.enter_context(nc.semaphore("sem_" + n)) for n in eng_names}
        dma_keys = ([f"xl{i}" for i in range(XR)] + [f"xs{i}" for i in range(XR)]
                    + ["p0", "p1", "stg0", "stg1", "wout", "wp", "gfin", "ident", "g1", "g2", "sink"])
        dma_sems = {k: es.enter_context(nc.semaphore("dsem_" + k)) for k in dma_keys}

        gb_rot = [0]

        def next_gb():
            b = gb_rot[0] % NGB
            gb_rot[0] += 1
            return b

        def dma(eng, key, out, in_, reads, writes):
            P.add(eng, lambda e, o=out, i=in_: e.dma_start(out=o, in_=i), reads, writes, dma=key)

        dma("pool", "ident", ident[:], ident_d, [], [("ident",)])
        dma("sp", "g1", g1[:], g1_d, [], [("g1",)])
        dma("sp", "g2", g2[:], g2_d, [], [("g2",)])
        sk32 = tgs[0][0:1, :]
        dma("sp", "sink", sk32, sink_d, [], [("sk32",), ("tgs", 0)])
        dma("sp", "gfin", gfin[:], gfin_d, [], [("gfin",)])
        P.add("dve", lambda e: e.memset(ones2[:], 2.0), [], [("ones2",)])
        P.add("dve", lambda e: e.memset(onesrow[:], 1.0), [], [("onesrow",)])

        stg_i = [0]

        def staged(src_ap, ncols, consume):
            s = stg_i[0] % 2
            stg_i[0] += 1
            st_ap = xring[:, s * STG: s * STG + ncols]
            dma("sp", f"stg{s}", st_ap, src_ap, [], [("stg", s)])
            consume(st_ap, s)

        flip = [0]

        def scale_cast(out_ap, in_ap, scal_ap, s, extra_reads, writes):
            if flip[0] % 2 == 0:
                P.add("dve", lambda e: e.tensor_single_scalar(out=out_ap, in_=in_ap, scalar=scal_ap, op=ALU.mult),
                      [("stg", s)] + extra_reads, writes)
            else:
                P.add("act", lambda e: e.activation(out=out_ap, in_=in_ap, func=AF.Copy, scale=scal_ap),
                      [("stg", s)] + extra_reads, writes)
            flip[0] += 1

        for c in range(8):
            for hp in range(2):
                c0 = hp * (WALL // 2)
                n = WALL // 2
                staged(w_in_d[c * 128:(c + 1) * 128, c0:c0 + n], n,
                       lambda st, s, c=c, c0=c0, n=n: scale_cast(Wall[:, c, c0:c0 + n], st, g1[:, c:c + 1], s,
                                                                 [("g1",)], [("Wall", c, c0)]))
        for c in range(8):
            staged(w_gate_d[c * 128:(c + 1) * 128, :], D,
                   lambda st, s, c=c: scale_cast(Wg[:, c, :], st, g2[:, c:c + 1], s, [("g2",)], [("Wg", c)]))
        for hh in range(2):
            def consA(st, s, hh=hh):
                for q in range(4):
                    h = hh * 4 + q
                    P.add("act", lambda e, h=h, q=q: e.activation(out=EA[:, h, :], in_=st[:, q * 256:(q + 1) * 256], func=AF.Exp),
                          [("stg", s)], [("EA", h)])
            staged(biasA_d[:, hh * 1024:(hh + 1) * 1024], 1024, consA)
        for hh in range(2):
            def consB(st, s, hh=hh):
                for q in range(4):
                    h = hh * 4 + q
                    P.add("act", lambda e, h=h, q=q: e.activation(out=EB[:, h, :], in_=st[:, q * 640:(q + 1) * 640], func=AF.Exp),
                          [("stg", s)], [("EB", h)])
            staged(biasB_d[:, hh * 2560:(hh + 1) * 2560], 2560, consB)
        dma("pool", "wout", Wout[:], w_out_d.rearrange("(c p) n -> p c n", p=128), [], [("Wout",)])
        dma("pool", "wp", Wp[:], w_pp_d.rearrange("(c p) n -> p c n", p=128), [], [("Wp",)])
        P.add("act", lambda e: e.activation(out=sk32, in_=sk32, func=AF.Exp), [("sk32",)], [("sk32",), ("tgs", 0)])
        P.add("dve", lambda e: e.tensor_single_scalar(out=sk32, in_=sk32, scalar=2.0, op=ALU.mult), [("sk32",)], [("sk32",), ("tgs", 0)])
        P.add("dve", lambda e: e.tensor_copy(out=skhi, in_=sk32), [("sk32",)], [("skhi",), ("tnh", 0)])
        P.add("dve", lambda e: e.tensor_tensor(out=sklo, in0=sk32, in1=skhi, op=ALU.subtract), [("sk32",), ("skhi",), ("tgs", 0)], [("sklo",), ("tnh", 1)])
        dma_sems["skc"] = es.enter_context(nc.semaphore("dsem_skc"))
        dma("sp", "skc", skhl[0:1, :], skhi, [("skhi",), ("tnh", 0)], [("skhl", 0)])
        dma("sp", "skc", skhl[1:2, :], sklo, [("sklo",), ("tnh", 1)], [("skhl", 1)])

        wall_keys = [("Wall", c, c0) for c in range(8) for c0 in (0, WALL // 2)]
        wg_keys = [("Wg", c) for c in range(8)]

        def xslot_keys(slot):
            ks = [("x", slot)]
            lo, hi = slot * D, (slot + 1) * D
            for s in range(2):
                if lo < (s + 1) * STG and hi > s * STG:
                    ks.append(("stg", s))
            return ks

        def load_x_tile(J, t):
            g = J * NT + t
            slot = g % XR
            dma("sp", f"xl{slot}", xring[:, slot * D:(slot + 1) * D], x_d[g * 128:(g + 1) * 128, :],
                [], xslot_keys(slot))

        def load_x(J):
            for t in range(NT):
                load_x_tile(J, t)

        EPS_AP = sb("eps_t", [128, 1], F32)
        P.add("dve", lambda e: e.memset(EPS_AP[:], EPS), [], [("eps",)])
        u2T = sb("u2T", [128, 8, TB], BF16)

        def xtile(J, t):
            g = J * NT + t
            slot = g % XR
            return xring[:, slot * D:(slot + 1) * D], slot, g

        def sq_stats(J, t, col0):
            xs, slot, g = xtile(J, t)
            P.add("act", lambda e: e.activation(out=junk[:], in_=xs, func=AF.Square, accum_out=ss[:, col0 + t:col0 + t + 1], saturate=False),
                  [("x", slot)], [("junk",), ("ss", col0 + t)])

        def sqrt_recip(col0, ncol):
            groups = sorted({(cc // NT) * NT for cc in range(col0, col0 + ncol)})
            P.add("act", lambda e: e.activation(out=sd[:, col0:col0 + ncol], in_=ss[:, col0:col0 + ncol], func=AF.Sqrt,
                                                bias=EPS_AP[:, 0:1], scale=1.0 / D),
                  [("ss", cc) for cc in range(col0, col0 + ncol)] + [("eps",)], [("sd", gq) for gq in groups])
            P.add("dve", lambda e: e.reciprocal(out=rs[:, col0:col0 + ncol], in_=sd[:, col0:col0 + ncol]),
                  [("sd", gq) for gq in groups], [("rs", gq) for gq in groups])

        def scale_u(J, t, col0):
            xs, slot, g = xtile(J, t)
            us = t % 2
            P.add("act", lambda e: e.activation(out=ubuf[us][:], in_=xs, func=AF.Copy, scale=rs[:, col0 + t:col0 + t + 1]),
                  [("x", slot), ("rs", col0)], [("u", us)])

        def transposes_u(t, dstT, dkey):
            us = t % 2
            for cc in range(8):
                P.add("pe", lambda e, cc=cc: e.transpose(Tb[:, cc, :], ubuf[us][:, cc * 128:(cc + 1) * 128], ident[:]),
                      [("u", us), ("ident",)], [("psG", 1)])
            P.add("dve", lambda e: e.tensor_copy(out=dstT[:, :, t * 128:(t + 1) * 128], in_=Tb[:, 0:8, :]),
                  [("psG", 1)], [(dkey, t)])

        def transposes_p(J, t):
            par = J % 2
            for cc in range(2):
                P.add("pe", lambda e, cc=cc: e.transpose(Tb[:, cc, :], pbf[par][:, t, cc * 128:(cc + 1) * 128], ident[:]),
                      [("pbf", par), ("ident",)], [("psG", 1)])
            P.add("dve", lambda e: e.tensor_copy(out=pT[:, :, t * 128:(t + 1) * 128], in_=Tb[:, 0:2, :]),
                  [("psG", 1)], [("pT", t)])

        def head_io(J, do_x=True, do_p=True):
            par = J % 2
            if do_x and J + 1 < nblk:
                load_x(J + 1)
            if do_p:
                dma("pool", f"p{par}", pbf[par][:], p_d[J * TB:(J + 1) * TB, :].rearrange("(t q) f -> q t f", q=128),
                    [], [("pbf", par)])

        FM = [("Qa", 0, 0), ("Qa", 2, 2), ("Ka", 0, 4), ("Ga", 0, 6), ("Ga", 2, 8), ("Qb", 0, 10), ("Qb", 2, 12),
              ("Kb", 0, 14), ("Kb", 2, 16), ("Gb", 0, 18), ("Gb", 2, 20)]
        uT_keys = [("uT", t) for t in range(NT)]

        def proj_fm(J, n_ev):
            par = J % 2
            ringcol = (J * TB) % 1024
            rt0 = (J * NT) % 8
            kind, ci, f0 = FM[n_ev]
            gb = next_gb()
            for i in range(2):
                for cc in range(8):
                    P.add("pe", lambda e, i=i, cc=cc: e.matmul(
                        Gb[gb][:, i * TB:(i + 1) * TB], Wall[:, cc, (f0 + i) * 128:(f0 + i + 1) * 128], uT[:, cc, :],
                        start=(cc == 0), stop=(cc == 7)), wall_keys + uT_keys, [("psG", gb)])
            src3 = Gb[gb][:, :].rearrange("p (a b) -> p a b", a=2)
            if kind in ("Qa", "Qb", "Ka", "Kb"):
                if kind in ("Qa", "Qb"):
                    dst = (Qa if kind == "Qa" else Qb)[par][:, ci:ci + 2, :]
                    wk = [(kind, par, ci), (kind, par, ci + 1)]
                else:
                    dst = (KaT if kind == "Ka" else KbT)[:, ci:ci + 2, ringcol:ringcol + TB]
                    wk = [(kind, ci, (rt0 + t) % 8) for t in range(NT)]
                if n_ev % 2 == 0:
                    P.add("dve", lambda e: e.tensor_copy(out=dst, in_=src3), [("psG", gb)], wk)
                else:
                    P.add("act", lambda e: e.activation(out=dst, in_=src3, func=AF.Copy), [("psG", gb)], wk)
            else:
                gc = ci + (0 if kind == "Ga" else 4)
                ts_ = n_ev % 2
                tn3 = tnh[ts_][:, :].rearrange("p (a b) -> p a b", a=2)
                P.add("act", lambda e: e.activation(out=tn3, in_=src3, func=AF.Tanh, scale=0.5), [("psG", gb)], [("tnh", ts_)])
                dst = GT[par][:, gc:gc + 2, :]
                P.add("dve", lambda e: e.scalar_tensor_tensor(out=dst, in0=tn3, scalar=1.0, in1=src3, op0=ALU.add, op1=ALU.mult),
                      [("psG", gb), ("tnh", ts_)], [("GT", par, gc), ("GT", par, gc + 1)])

        def proj_v(J, t, which):
            rt = ((J * NT) % 8 + t) % 8
            gb = next_gb()
            if which == 0:
                for cc in range(8):
                    P.add("pe", lambda e, cc=cc: e.matmul(
                        Gb[gb][:, :], uT[:, cc, t * 128:(t + 1) * 128], Wall[:, cc, NFM * 128:NFM * 128 + 512],
                        start=(cc == 0), stop=(cc == 7)), wall_keys + [("uT", t)], [("psG", gb)])
                P.add("act", lambda e: e.activation(out=Vr[:, rt, 0:512], in_=Gb[gb][:, :], func=AF.Copy), [("psG", gb)], [("Vb", rt)])
            else:
                for cc in range(8):
                    P.add("pe", lambda e, cc=cc: e.matmul(
                        Gb[gb][:, 0:128], uT[:, cc, t * 128:(t + 1) * 128], Wall[:, cc, NFM * 128 + 512:NFM * 128 + 640],
                        start=(cc == 0), stop=(cc == 7)), wall_keys + [("uT", t)], [("psG", gb)])
                P.add("dve", lambda e: e.tensor_copy(out=Vr[:, rt, 512:640], in_=Gb[gb][:, 0:128]), [("psG", gb)], [("Va", rt)])

        def wout(J, t, half):
            par = J % 2
            xs, slot, g = xtile(J, t)
            gb = next_gb()
            for cc in range(8):
                ysrc = (Qa if cc < 4 else Qb)[par][:, cc % 4, t * 128:(t + 1) * 128]
                yk = [("Qa" if cc < 4 else "Qb", par, cc % 4)]
                P.add("pe", lambda e, cc=cc, ysrc=ysrc: e.matmul(
                    Gb[gb][:, :], ysrc, Wout[:, cc, half * 512:(half + 1) * 512], start=(cc == 0), stop=(cc == 7)),
                    yk + [("Wout",)], [("psG", gb)])
            P.add("dve", lambda e: e.tensor_tensor(out=xs[:, half * 512:(half + 1) * 512], in0=Gb[gb][:, :],
                                                  in1=xs[:, half * 512:(half + 1) * 512], op=ALU.add),
                  [("psG", gb), ("x", slot)], [("x", slot)])

        def gate(J, t, half):
            xs, slot, g = xtile(J, t)
            gbg = next_gb()
            for cc in range(8):
                P.add("pe", lambda e, cc=cc: e.matmul(
                    Gb[gbg][:, :], u2T[:, cc, t * 128:(t + 1) * 128], Wg[:, cc, half * 512:(half + 1) * 512],
                    start=(cc == 0), stop=(cc == 7)), [("u2T", t)] + wg_keys, [("psG", gbg)])
            ti = 0
            P.add("act", lambda e: e.activation(out=tgs[ti][:], in_=Gb[gbg][:, :], func=AF.Tanh, scale=0.5),
                  [("psG", gbg)], [("tgs", ti)])
            gbp = next_gb()
            for cc in range(2):
                P.add("pe", lambda e, cc=cc: e.matmul(
                    Gb[gbp][:, :], pT[:, cc, t * 128:(t + 1) * 128], Wp[:, cc, half * 512:(half + 1) * 512],
                    start=(cc == 0), stop=(cc == 1)), [("pT", t), ("Wp",)], [("psG", gbp)])
            P.add("dve", lambda e: e.scalar_tensor_tensor(out=tgs[ti][:], in0=tgs[ti][:], scalar=1.0, in1=Gb[gbp][:, :],
                                                         op0=ALU.add, op1=ALU.mult),
                  [("tgs", ti), ("psG", gbp)], [("tgs", ti)])
            P.add("dve", lambda e: e.scalar_tensor_tensor(out=xs[:, half * 512:(half + 1) * 512], in0=tgs[ti][:], scalar=0.5,
                                                         in1=xs[:, half * 512:(half + 1) * 512], op0=ALU.mult, op1=ALU.add),
                  [("tgs", ti), ("x", slot)], [("x", slot)])

        def final(J, t):
            xs, slot, g = xtile(J, t)
            P.add("dve", lambda e: e.scalar_tensor_tensor(out=xs, in0=xs, scalar=rs[:, 2 * NT + t:2 * NT + t + 1], in1=gfin[:],
                                                         op0=ALU.mult, op1=ALU.mult),
                  [("x", slot), ("rs", 2 * NT), ("gfin",)], [("x", slot)])
            dma("sp", f"xs{slot}", out_d[g * 128:(g + 1) * 128, :], xs, [("x", slot)], [])

        def head(J):
            head_io(J)
            for t in range(NT):
                sq_stats(J, t, 0)
            sqrt_recip(0, NT)
            yield
            for t in range(NT):
                scale_u(J, t, 0)
                transposes_u(t, uT, "uT")
                yield
            for n_ev in range(len(FM)):
                proj_fm(J, n_ev)
                yield
            for t in range(NT):
                for which in range(2):
                    proj_v(J, t, which)
                    yield

        def tail(J):
            for t in range(NT):
                for half in range(2):
                    wout(J, t, half)
                    yield
                sq_stats(J, t, NT)
            sqrt_recip(NT, NT)
            yield
            for t in range(NT):
                scale_u(J, t, NT)
                transposes_u(t, u2T, "u2T")
                yield
                transposes_p(J, t)
                yield
            for t in range(NT):
                for half in range(2):
                    gate(J, t, half)
                    yield
                sq_stats(J, t, 2 * NT)
            sqrt_recip(2 * NT, NT)
            yield
            for t in range(NT):
                final(J, t)
                yield

        def stream2(J):
            H = J + 2
            if J + 3 < nblk:
                load_x(J + 3)
            for t in range(NT):
                for half in range(2):
                    wout(J, t, half)
                    yield
                sq_stats(J, t, NT)
            for t in range(NT):
                sq_stats(H, t, 0)
            sqrt_recip(0, 2 * NT)
            yield
            for t in range(NT):
                scale_u(H, t, 0)
            for t in range(NT):
                transposes_p(J, t)
                yield "switch"
            head_io(H, do_x=False)
            for t in range(NT):
                transposes_u(t, uT, "uT")
                yield "switch"
            for t in range(NT):
                scale_u(J, t, NT)
            for n_ev in range(0, 3):
                proj_fm(H, n_ev)
                yield
            for t in range(NT):
                transposes_u(t, u2T, "u2T")
                yield "switch"
            for n_ev in range(3, 6):
                proj_fm(H, n_ev)
                yield
            rest_h = [lambda n_ev=n_ev: proj_fm(H, n_ev) for n_ev in range(6, len(FM))] + \
                     [lambda t=t, which=which: proj_v(H, t, which) for t in range(NT) for which in range(2)]
            gates = [(t, half) for t in range(NT) for half in range(2)]
            gi = 0
            for i, fn in enumerate(rest_h):
                fn()
                yield
                if i % 2 == 1 and gi < len(gates):
                    t, half = gates[gi]
                    gate(J, t, half)
                    if half == 1:
                        sq_stats(J, t, 2 * NT)
                    gi += 1
                    yield
            while gi < len(gates):
                t, half = gates[gi]
                gate(J, t, half)
                if half == 1:
                    sq_stats(J, t, 2 * NT)
                gi += 1
                yield
            sqrt_recip(2 * NT, NT)
            yield
            for t in range(NT):
                final(J, t)
                yield

        pair_ctr = [0]
        unit_ctr = [0]

        def attn(J):
            par = J % 2
            s = J // BLK_PER_SEQ
            jl = J % BLK_PER_SEQ
            units = []
            for mixer in ("A", "B"):
                for j in range(4):
                    if mixer == "A":
                        kps, span = range(2 * jl - 1, 2 * jl + 2), 256
                    else:
                        kps, span = range(2 * jl - 4, 2 * jl + 2), 640
                    ul = []
                    for kp in kps:
                        if kp < 0:
                            continue
                        q_lo = max(128 * kp, TB * jl)
                        q_hi = min(128 * kp + span, TB * jl + TB)
                        if q_hi <= q_lo:
                            continue
                        ul.append((kp, q_lo - 128 * kp, q_hi - q_lo, q_lo - TB * jl))
                    pid = pair_ctr[0]
                    pair_ctr[0] += 1
                    for i, u in enumerate(ul):
                        uid = unit_ctr[0]
                        unit_ctr[0] += 1
                        units.append(dict(mixer=mixer, j=j, kp=u[0], r0=u[1], w=u[2], c0=u[3],
                                          first=(i == 0), last=(i == len(ul) - 1), pid=pid, uid=uid))

            sus = [units[i:i + 2] for i in range(0, len(units), 2)]

            def qk(m):
                X, Y = Sb[2 * (m % 2)], Sb[2 * (m % 2) + 1]
                for i, u in enumerate(sus[m]):
                    mixer, j, w, c0 = u["mixer"], u["j"], u["w"], u["c0"]
                    rt = (s * 16 + u["kp"]) % 8
                    if mixer == "A":
                        kv = j // 2
                        Kt = KaT[:, kv, rt * 128:(rt + 1) * 128]
                        Qt = Qa[par][:, j, c0:c0 + w]
                        rk = [("Ka", 0, rt), ("Qa", par, j)]
                    else:
                        Kt = KbT[:, j, rt * 128:(rt + 1) * 128]
                        Qt = Qb[par][:, j, c0:c0 + w]
                        rk = [("Kb", (j // 2) * 2, rt), ("Qb", par, j)]
                    for hh, bank in ((0, X), (1, Y)):
                        P.add("pe", lambda e, hh=hh, bank=bank, Kt=Kt, Qt=Qt, i=i, w=w: e.matmul(
                            bank[:, i * 256:i * 256 + w], Kt[hh * 64:(hh + 1) * 64, :], Qt[hh * 64:(hh + 1) * 64, :],
                            start=True, stop=True), rk, [("psS", 2 * (m % 2) + hh)])

            def front(m):
                su = sus[m]
                pt = m % 3
                nu = len(su)
                wmax = max(u["w"] for u in su)
                pk = [("Pt", pt, i) for i in range(nu)]
                for hh in range(2):
                    bank3 = Sb[2 * (m % 2) + hh][:, :].rearrange("p (a b) -> p a b", a=2)
                    bk = [("psS", 2 * (m % 2) + hh)]
                    P.add("act", lambda e, hh=hh, bank3=bank3: e.activation(
                        out=Pt[pt][:, 0:nu, hh, 0:wmax], in_=bank3[:, 0:nu, 0:wmax], func=AF.Exp, scale=0.125), bk, pk)
                for i, u in enumerate(su):
                    mixer, j, w, r0 = u["mixer"], u["j"], u["w"], u["r0"]
                    Et = EA if mixer == "A" else EB
                    ek = [("EA" if mixer == "A" else "EB", 2 * j), ("EA" if mixer == "A" else "EB", 2 * j + 1)]
                    P.add(EMUL_ENG, lambda e, i=i, j=j, w=w, r0=r0, Et=Et: e.tensor_tensor(
                        out=Pt[pt][:, i, :, 0:w], in0=Pt[pt][:, i, :, 0:w], in1=Et[:, 2 * j:2 * j + 2, r0:r0 + w], op=ALU.mult),
                        [("Pt", pt, i)] + ek, [("Pt", pt, i)])

            def back(m):
                pt = m % 3
                for i, u in enumerate(sus[m]):
                    back_unit(u, Pt[pt], i, pt)

            def back_unit(u, Ptile, i, pt):
                mixer, j, w, c0 = u["mixer"], u["j"], u["w"], u["c0"]
                rt = (s * 16 + u["kp"]) % 8
                ab = u["pid"] % 2
                if mixer == "A":
                    kv = j // 2
                    V0 = Vr[:, rt, 512 + kv * 64:512 + kv * 64 + 64]
                    V1 = V0
                    vk = [("Va", rt)]
                else:
                    V0 = Vr[:, rt, (2 * j) * 64:(2 * j + 1) * 64]
                    V1 = Vr[:, rt, (2 * j + 1) * 64:(2 * j + 2) * 64]
                    vk = [("Vb", rt)]
                first = u["first"]
                acc = Ab[ab]
                P.add("pe", lambda e: e.matmul(acc[0:64, c0:c0 + w], V0, Ptile[:, i, 0, 0:w], start=first, stop=True,
                                               skip_group_check=True, tile_position=(0, 0)),
                      [("Pt", pt, i)] + vk, [("psA", ab)])
                P.add("pe", lambda e: e.matmul(acc[64:128, c0:c0 + w], V1, Ptile[:, i, 1, 0:w], start=first, stop=True,
                                               skip_group_check=True, tile_position=(0, 64)),
                      [("Pt", pt, i)] + vk, [("psA", ab)])
                P.add("pe", lambda e: e.matmul(acc[0:64, 256 + c0:256 + c0 + w], ones2[:, 0:64], Ptile[:, i, 0, 0:w],
                                               start=False, stop=True, skip_group_check=True, tile_position=(0, 0)),
                      [("Pt", pt, i), ("ones2",)], [("psA", ab)])
                P.add("pe", lambda e: e.matmul(acc[64:128, 256 + c0:256 + c0 + w], ones2[:, 0:64], Ptile[:, i, 1, 0:w],
                                               start=False, stop=True, skip_group_check=True, tile_position=(0, 64)),
                      [("Pt", pt, i), ("ones2",)], [("psA", ab)])
                if not u["last"]:
                    return
                if mixer == "A":
                    P.add("pe", lambda e: e.matmul(acc[:, 256:512], skhl[0:2, j * 128:(j + 1) * 128], onesrow[0:2, :],
                                                   start=False, stop=True, skip_group_check=True),
                          [("skhl", 0), ("skhl", 1), ("onesrow",)], [("psA", ab)])
                rn = 0
                gc = j if mixer == "A" else 4 + j
                ydst = (Qa if mixer == "A" else Qb)[par][:, j, :]
                yk = [("Qa" if mixer == "A" else "Qb", par, j)]
                P.add("dve", lambda e: e.reciprocal(out=Rn[rn][:], in_=acc[:, 256:512]), [("psA", ab)], [("Rn", rn)])
                P.add("dve", lambda e: e.tensor_tensor(out=Rn[rn][:], in0=Rn[rn][:], in1=GT[par][:, gc, :], op=ALU.mult),
                      [("Rn", rn), ("GT", par, gc)], [("Rn", rn)])
                P.add("dve", lambda e: e.tensor_tensor(out=ydst, in0=acc[:, 0:256], in1=Rn[rn][:], op=ALU.mult),
                      [("psA", ab), ("Rn", rn)], yk)

            M = len(sus)
            LAG = 2
            for k in range(0, M + LAG, 2):
                for m in (k, k + 1):
                    if m < M:
                        qk(m)
                for m in (k, k + 1):
                    if LAG <= m < M + LAG:
                        back(m - LAG)
                for m in (k, k + 1):
                    if m < M:
                        front(m)
                yield

        def drain(g):
            for _ in g:
                pass

        def interleave(ga, gb, nb=1):
            da = db = False
            while not (da and db):
                if not da:
                    try:
                        next(ga)
                    except StopIteration:
                        da = True
                for _ in range(nb):
                    if not db:
                        try:
                            if next(gb) == "switch" and not da:
                                break
                        except StopIteration:
                            db = True

        def chain(*gens):
            for g in gens:
                yield from g

        def empty():
            return
            yield

        load_x(0)
        drain(head(0))
        if nblk > 1:
            interleave(attn(0), head(1))
        else:
            drain(attn(0))
        for J in range(nblk):
            s2 = stream2(J) if J + 2 < nblk else tail(J)
            if J + 1 < nblk:
                interleave(attn(J + 1), s2, nb=2)
            else:
                drain(s2)
        if dbg:
            allk = list(P.last_writer.keys())
            dumps = dict(uT=uT, Qa0=Qa[0], Qb0=Qb[0], GT0=GT[0], KaT=KaT, KbT=KbT, Vr=Vr, EA=EA, EB=EB, xring=xring,
                         Wg=Wg, Wp=Wp, Wout=Wout, skhi=skhi, sklo=sklo, rs=rs, pT=pT)
            for nm, tl in dumps.items():
                shp = list(tl[:].shape)
                dd = nc.dram_tensor("dbg_" + nm, shp, tl[:].dtype, kind="ExternalOutput").ap()
                key = "dbg_" + nm
                dma_sems[key] = es.enter_context(nc.semaphore("dsem_" + key))
                dma("sp", key, dd, tl[:], allk, [])

        if RESCHED:
            _list_schedule(P)
        P.finalize()

        with nc.Block() as block:
            @block.sync
            def _(e):
                fw = [k for k in P.dma_total if k.startswith("xs") or k.startswith("dbg_")]
                P.emit_engine("sp", e, eng_sems, dma_sems, final_dma_waits=fw)

            @block.gpsimd
            def _(e):
                P.emit_engine("pool", e, eng_sems, dma_sems)

            @block.tensor
            def _(e):
                P.emit_engine("pe", e, eng_sems, dma_sems)

            @block.scalar
            def _(e):
                P.emit_engine("act", e, eng_sems, dma_sems)

            @block.vector
            def _(e):
                P.emit_engine("dve", e, eng_sems, dma_sems)
    return nc


def _host_layout(x, p, norm_g, w_in, sink_a, rel_bias_b, w_out, ple_norm_g, w_ple_proj, w_ple_gate, final_norm_g):
    f = np.float32
    w = np.asarray(w_in, f)[0]
    qa, ka, va, ga = w[:, 0:512], w[:, 512:640], w[:, 640:768], w[:, 768:1280]
    qb, kb, vb, gb = w[:, 1280:1792], w[:, 1792:2304], w[:, 2304:2816], w[:, 2816:3328]
    ka_dup = np.concatenate([ka[:, 0:64], ka[:, 0:64], ka[:, 64:128], ka[:, 64:128]], axis=1)
    w_in_r = np.ascontiguousarray(np.concatenate([qa, ka_dup, ga, qb, kb, gb, vb, va], axis=1))
    assert w_in_r.shape == (D, WALL)
    g1 = np.ascontiguousarray(np.asarray(norm_g, f)[0].reshape(8, 128).T)
    g2 = np.ascontiguousarray(np.asarray(ple_norm_g, f)[0].reshape(8, 128).T)
    gfin = np.ascontiguousarray(np.tile(np.asarray(final_norm_g, f)[None, :], (128, 1)))
    ki = np.arange(128)[:, None]
    rA = np.arange(256)[None, :]
    slopes = np.asarray(2.0 ** (-8.0 * np.arange(1, 9) / 8), dtype=f)
    distA = np.abs(rA - ki).astype(f)
    maskA = ((ki >= 64) & (rA < 64)) | ((ki < 64) & (rA >= 192))
    biasA = np.empty((128, 8, 256), f)
    for h in range(8):
        biasA[:, h, :] = np.where(maskA, f(MASK_NEG), -slopes[h] * distA)
    rB = np.arange(640)[None, :]
    idx = np.clip(rB - ki, -128, 128) + 128
    maskB = ((ki >= 64) & (rB < 64)) | ((ki < 64) & (rB >= 576))
    tab = np.asarray(rel_bias_b, f)[0]
    biasB = np.empty((128, 8, 640), f)
    for h in range(8):
        biasB[:, h, :] = np.where(maskB, f(MASK_NEG), tab[h][idx])
    sk = np.asarray(sink_a, f)[0]
    sinkrow = np.empty((1, 512), f)
    for j in range(4):
        sinkrow[0, j * 128:j * 128 + 64] = sk[2 * j]
        sinkrow[0, j * 128 + 64:(j + 1) * 128] = sk[2 * j + 1]
    shared = dict(
        w_in_r=w_in_r, g1=g1, w_out=np.ascontiguousarray(np.asarray(w_out, f)[0]),
        w_gate=np.ascontiguousarray(np.asarray(w_ple_gate, f)[0]), g2=g2,
        w_pp=np.ascontiguousarray(np.asarray(w_ple_proj, f)[0]), gfin=gfin,
        biasA=np.ascontiguousarray(biasA.reshape(128, 8 * 256)), biasB=np.ascontiguousarray(biasB.reshape(128, 8 * 640)),
        sinkrow=sinkrow, ident=np.eye(128, dtype=f),
    )
    xf = np.asarray(x, f).reshape(NCORES, TOK, D)
    pf = np.asarray(p, f)[0].reshape(NCORES, TOK, PLE)
    return [dict(shared, x=np.ascontiguousarray(xf[c]), p=np.ascontiguousarray(pf[c])) for c in range(NCORES)]


_NC_CACHE = {}


def kernel(x, p, norm_g, w_in, sink_a, rel_bias_b, w_out, ple_norm_g, w_ple_proj, w_ple_gate, final_norm_g):
    in_maps = _host_layout(x, p, norm_g, w_in, sink_a, rel_bias_b, w_out, ple_norm_g, w_ple_proj, w_ple_gate, final_norm_g)
    if "nc" not in _NC_CACHE:
        _NC_CACHE["nc"] = build_program()
    res = run_bass_kernel_spmd(_NC_CACHE["nc"], in_maps, core_ids=list(range(NCORES)))
    outs = [np.asarray(r["out"], np.float32).reshape(SEQ_PER_CORE, SEQ, D) for r in res.results]
    return np.concatenate(outs, axis=0)
```

```python
import numpy as np
from contextlib import ExitStack
import concourse.bass as bass
import concourse.mybir as mybir
from concourse.bass_utils import run_bass_kernel_spmd

F32 = mybir.dt.float32
BF16 = mybir.dt.bfloat16
ALU = mybir.AluOpType
AF = mybir.ActivationFunctionType

NCORES = 8
D = 1024
SEQ = 2048
SEQ_PER_CORE = 4
TOK = SEQ_PER_CORE * SEQ
TB = 256
NT = TB // 128
NBLK = TOK // TB
BLK_PER_SEQ = SEQ // TB
XR = 8
PLE = 256
EPS = 1e-6
NFM = 22
WALL = NFM * 128 + 640
STG = 3072
MASK_NEG = -200.0
RESCHED = True
EMUL_ENG = "pool"


class _Op:
    __slots__ = ("eng", "fn", "deps", "dma", "idx", "signal", "sigval", "dma_val", "need_eng", "need_dma", "force")


class Prog:
    def __init__(self):
        self.ops = []
        self.last_writer = {}
        self.readers = {}

    def add(self, eng, fn, reads=(), writes=(), dma=None, after=()):
        idx = len(self.ops)
        deps = {}
        for k in reads:
            w = self.last_writer.get(k)
            if w is not None:
                deps[w] = True
        for k in writes:
            w = self.last_writer.get(k)
            if w is not None:
                deps.setdefault(w, False)
            for r in self.readers.get(k, ()):
                deps.setdefault(r, False)
        op = _Op()
        op.eng, op.fn, op.deps, op.dma, op.idx = eng, fn, deps, dma, idx
        op.signal = False
        op.sigval = 0
        op.dma_val = 0
        op.force = tuple(after)
        self.ops.append(op)
        for k in reads:
            self.readers.setdefault(k, []).append(idx)
        for k in writes:
            self.last_writer[k] = idx
            self.readers[k] = []
        return idx

    def finalize(self):
        ops = self.ops
        dma_count = {}
        for op in ops:
            if op.dma is not None:
                dma_count[op.dma] = dma_count.get(op.dma, 0) + 1
                op.dma_val = 16 * dma_count[op.dma]
        for op in ops:
            need_eng, need_dma = {}, {}
            for d, raw in op.deps.items():
                p = ops[d]
                if p.dma is not None:
                    if p.dma_val > need_dma.get(p.dma, 0):
                        need_dma[p.dma] = p.dma_val
                else:
                    if p.eng == op.eng and (op.eng == "pe" or not raw):
                        continue
                    if d > need_eng.get(p.eng, -1):
                        need_eng[p.eng] = d
            for d in op.force:
                if d > need_eng.get(ops[d].eng, -1):
                    need_eng[ops[d].eng] = d
            op.need_eng, op.need_dma = need_eng, need_dma
            for d in need_eng.values():
                ops[d].signal = True
        cnt = {}
        for op in ops:
            if op.dma is None and op.signal:
                cnt[op.eng] = cnt.get(op.eng, 0) + 1
                op.sigval = cnt[op.eng]
        self.dma_total = {k: 16 * v for k, v in dma_count.items()}

    def emit_engine(self, eng_name, e, eng_sems, dma_sems, final_dma_waits=()):
        ops = self.ops
        waited = {}
        for op in ops:
            if op.eng != eng_name:
                continue
            for pe_name, d in op.need_eng.items():
                v = ops[d].sigval
                key = ("e", pe_name)
                if waited.get(key, 0) < v:
                    e.wait_ge(eng_sems[pe_name], v)
                    waited[key] = v
            for k, v in op.need_dma.items():
                key = ("d", k)
                if waited.get(key, 0) < v:
                    e.wait_ge(dma_sems[k], v)
                    waited[key] = v
            ins = op.fn(e)
            if op.dma is not None:
                ins.then_inc(dma_sems[op.dma], 16)
            elif op.signal:
                ins.then_inc(eng_sems[eng_name], 1)
        for k in final_dma_waits:
            e.wait_ge(dma_sems[k], self.dma_total[k])


def _free_elems(ap):
    n = 1
    for d in ap.shape[1:]:
        n *= d
    return n


class _Rec:
    __slots__ = ("kind", "F", "info")

    def then_inc(self, *a, **k):
        return self


class _CostEng:
    def __init__(self):
        self.last = None

    def _r(self, kind, F, **info):
        r = _Rec()
        r.kind, r.F, r.info = kind, F, info
        self.last = r
        return r

    def matmul(self, out, lhsT, rhs, start=None, stop=None, skip_group_check=False, tile_position=None, **kw):
        return self._r("mm", _free_elems(rhs), K=lhsT.shape[0], M=_free_elems(lhsT),
                       rowbase=lhsT.base_partition(), colbase=out.base_partition())

    def transpose(self, out, in_, ident):
        return self._r("tr", 128)

    def activation(self, out, in_, func, bias=0.0, scale=1.0, accum_out=None, **kw):
        return self._r("act", _free_elems(in_), func=str(func), accum=accum_out is not None)

    def tensor_tensor(self, out, in0, in1, op, **kw):
        ps = ("bank" in in0.name) or ("bank" in in1.name)
        return self._r("tt", _free_elems(out), bf16=("bfloat16" in str(out.dtype)) and not ps)

    def tensor_copy(self, out, in_, **kw):
        return self._r("cp", _free_elems(out))

    def tensor_single_scalar(self, out, in_, scalar, op, **kw):
        return self._r("ts", _free_elems(out))

    def scalar_tensor_tensor(self, out, in0, scalar, in1, op0, op1, **kw):
        return self._r("stt", _free_elems(out))

    def reciprocal(self, out, in_, **kw):
        return self._r("rcp", _free_elems(out))

    def memset(self, ap, c):
        return self._r("ms", _free_elems(ap))

    def dma_start(self, out, in_, **kw):
        n = 1
        for d in out.shape:
            n *= d
        return self._r("dma", n)


def _op_dur(eng, r, state):
    k = r.kind
    if eng == "pe":
        if k == "tr":
            cfg, d = (128, 128), 56.0
        else:
            K_, M_, N_ = r.info["K"], r.info["M"], r.F
            cfg = (64 if 2 < K_ <= 64 else (32 if K_ <= 2 else 128), 64 if M_ <= 64 else 128)
            d = max(N_, 64) * 0.455 + 8
            prev = state.get("prevmm")
            if (prev is not None and prev["cfg"] == cfg and cfg != (128, 128) and cfg[0] != 32
                    and (prev["rb"], prev["cb"]) != (r.info["rowbase"], r.info["colbase"]) and not prev.get("paired")):
                state["prevmm"] = dict(cfg=cfg, rb=r.info["rowbase"], cb=r.info["colbase"], paired=True)
                return 4.0
        sw = 100.0 if state.get("cfg") not in (None, cfg) else 0.0
        state["cfg"] = cfg
        state["prevmm"] = dict(cfg=cfg, rb=r.info.get("rowbase", 0), cb=r.info.get("colbase", 0)) if k == "mm" else None
        return d + sw
    if eng == "act":
        if k == "dma":
            return 60.0
        return 0.88 * r.F + 110 + (93 if r.info.get("accum") else 0)
    if eng == "dve":
        if k == "rcp":
            d = 5.0 * r.F + 20
        elif k == "tt" and r.info.get("bf16"):
            d = 60 + 0.6 * r.F
        elif k in ("tt", "stt", "cp"):
            d = 70 + 1.2 * r.F
        else:
            d = 60 + 0.6 * r.F
        return max(270.0, 1.1 * d)
    if eng == "pool":
        return 900.0 if k == "dma" else 2.45 * r.F
    return 60.0


def _list_schedule(P, fixed=("sp", "pool"), SEM=70.0):
    ops = P.ops
    n = len(ops)
    engs_c = {}
    recs = []
    for op in ops:
        ce = engs_c.setdefault(op.eng, _CostEng())
        op.fn(ce)
        recs.append(ce.last)
    st = {}
    dur1 = [_op_dur(op.eng, recs[i], st.setdefault(op.eng, {})) for i, op in enumerate(ops)]
    node_of = [0] * n
    nodes = []
    for i, op in enumerate(ops):
        if op.eng == "pe" and nodes and ops[nodes[-1][-1]].eng == "pe" and nodes[-1][-1] == i - 1:
            nodes[-1].append(i)
        else:
            nodes.append([i])
        node_of[i] = len(nodes) - 1
    m = len(nodes)
    neng = [ops[nd[0]].eng for nd in nodes]
    ndur = [sum(dur1[i] for i in nd) for nd in nodes]
    nlat = [(2000.0 + recs[nd[0]].F * 4 / 200.0) if ops[nd[0]].dma is not None else 0.0 for nd in nodes]
    preds = [set() for _ in range(m)]
    for i, op in enumerate(ops):
        a = node_of[i]
        for d in list(op.deps.keys()) + list(op.force):
            b = node_of[d]
            if b != a:
                preds[a].add(b)
    last = {}
    for a in range(m):
        if neng[a] in fixed:
            if neng[a] in last:
                preds[a].add(last[neng[a]])
            last[neng[a]] = a
    succs = [[] for _ in range(m)]
    for a in range(m):
        for p in preds[a]:
            succs[p].append(a)
    prio = [0.0] * m
    for a in range(m - 1, -1, -1):
        mx = 0.0
        for s in succs[a]:
            if prio[s] > mx:
                mx = prio[s]
        prio[a] = ndur[a] + nlat[a] + mx
    npred = [len(p) for p in preds]
    ready_t = [0.0] * m
    avail = {}
    for a in range(m):
        if npred[a] == 0:
            avail.setdefault(neng[a], []).append(a)
    free = {}
    engs = sorted(set(neng))
    order_nodes = []
    while len(order_nodes) < m:
        best = None
        for e in engs:
            h = avail.get(e)
            if not h:
                continue
            t = free.get(e, 0.0)
            rdy = [a for a in h if ready_t[a] <= t]
            if rdy:
                a = max(rdy, key=lambda a: (prio[a], -a))
                s = t
            else:
                a = min(h, key=lambda a: (ready_t[a], -prio[a]))
                s = ready_t[a]
            if best is None or s < best[0] or (s == best[0] and prio[a] > prio[best[2]]):
                best = (s, e, a)
        s, e, a = best
        avail[e].remove(a)
        free[e] = s + ndur[a]
        fin = s + ndur[a] + nlat[a]
        order_nodes.append(a)
        for sc in succs[a]:
            npred[sc] -= 1
            same = neng[sc] == e and nlat[a] == 0.0
            r = fin + (0.0 if same else SEM)
            if r > ready_t[sc]:
                ready_t[sc] = r
            if npred[sc] == 0:
                avail.setdefault(neng[sc], []).append(sc)
    order = [i for a in order_nodes for i in nodes[a]]
    newpos = {old: new for new, old in enumerate(order)}
    new_ops = [ops[i] for i in order]
    for new, op in enumerate(new_ops):
        op.idx = new
        op.deps = {newpos[d]: raw for d, raw in op.deps.items()}
        op.force = tuple(newpos[d] for d in op.force)
    P.ops = new_ops


def build_program(nblk=NBLK, stages=("head", "attn", "tail"), dbg=False):
    nc = bass.Bass("TRN2", target_bir_lowering=False)

    def din(name, shape):
        return nc.dram_tensor(name, shape, F32, kind="ExternalInput").ap()

    x_d = din("x", [TOK, D])
    p_d = din("p", [TOK, PLE])
    w_in_d = din("w_in_r", [D, WALL])
    g1_d = din("g1", [128, 8])
    w_out_d = din("w_out", [D, D])
    w_gate_d = din("w_gate", [D, D])
    g2_d = din("g2", [128, 8])
    w_pp_d = din("w_pp", [PLE, D])
    gfin_d = din("gfin", [128, D])
    biasA_d = din("biasA", [128, 8 * 256])
    biasB_d = din("biasB", [128, 8 * 640])
    sink_d = din("sinkrow", [1, 512])
    ident_d = din("ident", [128, 128])
    out_d = nc.dram_tensor("out", [TOK, D], F32, kind="ExternalOutput").ap()

    P = Prog()
    es = ExitStack()
    with es:
        def sb(name, shape, dt):
            return es.enter_context(nc.sbuf_tensor(name, shape, dt))

        def ps(name, shape, dt):
            return es.enter_context(nc.psum_tensor(name, shape, dt))

        Wall = sb("Wall", [128, 8, WALL], BF16)
        Wout = sb("Wout", [128, 8, D], BF16)
        Wg = sb("Wg", [128, 8, D], BF16)
        Wp = sb("Wp", [128, 2, D], BF16)
        gfin = sb("gfin_t", [128, D], F32)
        EA = sb("EA", [128, 8, 256], BF16)
        EB = sb("EB", [128, 8, 640], BF16)
        g1 = sb("g1_t", [128, 8], F32)
        g2 = sb("g2_t", [128, 8], F32)
        ident = sb("ident_t", [128, 128], BF16)
        ones2 = sb("ones2", [128, 64], BF16)
        onesrow = sb("onesrow", [2, 256], BF16)
        skhl = sb("skhl", [2, 512], BF16)
        xring = sb("xring", [128, XR * D], F32)
        junk = sb("junk", [128, D], mybir.dt.float8e4)
        ubuf = [sb(f"ubuf{i}", [128, D], BF16) for i in range(2)]
        uT = sb("uT", [128, 8, TB], BF16)
        Qa = [sb(f"Qa{i}", [128, 4, TB], BF16) for i in range(2)]
        Qb = [sb(f"Qb{i}", [128, 4, TB], BF16) for i in range(2)]
        GT = [sb(f"GT{i}", [128, 8, TB], BF16) for i in range(2)]
        KaT = sb("KaT", [128, 2, 1024], BF16)
        KbT = sb("KbT", [128, 4, 1024], BF16)
        Vr = sb("Vr", [128, 8, 640], BF16)
        pbf = [sb(f"pbf{i}", [128, NT, PLE], BF16) for i in range(2)]
        pT = sb("pT", [128, 2, TB], BF16)
        Pt = [sb(f"Pt{i}", [128, 2, 2, 256], BF16) for i in range(3)]
        Rn = [sb("Rn0", [128, 256], F32)] * 2
        tgs = [sb("tgs0", [128, 512], F32)] * 2
        tnh = [sb(f"tnh{i}", [128, 512], BF16) for i in range(2)]
        skhi, sklo = tnh[0][0:1, :], tnh[1][0:1, :]
        ss = sb("ss", [128, 3 * NT], F32)
        sd = sb("sd", [128, 3 * NT], F32)
        rs = sb("rs", [128, 3 * NT], F32)

        banks = [ps(f"bank{i}", [128, 512], F32) for i in range(8)]
        Sb = banks[0:4]
        Ab = banks[4:6]
        Gb = banks[6:8]
        Tb = banks[7][:, :].bitcast(BF16).rearrange("p (a b) -> p a b", a=8)
        NGB = 2

        eng_names = ["pe", "act", "dve", "pool"]
        eng_sems = {n: es.enter_context(nc.semaphore("sem_" + n)) for n in eng_names}
        dma_keys = ([f"xl{i}" for i in range(XR)] + [f"xs{i}" for i in range(XR)]
                    + ["p0", "p1", "stg0", "stg1", "wout", "wp", "gfin", "ident", "g1", "g2", "sink"])
        dma_sems = {k: es.enter_context(nc.semaphore("dsem_" + k)) for k in dma_keys}

        gb_rot = [0]

        def next_gb():
            b = gb_rot[0] % NGB
            gb_rot[0] += 1
            return b

        def dma(eng, key, out, in_, reads, writes):
            P.add(eng, lambda e, o=out, i=in_: e.dma_start(out=o, in_=i), reads, writes, dma=key)

        dma("pool", "ident", ident[:], ident_d, [], [("ident",)])
        dma("sp", "g1", g1[:], g1_d, [], [("g1",)])
        dma("sp", "g2", g2[:], g2_d, [], [("g2",)])
        sk32 = tgs[0][0:1, :]
        dma("sp", "sink", sk32, sink_d, [], [("sk32",), ("tgs", 0)])
        dma("sp", "gfin", gfin[:], gfin_d, [], [("gfin",)])
        P.add("dve", lambda e: e.memset(ones2[:], 2.0), [], [("ones2",)])
        P.add("dve", lambda e: e.memset(onesrow[:], 1.0), [], [("onesrow",)])

        stg_i = [0]

        def staged(src_ap, ncols, consume):
            s = stg_i[0] % 2
            stg_i[0] += 1
            st_ap = xring[:, s * STG: s * STG + ncols]
            dma("sp", f"stg{s}", st_ap, src_ap, [], [("stg", s)])
            consume(st_ap, s)

        flip = [0]

        def scale_cast(out_ap, in_ap, scal_ap, s, extra_reads, writes):
            if flip[0] % 2 == 0:
                P.add("dve", lambda e: e.tensor_single_scalar(out=out_ap, in_=in_ap, scalar=scal_ap, op=ALU.mult),
                      [("stg", s)] + extra_reads, writes)
            else:
                P.add("act", lambda e: e.activation(out=out_ap, in_=in_ap, func=AF.Copy, scale=scal_ap),
                      [("stg", s)] + extra_reads, writes)
            flip[0] += 1

        for c in range(8):
            for hp in range(2):
                c0 = hp * (WALL // 2)
                n = WALL // 2
                staged(w_in_d[c * 128:(c + 1) * 128, c0:c0 + n], n,
                       lambda st, s, c=c, c0=c0, n=n: scale_cast(Wall[:, c, c0:c0 + n], st, g1[:, c:c + 1], s,
                                                                 [("g1",)], [("Wall", c, c0)]))
        for c in range(8):
            staged(w_gate_d[c * 128:(c + 1) * 128, :], D,
                   lambda st, s, c=c: scale_cast(Wg[:, c, :], st, g2[:, c:c + 1], s, [("g2",)], [("Wg", c)]))
        for hh in range(2):
            def consA(st, s, hh=hh):
                for q in range(4):
                    h = hh * 4 + q
                    P.add("act", lambda e, h=h, q=q: e.activation(out=EA[:, h, :], in_=st[:, q * 256:(q + 1) * 256], func=AF.Exp),
                          [("stg", s)], [("EA", h)])
            staged(biasA_d[:, hh * 1024:(hh + 1) * 1024], 1024, consA)
        for hh in range(2):
            def consB(st, s, hh=hh):
                for q in range(4):
                    h = hh * 4 + q
                    P.add("act", lambda e, h=h, q=q: e.activation(out=EB[:, h, :], in_=st[:, q * 640:(q + 1) * 640], func=AF.Exp),
                          [("stg", s)], [("EB", h)])
            staged(biasB_d[:, hh * 2560:(hh + 1) * 2560], 2560, consB)
        dma("pool", "wout", Wout[:], w_out_d.rearrange("(c p) n -> p c n", p=128), [], [("Wout",)])
        dma("pool", "wp", Wp[:], w_pp_d.rearrange("(c p) n -> p c n", p=128), [], [("Wp",)])
        P.add("act", lambda e: e.activation(out=sk32, in_=sk32, func=AF.Exp), [("sk32",)], [("sk32",), ("tgs", 0)])
        P.add("dve", lambda e: e.tensor_single_scalar(out=sk32, in_=sk32, scalar=2.0, op=ALU.mult), [("sk32",)], [("sk32",), ("tgs", 0)])
        P.add("dve", lambda e: e.tensor_copy(out=skhi, in_=sk32), [("sk32",)], [("skhi",), ("tnh", 0)])
        P.add("dve", lambda e: e.tensor_tensor(out=sklo, in0=sk32, in1=skhi, op=ALU.subtract), [("sk32",), ("skhi",), ("tgs", 0)], [("sklo",), ("tnh", 1)])
        dma_sems["skc"] = es.enter_context(nc.semaphore("dsem_skc"))
        dma("sp", "skc", skhl[0:1, :], skhi, [("skhi",), ("tnh", 0)], [("skhl", 0)])
        dma("sp", "skc", skhl[1:2, :], sklo, [("sklo",), ("tnh", 1)], [("skhl", 1)])

        wall_keys = [("Wall", c, c0) for c in range(8) for c0 in (0, WALL // 2)]
        wg_keys = [("Wg", c) for c in range(8)]

        def xslot_keys(slot):
            ks = [("x", slot)]
            lo, hi = slot * D, (slot + 1) * D
            for s in range(2):
                if lo < (s + 1) * STG and hi > s * STG:
                    ks.append(("stg", s))
            return ks

        def load_x_tile(J, t):
            g = J * NT + t
            slot = g % XR
            dma("sp", f"xl{slot}", xring[:, slot * D:(slot + 1) * D], x_d[g * 128:(g + 1) * 128, :],
                [], xslot_keys(slot))

        def load_x(J):
            for t in range(NT):
                load_x_tile(J, t)

        EPS_AP = sb("eps_t", [128, 1], F32)
        P.add("dve", lambda e: e.memset(EPS_AP[:], EPS), [], [("eps",)])
        u2T = sb("u2T", [128, 8, TB], BF16)

        def xtile(J, t):
            g = J * NT + t
            slot = g % XR
            return xring[:, slot * D:(slot + 1) * D], slot, g

        def sq_stats(J, t, col0):
            xs, slot, g = xtile(J, t)
            P.add("act", lambda e: e.activation(out=junk[:], in_=xs, func=AF.Square, accum_out=ss[:, col0 + t:col0 + t + 1], saturate=False),
                  [("x", slot)], [("junk",), ("ss", col0 + t)])

        def sqrt_recip(col0, ncol):
            groups = sorted({(cc // NT) * NT for cc in range(col0, col0 + ncol)})
            P.add("act", lambda e: e.activation(out=sd[:, col0:col0 + ncol], in_=ss[:, col0:col0 + ncol], func=AF.Sqrt,
                                                bias=EPS_AP[:, 0:1], scale=1.0 / D),
                  [("ss", cc) for cc in range(col0, col0 + ncol)] + [("eps",)], [("sd", gq) for gq in groups])
            P.add("dve", lambda e: e.reciprocal(out=rs[:, col0:col0 + ncol], in_=sd[:, col0:col0 + ncol]),
                  [("sd", gq) for gq in groups], [("rs", gq) for gq in groups])

        def scale_u(J, t, col0):
            xs, slot, g = xtile(J, t)
            us = t % 2
            P.add("act", lambda e: e.activation(out=ubuf[us][:], in_=xs, func=AF.Copy, scale=rs[:, col0 + t:col0 + t + 1]),
                  [("x", slot), ("rs", col0)], [("u", us)])

        def transposes_u(t, dstT, dkey):
            us = t % 2
            for cc in range(8):
                P.add("pe", lambda e, cc=cc: e.transpose(Tb[:, cc, :], ubuf[us][:, cc * 128:(cc + 1) * 128], ident[:]),
                      [("u", us), ("ident",)], [("psG", 1)])
            P.add("dve", lambda e: e.tensor_copy(out=dstT[:, :, t * 128:(t + 1) * 128], in_=Tb[:, 0:8, :]),
                  [("psG", 1)], [(dkey, t)])

        def transposes_p(J, t):
            par = J % 2
            for cc in range(2):
                P.add("pe", lambda e, cc=cc: e.transpose(Tb[:, cc, :], pbf[par][:, t, cc * 128:(cc + 1) * 128], ident[:]),
                      [("pbf", par), ("ident",)], [("psG", 1)])
            P.add("dve", lambda e: e.tensor_copy(out=pT[:, :, t * 128:(t + 1) * 128], in_=Tb[:, 0:2, :]),
                  [("psG", 1)], [("pT", t)])

        def head_io(J, do_x=True, do_p=True):
            par = J % 2
            if do_x and J + 1 < nblk:
                load_x(J + 1)
            if do_p:
                dma("pool", f"p{par}", pbf[par][:], p_d[J * TB:(J + 1) * TB, :].rearrange("(t q) f -> q t f", q=128),
                    [], [("pbf", par)])

        FM = [("Qa", 0, 0), ("Qa", 2, 2), ("Ka", 0, 4), ("Ga", 0, 6), ("Ga", 2, 8), ("Qb", 0, 10), ("Qb", 2, 12),
              ("Kb", 0, 14), ("Kb", 2, 16), ("Gb", 0, 18), ("Gb", 2, 20)]
        uT_keys = [("uT", t) for t in range(NT)]

        def proj_fm(J, n_ev):
            par = J % 2
            ringcol = (J * TB) % 1024
            rt0 = (J * NT) % 8
            kind, ci, f0 = FM[n_ev]
            gb = next_gb()
            for i in range(2):
                for cc in range(8):
                    P.add("pe", lambda e, i=i, cc=cc: e.matmul(
                        Gb[gb][:, i * TB:(i + 1) * TB], Wall[:, cc, (f0 + i) * 128:(f0 + i + 1) * 128], uT[:, cc, :],
                        start=(cc == 0), stop=(cc == 7)), wall_keys + uT_keys, [("psG", gb)])
            src3 = Gb[gb][:, :].rearrange("p (a b) -> p a b", a=2)
            if kind in ("Qa", "Qb", "Ka", "Kb"):
                if kind in ("Qa", "Qb"):
                    dst = (Qa if kind == "Qa" else Qb)[par][:, ci:ci + 2, :]
                    wk = [(kind, par, ci), (kind, par, ci + 1)]
                else:
                    dst = (KaT if kind == "Ka" else KbT)[:, ci:ci + 2, ringcol:ringcol + TB]
                    wk = [(kind, ci, (rt0 + t) % 8) for t in range(NT)]
                if n_ev % 2 == 0:
                    P.add("dve", lambda e: e.tensor_copy(out=dst, in_=src3), [("psG", gb)], wk)
                else:
                    P.add("act", lambda e: e.activation(out=dst, in_=src3, func=AF.Copy), [("psG", gb)], wk)
            else:
                gc = ci + (0 if kind == "Ga" else 4)
                ts_ = n_ev % 2
                tn3 = tnh[ts_][:, :].rearrange("p (a b) -> p a b", a=2)
                P.add("act", lambda e: e.activation(out=tn3, in_=src3, func=AF.Tanh, scale=0.5), [("psG", gb)], [("tnh", ts_)])
                dst = GT[par][:, gc:gc + 2, :]
                P.add("dve", lambda e: e.scalar_tensor_tensor(out=dst, in0=tn3, scalar=1.0, in1=src3, op0=ALU.add, op1=ALU.mult),
                      [("psG", gb), ("tnh", ts_)], [("GT", par, gc), ("GT", par, gc + 1)])

        def proj_v(J, t, which):
            rt = ((J * NT) % 8 + t) % 8
            gb = next_gb()
            if which == 0:
                for cc in range(8):
                    P.add("pe", lambda e, cc=cc: e.matmul(
                        Gb[gb][:, :], uT[:, cc, t * 128:(t + 1) * 128], Wall[:, cc, NFM * 128:NFM * 128 + 512],
                        start=(cc == 0), stop=(cc == 7)), wall_keys + [("uT", t)], [("psG", gb)])
                P.add("act", lambda e: e.activation(out=Vr[:, rt, 0:512], in_=Gb[gb][:, :], func=AF.Copy), [("psG", gb)], [("Vb", rt)])
            else:
                for cc in range(8):
                    P.add("pe", lambda e, cc=cc: e.matmul(
                        Gb[gb][:, 0:128], uT[:, cc, t * 128:(t + 1) * 128], Wall[:, cc, NFM * 128 + 512:NFM * 128 + 640],
                        start=(cc == 0), stop=(cc == 7)), wall_keys + [("uT", t)], [("psG", gb)])
                P.add("dve", lambda e: e.tensor_copy(out=Vr[:, rt, 512:640], in_=Gb[gb][:, 0:128]), [("psG", gb)], [("Va", rt)])

        def wout(J, t, half):
            par = J % 2
            xs, slot, g = xtile(J, t)
            gb = next_gb()
            for cc in range(8):
                ysrc = (Qa if cc < 4 else Qb)[par][:, cc % 4, t * 128:(t + 1) * 128]
                yk = [("Qa" if cc < 4 else "Qb", par, cc % 4)]
                P.add("pe", lambda e, cc=cc, ysrc=ysrc: e.matmul(
                    Gb[gb][:, :], ysrc, Wout[:, cc, half * 512:(half + 1) * 512], start=(cc == 0), stop=(cc == 7)),
                    yk + [("Wout",)], [("psG", gb)])
            P.add("dve", lambda e: e.tensor_tensor(out=xs[:, half * 512:(half + 1) * 512], in0=Gb[gb][:, :],
                                                  in1=xs[:, half * 512:(half + 1) * 512], op=ALU.add),
                  [("psG", gb), ("x", slot)], [("x", slot)])

        def gate(J, t, half):
            xs, slot, g = xtile(J, t)
            gbg = next_gb()
            for cc in range(8):
                P.add("pe", lambda e, cc=cc: e.matmul(
                    Gb[gbg][:, :], u2T[:, cc, t * 128:(t + 1) * 128], Wg[:, cc, half * 512:(half + 1) * 512],
                    start=(cc == 0), stop=(cc == 7)), [("u2T", t)] + wg_keys, [("psG", gbg)])
            ti = 0
            P.add("act", lambda e: e.activation(out=tgs[ti][:], in_=Gb[gbg][:, :], func=AF.Tanh, scale=0.5),
                  [("psG", gbg)], [("tgs", ti)])
            gbp = next_gb()
            for cc in range(2):
                P.add("pe", lambda e, cc=cc: e.matmul(
                    Gb[gbp][:, :], pT[:, cc, t * 128:(t + 1) * 128], Wp[:, cc, half * 512:(half + 1) * 512],
                    start=(cc == 0), stop=(cc == 1)), [("pT", t), ("Wp",)], [("psG", gbp)])
            P.add("dve", lambda e: e.scalar_tensor_tensor(out=tgs[ti][:], in0=tgs[ti][:], scalar=1.0, in1=Gb[gbp][:, :],
                                                         op0=ALU.add, op1=ALU.mult),
                  [("tgs", ti), ("psG", gbp)], [("tgs", ti)])
            P.add("dve", lambda e: e.scalar_tensor_tensor(out=xs[:, half * 512:(half + 1) * 512], in0=tgs[ti][:], scalar=0.5,
                                                         in1=xs[:, half * 512:(half + 1) * 512], op0=ALU.mult, op1=ALU.add),
                  [("tgs", ti), ("x", slot)], [("x", slot)])

        def final(J, t):
            xs, slot, g = xtile(J, t)
            P.add("dve", lambda e: e.scalar_tensor_tensor(out=xs, in0=xs, scalar=rs[:, 2 * NT + t:2 * NT + t + 1], in1=gfin[:],
                                                         op0=ALU.mult, op1=ALU.mult),
                  [("x", slot), ("rs", 2 * NT), ("gfin",)], [("x", slot)])
            dma("sp", f"xs{slot}", out_d[g * 128:(g + 1) * 128, :], xs, [("x", slot)], [])

        def head(J):
            head_io(J)
            for t in range(NT):
                sq_stats(J, t, 0)
            sqrt_recip(0, NT)
            yield
            for t in range(NT):
                scale_u(J, t, 0)
                transposes_u(t, uT, "uT")
                yield
            for n_ev in range(len(FM)):
                proj_fm(J, n_ev)
                yield
            for t in range(NT):
                for which in range(2):
                    proj_v(J, t, which)
                    yield

        def tail(J):
            for t in range(NT):
                for half in range(2):
                    wout(J, t, half)
                    yield
                sq_stats(J, t, NT)
            sqrt_recip(NT, NT)
            yield
            for t in range(NT):
                scale_u(J, t, NT)
                transposes_u(t, u2T, "u2T")
                yield
                transposes_p(J, t)
                yield
            for t in range(NT):
                for half in range(2):
                    gate(J, t, half)
                    yield
                sq_stats(J, t, 2 * NT)
            sqrt_recip(2 * NT, NT)
            yield
            for t in range(NT):
                final(J, t)
                yield

        def stream2(J):
            H = J + 2
            if J + 3 < nblk:
                load_x(J + 3)
            for t in range(NT):
                for half in range(2):
                    wout(J, t, half)
                    yield
                sq_stats(J, t, NT)
            for t in range(NT):
                sq_stats(H, t, 0)
            sqrt_recip(0, 2 * NT)
            yield
            for t in range(NT):
                scale_u(H, t, 0)
            for t in range(NT):
                transposes_p(J, t)
                yield "switch"
            head_io(H, do_x=False)
            for t in range(NT):
                transposes_u(t, uT, "uT")
                yield "switch"
            for t in range(NT):
                scale_u(J, t, NT)
            for n_ev in range(0, 3):
                proj_fm(H, n_ev)
                yield
            for t in range(NT):
                transposes_u(t, u2T, "u2T")
                yield "switch"
            for n_ev in range(3, 6):
                proj_fm(H, n_ev)
                yield
            rest_h = [lambda n_ev=n_ev: proj_fm(H, n_ev) for n_ev in range(6, len(FM))] + \
                     [lambda t=t, which=which: proj_v(H, t, which) for t in range(NT) for which in range(2)]
            gates = [(t, half) for t in range(NT) for half in range(2)]
            gi = 0
            for i, fn in enumerate(rest_h):
                fn()
                yield
                if i % 2 == 1 and gi < len(gates):
                    t, half = gates[gi]
                    gate(J, t, half)
                    if half == 1:
                        sq_stats(J, t, 2 * NT)
                    gi += 1
                    yield
            while gi < len(gates):
                t, half = gates[gi]
                gate(J, t, half)
                if half == 1:
                    sq_stats(J, t, 2 * NT)
                gi += 1
                yield
            sqrt_recip(2 * NT, NT)
            yield
            for t in range(NT):
                final(J, t)
                yield

        pair_ctr = [0]
        unit_ctr = [0]

        def attn(J):
            par = J % 2
            s = J // BLK_PER_SEQ
            jl = J % BLK_PER_SEQ
            units = []
            for mixer in ("A", "B"):
                for j in range(4):
                    if mixer == "A":
                        kps, span = range(2 * jl - 1, 2 * jl + 2), 256
                    else:
                        kps, span = range(2 * jl - 4, 2 * jl + 2), 640
                    ul = []
                    for kp in kps:
                        if kp < 0:
                            continue
                        q_lo = max(128 * kp, TB * jl)
                        q_hi = min(128 * kp + span, TB * jl + TB)
                        if q_hi <= q_lo:
                            continue
                        ul.append((kp, q_lo - 128 * kp, q_hi - q_lo, q_lo - TB * jl))
                    pid = pair_ctr[0]
                    pair_ctr[0] += 1
                    for i, u in enumerate(ul):
                        uid = unit_ctr[0]
                        unit_ctr[0] += 1
                        units.append(dict(mixer=mixer, j=j, kp=u[0], r0=u[1], w=u[2], c0=u[3],
                                          first=(i == 0), last=(i == len(ul) - 1), pid=pid, uid=uid))

            sus = [units[i:i + 2] for i in range(0, len(units), 2)]

            def qk(m):
                X, Y = Sb[2 * (m % 2)], Sb[2 * (m % 2) + 1]
                for i, u in enumerate(sus[m]):
                    mixer, j, w, c0 = u["mixer"], u["j"], u["w"], u["c0"]
                    rt = (s * 16 + u["kp"]) % 8
                    if mixer == "A":
                        kv = j // 2
                        Kt = KaT[:, kv, rt * 128:(rt + 1) * 128]
                        Qt = Qa[par][:, j, c0:c0 + w]
                        rk = [("Ka", 0, rt), ("Qa", par, j)]
                    else:
                        Kt = KbT[:, j, rt * 128:(rt + 1) * 128]
                        Qt = Qb[par][:, j, c0:c0 + w]
                        rk = [("Kb", (j // 2) * 2, rt), ("Qb", par, j)]
                    for hh, bank in ((0, X), (1, Y)):
                        P.add("pe", lambda e, hh=hh, bank=bank, Kt=Kt, Qt=Qt, i=i, w=w: e.matmul(
                            bank[:, i * 256:i * 256 + w], Kt[hh * 64:(hh + 1) * 64, :], Qt[hh * 64:(hh + 1) * 64, :],
                            start=True, stop=True), rk, [("psS", 2 * (m % 2) + hh)])

            def front(m):
                su = sus[m]
                pt = m % 3
                nu = len(su)
                wmax = max(u["w"] for u in su)
                pk = [("Pt", pt, i) for i in range(nu)]
                for hh in range(2):
                    bank3 = Sb[2 * (m % 2) + hh][:, :].rearrange("p (a b) -> p a b", a=2)
                    bk = [("psS", 2 * (m % 2) + hh)]
                    P.add("act", lambda e, hh=hh, bank3=bank3: e.activation(
                        out=Pt[pt][:, 0:nu, hh, 0:wmax], in_=bank3[:, 0:nu, 0:wmax], func=AF.Exp, scale=0.125), bk, pk)
                for i, u in enumerate(su):
                    mixer, j, w, r0 = u["mixer"], u["j"], u["w"], u["r0"]
                    Et = EA if mixer == "A" else EB
                    ek = [("EA" if mixer == "A" else "EB", 2 * j), ("EA" if mixer == "A" else "EB", 2 * j + 1)]
                    P.add(EMUL_ENG, lambda e, i=i, j=j, w=w, r0=r0, Et=Et: e.tensor_tensor(
                        out=Pt[pt][:, i, :, 0:w], in0=Pt[pt][:, i, :, 0:w], in1=Et[:, 2 * j:2 * j + 2, r0:r0 + w], op=ALU.mult),
                        [("Pt", pt, i)] + ek, [("Pt", pt, i)])

            def back(m):
                pt = m % 3
                for i, u in enumerate(sus[m]):
                    back_unit(u, Pt[pt], i, pt)

            def back_unit(u, Ptile, i, pt):
                mixer, j, w, c0 = u["mixer"], u["j"], u["w"], u["c0"]
                rt = (s * 16 + u["kp"]) % 8
                ab = u["pid"] % 2
                if mixer == "A":
                    kv = j // 2
                    V0 = Vr[:, rt, 512 + kv * 64:512 + kv * 64 + 64]
                    V1 = V0
                    vk = [("Va", rt)]
                else:
                    V0 = Vr[:, rt, (2 * j) * 64:(2 * j + 1) * 64]
                    V1 = Vr[:, rt, (2 * j + 1) * 64:(2 * j + 2) * 64]
                    vk = [("Vb", rt)]
                first = u["first"]
                acc = Ab[ab]
                P.add("pe", lambda e: e.matmul(acc[0:64, c0:c0 + w], V0, Ptile[:, i, 0, 0:w], start=first, stop=True,
                                               skip_group_check=True, tile_position=(0, 0)),
                      [("Pt", pt, i)] + vk, [("psA", ab)])
                P.add("pe", lambda e: e.matmul(acc[64:128, c0:c0 + w], V1, Ptile[:, i, 1, 0:w], start=first, stop=True,
                                               skip_group_check=True, tile_position=(0, 64)),
                      [("Pt", pt, i)] + vk, [("psA", ab)])
                P.add("pe", lambda e: e.matmul(acc[0:64, 256 + c0:256 + c0 + w], ones2[:, 0:64], Ptile[:, i, 0, 0:w],
                                               start=False, stop=True, skip_group_check=True, tile_position=(0, 0)),
                      [("Pt", pt, i), ("ones2",)], [("psA", ab)])
                P.add("pe", lambda e: e.matmul(acc[64:128, 256 + c0:256 + c0 + w], ones2[:, 0:64], Ptile[:, i, 1, 0:w],
                                               start=False, stop=True, skip_group_check=True, tile_position=(0, 64)),
                      [("Pt", pt, i), ("ones2",)], [("psA", ab)])
                if not u["last"]:
                    return
                if mixer == "A":
                    P.add("pe", lambda e: e.matmul(acc[:, 256:512], skhl[0:2, j * 128:(j + 1) * 128], onesrow[0:2, :],
                                                   start=False, stop=True, skip_group_check=True),
                          [("skhl", 0), ("skhl", 1), ("onesrow",)], [("psA", ab)])
                rn = 0
                gc = j if mixer == "A" else 4 + j
                ydst = (Qa if mixer == "A" else Qb)[par][:, j, :]
                yk = [("Qa" if mixer == "A" else "Qb", par, j)]
                P.add("dve", lambda e: e.reciprocal(out=Rn[rn][:], in_=acc[:, 256:512]), [("psA", ab)], [("Rn", rn)])
                P.add("dve", lambda e: e.tensor_tensor(out=Rn[rn][:], in0=Rn[rn][:], in1=GT[par][:, gc, :], op=ALU.mult),
                      [("Rn", rn), ("GT", par, gc)], [("Rn", rn)])
                P.add("dve", lambda e: e.tensor_tensor(out=ydst, in0=acc[:, 0:256], in1=Rn[rn][:], op=ALU.mult),
                      [("psA", ab), ("Rn", rn)], yk)

            M = len(sus)
            LAG = 2
            for k in range(0, M + LAG, 2):
                for m in (k, k + 1):
                    if m < M:
                        qk(m)
                for m in (k, k + 1):
                    if LAG <= m < M + LAG:
                        back(m - LAG)
                for m in (k, k + 1):
                    if m < M:
                        front(m)
                yield

        def drain(g):
            for _ in g:
                pass

        def interleave(ga, gb, nb=1):
            da = db = False
            while not (da and db):
                if not da:
                    try:
                        next(ga)
                    except StopIteration:
                        da = True
                for _ in range(nb):
                    if not db:
                        try:
                            if next(gb) == "switch" and not da:
                                break
                        except StopIteration:
                            db = True

        def chain(*gens):
            for g in gens:
                yield from g

        def empty():
            return
            yield

        load_x(0)
        drain(head(0))
        if nblk > 1:
            interleave(attn(0), head(1))
        else:
            drain(attn(0))
        for J in range(nblk):
            s2 = stream2(J) if J + 2 < nblk else tail(J)
            if J + 1 < nblk:
                interleave(attn(J + 1), s2, nb=2)
            else:
                drain(s2)
        if dbg:
            allk = list(P.last_writer.keys())
            dumps = dict(uT=uT, Qa0=Qa[0], Qb0=Qb[0], GT0=GT[0], KaT=KaT, KbT=KbT, Vr=Vr, EA=EA, EB=EB, xring=xring,
                         Wg=Wg, Wp=Wp, Wout=Wout, skhi=skhi, sklo=sklo, rs=rs, pT=pT)
            for nm, tl in dumps.items():
                shp = list(tl[:].shape)
                dd = nc.dram_tensor("dbg_" + nm, shp, tl[:].dtype, kind="ExternalOutput").ap()
                key = "dbg_" + nm
                dma_sems[key] = es.enter_context(nc.semaphore("dsem_" + key))
                dma("sp", key, dd, tl[:], allk, [])

        if RESCHED:
            _list_schedule(P)
        P.finalize()

        with nc.Block() as block:
            @block.sync
            def _(e):
                fw = [k for k in P.dma_total if k.startswith("xs") or k.startswith("dbg_")]
                P.emit_engine("sp", e, eng_sems, dma_sems, final_dma_waits=fw)

            @block.gpsimd
            def _(e):
                P.emit_engine("pool", e, eng_sems, dma_sems)

            @block.tensor
            def _(e):
                P.emit_engine("pe", e, eng_sems, dma_sems)

            @block.scalar
            def _(e):
                P.emit_engine("act", e, eng_sems, dma_sems)

            @block.vector
            def _(e):
                P.emit_engine("dve", e, eng_sems, dma_sems)
    return nc


def _host_layout(x, p, norm_g, w_in, sink_a, rel_bias_b, w_out, ple_norm_g, w_ple_proj, w_ple_gate, final_norm_g):
    f = np.float32
    w = np.asarray(w_in, f)[0]
    qa, ka, va, ga = w[:, 0:512], w[:, 512:640], w[:, 640:768], w[:, 768:1280]
    qb, kb, vb, gb = w[:, 1280:1792], w[:, 1792:2304], w[:, 2304:2816], w[:, 2816:3328]
    ka_dup = np.concatenate([ka[:, 0:64], ka[:, 0:64], ka[:, 64:128], ka[:, 64:128]], axis=1)
    w_in_r = np.ascontiguousarray(np.concatenate([qa, ka_dup, ga, qb, kb, gb, vb, va], axis=1))
    assert w_in_r.shape == (D, WALL)
    g1 = np.ascontiguousarray(np.asarray(norm_g, f)[0].reshape(8, 128).T)
    g2 = np.ascontiguousarray(np.asarray(ple_norm_g, f)[0].reshape(8, 128).T)
    gfin = np.ascontiguousarray(np.tile(np.asarray(final_norm_g, f)[None, :], (128, 1)))
    ki = np.arange(128)[:, None]
    rA = np.arange(256)[None, :]
    slopes = np.asarray(2.0 ** (-8.0 * np.arange(1, 9) / 8), dtype=f)
    distA = np.abs(rA - ki).astype(f)
    maskA = ((ki >= 64) & (rA < 64)) | ((ki < 64) & (rA >= 192))
    biasA = np.empty((128, 8, 256), f)
    for h in range(8):
        biasA[:, h, :] = np.where(maskA, f(MASK_NEG), -slopes[h] * distA)
    rB = np.arange(640)[None, :]
    idx = np.clip(rB - ki, -128, 128) + 128
    maskB = ((ki >= 64) & (rB < 64)) | ((ki < 64) & (rB >= 576))
    tab = np.asarray(rel_bias_b, f)[0]
    biasB = np.empty((128, 8, 640), f)
    for h in range(8):
        biasB[:, h, :] = np.where(maskB, f(MASK_NEG), tab[h][idx])
    sk = np.asarray(sink_a, f)[0]
    sinkrow = np.empty((1, 512), f)
    for j in range(4):
        sinkrow[0, j * 128:j * 128 + 64] = sk[2 * j]
        sinkrow[0, j * 128 + 64:(j + 1) * 128] = sk[2 * j + 1]
    shared = dict(
        w_in_r=w_in_r, g1=g1, w_out=np.ascontiguousarray(np.asarray(w_out, f)[0]),
        w_gate=np.ascontiguousarray(np.asarray(w_ple_gate, f)[0]), g2=g2,
        w_pp=np.ascontiguousarray(np.asarray(w_ple_proj, f)[0]), gfin=gfin,
        biasA=np.ascontiguousarray(biasA.reshape(128, 8 * 256)), biasB=np.ascontiguousarray(biasB.reshape(128, 8 * 640)),
        sinkrow=sinkrow, ident=np.eye(128, dtype=f),
    )
    xf = np.asarray(x, f).reshape(NCORES, TOK, D)
    pf = np.asarray(p, f)[0].reshape(NCORES, TOK, PLE)
    return [dict(shared, x=np.ascontiguousarray(xf[c]), p=np.ascontiguousarray(pf[c])) for c in range(NCORES)]


_NC_CACHE = {}


def kernel(x, p, norm_g, w_in, sink_a, rel_bias_b, w_out, ple_norm_g, w_ple_proj, w_ple_gate, final_norm_g):
    in_maps = _host_layout(x, p, norm_g, w_in, sink_a, rel_bias_b, w_out, ple_norm_g, w_ple_proj, w_ple_gate, final_norm_g)
    if "nc" not in _NC_CACHE:
        _NC_CACHE["nc"] = build_program()
    res = run_bass_kernel_spmd(_NC_CACHE["nc"], in_maps, core_ids=list(range(NCORES)))
    outs = [np.asarray(r["out"], np.float32).reshape(SEQ_PER_CORE, SEQ, D) for r in res.results]
    return np.concatenate(outs, axis=0)
```

```python
import numpy as np
from contextlib import ExitStack
import concourse.bass as bass
import concourse.mybir as mybir
from concourse.bass_utils import run_bass_kernel_spmd

F32 = mybir.dt.float32
BF16 = mybir.dt.bfloat16
ALU = mybir.AluOpType
AF = mybir.ActivationFunctionType

NCORES = 8
D = 1024
SEQ = 2048
SEQ_PER_CORE = 4
TOK = SEQ_PER_CORE * SEQ
TB = 256
NT = TB // 128
NBLK = TOK // TB
BLK_PER_SEQ = SEQ // TB
XR = 8
PLE = 256
EPS = 1e-6
NFM = 22
WALL = NFM * 128 + 640
STG = 3072
STG_OFF = 2 * D
MASK_NEG = -200.0
RESCHED = True
EMUL_ENG = "pool"


class _Op:
    __slots__ = ("eng", "fn", "deps", "dma", "idx", "signal", "sigval", "dma_val", "need_eng", "need_dma", "force")


class Prog:
    def __init__(self):
        self.ops = []
        self.last_writer = {}
        self.readers = {}

    def add(self, eng, fn, reads=(), writes=(), dma=None, after=()):
        idx = len(self.ops)
        deps = {}
        for k in reads:
            w = self.last_writer.get(k)
            if w is not None:
                deps[w] = True
        for k in writes:
            w = self.last_writer.get(k)
            if w is not None:
                deps.setdefault(w, False)
            for r in self.readers.get(k, ()):
                deps.setdefault(r, False)
        op = _Op()
        op.eng, op.fn, op.deps, op.dma, op.idx = eng, fn, deps, dma, idx
        op.signal = False
        op.sigval = 0
        op.dma_val = 0
        op.force = tuple(after)
        self.ops.append(op)
        for k in reads:
            self.readers.setdefault(k, []).append(idx)
        for k in writes:
            self.last_writer[k] = idx
            self.readers[k] = []
        return idx

    def finalize(self):
        ops = self.ops
        dma_count = {}
        for op in ops:
            if op.dma is not None:
                dma_count[op.dma] = dma_count.get(op.dma, 0) + 1
                op.dma_val = 16 * dma_count[op.dma]
        for op in ops:
            need_eng, need_dma = {}, {}
            for d, raw in op.deps.items():
                p = ops[d]
                if p.dma is not None:
                    if p.dma_val > need_dma.get(p.dma, 0):
                        need_dma[p.dma] = p.dma_val
                else:
                    if p.eng == op.eng and (op.eng == "pe" or not raw):
                        continue
                    if d > need_eng.get(p.eng, -1):
                        need_eng[p.eng] = d
            for d in op.force:
                if d > need_eng.get(ops[d].eng, -1):
                    need_eng[ops[d].eng] = d
            op.need_eng, op.need_dma = need_eng, need_dma
            for d in need_eng.values():
                ops[d].signal = True
        cnt = {}
        for op in ops:
            if op.dma is None and op.signal:
                cnt[op.eng] = cnt.get(op.eng, 0) + 1
                op.sigval = cnt[op.eng]
        self.dma_total = {k: 16 * v for k, v in dma_count.items()}

    def emit_engine(self, eng_name, e, eng_sems, dma_sems, final_dma_waits=()):
        ops = self.ops
        waited = {}
        for op in ops:
            if op.eng != eng_name:
                continue
            for pe_name, d in op.need_eng.items():
                v = ops[d].sigval
                key = ("e", pe_name)
                if waited.get(key, 0) < v:
                    e.wait_ge(eng_sems[pe_name], v)
                    waited[key] = v
            for k, v in op.need_dma.items():
                key = ("d", k)
                if waited.get(key, 0) < v:
                    e.wait_ge(dma_sems[k], v)
                    waited[key] = v
            ins = op.fn(e)
            if op.dma is not None:
                ins.then_inc(dma_sems[op.dma], 16)
            elif op.signal:
                ins.then_inc(eng_sems[eng_name], 1)
        for k in final_dma_waits:
            e.wait_ge(dma_sems[k], self.dma_total[k])


def _free_elems(ap):
    n = 1
    for d in ap.shape[1:]:
        n *= d
    return n


class _Rec:
    __slots__ = ("kind", "F", "info")

    def then_inc(self, *a, **k):
        return self


class _CostEng:
    def __init__(self):
        self.last = None

    def _r(self, kind, F, **info):
        r = _Rec()
        r.kind, r.F, r.info = kind, F, info
        self.last = r
        return r

    def matmul(self, out, lhsT, rhs, start=None, stop=None, skip_group_check=False, tile_position=None, **kw):
        return self._r("mm", _free_elems(rhs), K=lhsT.shape[0], M=_free_elems(lhsT),
                       rowbase=lhsT.base_partition(), colbase=out.base_partition())

    def transpose(self, out, in_, ident):
        return self._r("tr", 128)

    def activation(self, out, in_, func, bias=0.0, scale=1.0, accum_out=None, **kw):
        return self._r("act", _free_elems(in_), func=str(func), accum=accum_out is not None)

    def tensor_tensor(self, out, in0, in1, op, **kw):
        ps = ("bank" in in0.name) or ("bank" in in1.name)
        return self._r("tt", _free_elems(out), bf16=("bfloat16" in str(out.dtype)) and not ps)

    def tensor_copy(self, out, in_, **kw):
        return self._r("cp", _free_elems(out))

    def tensor_single_scalar(self, out, in_, scalar, op, **kw):
        return self._r("ts", _free_elems(out))

    def scalar_tensor_tensor(self, out, in0, scalar, in1, op0, op1, **kw):
        return self._r("stt", _free_elems(out))

    def reciprocal(self, out, in_, **kw):
        return self._r("rcp", _free_elems(out))

    def memset(self, ap, c):
        return self._r("ms", _free_elems(ap))

    def dma_start(self, out, in_, **kw):
        n = 1
        for d in out.shape:
            n *= d
        return self._r("dma", n)


def _op_dur(eng, r, state):
    k = r.kind
    if eng == "pe":
        if k == "tr":
            cfg, d = (128, 128), 56.0
        else:
            K_, M_, N_ = r.info["K"], r.info["M"], r.F
            cfg = (64 if 2 < K_ <= 64 else (32 if K_ <= 2 else 128), 64 if M_ <= 64 else 128)
            d = max(N_, 64) * 0.455 + 8
            prev = state.get("prevmm")
            if (prev is not None and prev["cfg"] == cfg and cfg != (128, 128) and cfg[0] != 32
                    and (prev["rb"], prev["cb"]) != (r.info["rowbase"], r.info["colbase"]) and not prev.get("paired")):
                state["prevmm"] = dict(cfg=cfg, rb=r.info["rowbase"], cb=r.info["colbase"], paired=True)
                return 4.0
        sw = 100.0 if state.get("cfg") not in (None, cfg) else 0.0
        state["cfg"] = cfg
        state["prevmm"] = dict(cfg=cfg, rb=r.info.get("rowbase", 0), cb=r.info.get("colbase", 0)) if k == "mm" else None
        return d + sw
    if eng == "act":
        if k == "dma":
            return 60.0
        return 0.88 * r.F + 110 + (93 if r.info.get("accum") else 0)
    if eng == "dve":
        if k == "rcp":
            d = 5.0 * r.F + 20
        elif k == "tt" and r.info.get("bf16"):
            d = 60 + 0.6 * r.F
        elif k in ("tt", "stt", "cp"):
            d = 70 + 1.2 * r.F
        else:
            d = 60 + 0.6 * r.F
        return max(270.0, 1.1 * d)
    if eng == "pool":
        return 900.0 if k == "dma" else 2.45 * r.F
    return 60.0


def _list_schedule(P, fixed=("sp", "pool"), SEM=70.0):
    ops = P.ops
    n = len(ops)
    engs_c = {}
    recs = []
    for op in ops:
        ce = engs_c.setdefault(op.eng, _CostEng())
        op.fn(ce)
        recs.append(ce.last)
    st = {}
    dur1 = [_op_dur(op.eng, recs[i], st.setdefault(op.eng, {})) for i, op in enumerate(ops)]
    node_of = [0] * n
    nodes = []
    for i, op in enumerate(ops):
        if op.eng == "pe" and nodes and ops[nodes[-1][-1]].eng == "pe" and nodes[-1][-1] == i - 1:
            nodes[-1].append(i)
        else:
            nodes.append([i])
        node_of[i] = len(nodes) - 1
    m = len(nodes)
    neng = [ops[nd[0]].eng for nd in nodes]
    ndur = [sum(dur1[i] for i in nd) for nd in nodes]
    nlat = [(2000.0 + recs[nd[0]].F * 4 / 200.0) if ops[nd[0]].dma is not None else 0.0 for nd in nodes]
    preds = [set() for _ in range(m)]
    for i, op in enumerate(ops):
        a = node_of[i]
        for d in list(op.deps.keys()) + list(op.force):
            b = node_of[d]
            if b != a:
                preds[a].add(b)
    last = {}
    for a in range(m):
        if neng[a] in fixed:
            if neng[a] in last:
                preds[a].add(last[neng[a]])
            last[neng[a]] = a
    succs = [[] for _ in range(m)]
    for a in range(m):
        for p in preds[a]:
            succs[p].append(a)
    prio = [0.0] * m
    for a in range(m - 1, -1, -1):
        mx = 0.0
        for s in succs[a]:
            if prio[s] > mx:
                mx = prio[s]
        prio[a] = ndur[a] + nlat[a] + mx
    npred = [len(p) for p in preds]
    ready_t = [0.0] * m
    avail = {}
    for a in range(m):
        if npred[a] == 0:
            avail.setdefault(neng[a], []).append(a)
    free = {}
    engs = sorted(set(neng))
    order_nodes = []
    while len(order_nodes) < m:
        best = None
        for e in engs:
            h = avail.get(e)
            if not h:
                continue
            t = free.get(e, 0.0)
            rdy = [a for a in h if ready_t[a] <= t]
            if rdy:
                a = max(rdy, key=lambda a: (prio[a], -a))
                s = t
            else:
                a = min(h, key=lambda a: (ready_t[a], -prio[a]))
                s = ready_t[a]
            if best is None or s < best[0] or (s == best[0] and prio[a] > prio[best[2]]):
                best = (s, e, a)
        s, e, a = best
        avail[e].remove(a)
        free[e] = s + ndur[a]
        fin = s + ndur[a] + nlat[a]
        order_nodes.append(a)
        for sc in succs[a]:
            npred[sc] -= 1
            same = neng[sc] == e and nlat[a] == 0.0
            r = fin + (0.0 if same else SEM)
            if r > ready_t[sc]:
                ready_t[sc] = r
            if npred[sc] == 0:
                avail.setdefault(neng[sc], []).append(sc)
    order = [i for a in order_nodes for i in nodes[a]]
    newpos = {old: new for new, old in enumerate(order)}
    new_ops = [ops[i] for i in order]
    for new, op in enumerate(new_ops):
        op.idx = new
        op.deps = {newpos[d]: raw for d, raw in op.deps.items()}
        op.force = tuple(newpos[d] for d in op.force)
    P.ops = new_ops


def build_program(nblk=NBLK, stages=("head", "attn", "tail"), dbg=False):
    nc = bass.Bass("TRN2", target_bir_lowering=False)

    def din(name, shape):
        return nc.dram_tensor(name, shape, F32, kind="ExternalInput").ap()

    x_d = din("x", [TOK, D])
    p_d = din("p", [TOK, PLE])
    w_in_d = din("w_in_r", [D, WALL])
    g1_d = din("g1", [128, 8])
    w_out_d = din("w_out", [D, D])
    w_gate_d = din("w_gate", [D, D])
    g2_d = din("g2", [128, 8])
    w_pp_d = din("w_pp", [PLE, D])
    gfin_d = din("gfin", [128, D])
    biasA_d = din("biasA", [128, 8 * 256])
    biasB_d = din("biasB", [128, 8 * 640])
    sink_d = din("sinkrow", [1, 512])
    ident_d = din("ident", [128, 128])
    out_d = nc.dram_tensor("out", [TOK, D], F32, kind="ExternalOutput").ap()

    P = Prog()
    es = ExitStack()
    with es:
        def sb(name, shape, dt):
            return es.enter_context(nc.sbuf_tensor(name, shape, dt))

        def ps(name, shape, dt):
            return es.enter_context(nc.psum_tensor(name, shape, dt))

        Wall = sb("Wall", [128, 8, WALL], BF16)
        Wout = sb("Wout", [128, 8, D], BF16)
        Wg = sb("Wg", [128, 8, D], BF16)
        Wp = sb("Wp", [128, 2, D], BF16)
        gfin = sb("gfin_t", [128, D], F32)
        EA = sb("EA", [128, 8, 256], BF16)
        EB = sb("EB", [128, 8, 640], BF16)
        g1 = sb("g1_t", [128, 8], F32)
        g2 = sb("g2_t", [128, 8], F32)
        ident = sb("ident_t", [128, 128], BF16)
        ones2 = sb("ones2", [128, 64], BF16)
        onesrow = sb("onesrow", [2, 256], BF16)
        skhl = sb("skhl", [2, 512], BF16)
        xring = sb("xring", [128, XR * D], F32)
        junk = sb("junk", [128, D], mybir.dt.float8e4)
        ubuf = [sb(f"ubuf{i}", [128, D], BF16) for i in range(2)]
        uT = sb("uT", [128, 8, TB], BF16)
        Qa = [sb(f"Qa{i}", [128, 4, TB], BF16) for i in range(2)]
        Qb = [sb(f"Qb{i}", [128, 4, TB], BF16) for i in range(2)]
        GT = [sb(f"GT{i}", [128, 8, TB], BF16) for i in range(2)]
        KaT = sb("KaT", [128, 2, 1024], BF16)
        KbT = sb("KbT", [128, 4, 1024], BF16)
        Vr = sb("Vr", [128, 8, 640], BF16)
        pbf = [sb(f"pbf{i}", [128, NT, PLE], BF16) for i in range(2)]
        pT = sb("pT", [128, 2, TB], BF16)
        Pt = [sb(f"Pt{i}", [128, 2, 2, 256], BF16) for i in range(3)]
        Rn = [sb("Rn0", [128, 256], F32)] * 2
        tgs = [sb("tgs0", [128, 512], F32)] * 2
        tnh = [sb(f"tnh{i}", [128, 512], BF16) for i in range(2)]
        skhi, sklo = tnh[0][0:1, :], tnh[1][0:1, :]
        ss = sb("ss", [128, 3 * NT], F32)
        sd = sb("sd", [128, 3 * NT], F32)
        rs = sb("rs", [128, 3 * NT], F32)

        banks = [ps(f"bank{i}", [128, 512], F32) for i in range(8)]
        Sb = banks[0:4]
        Ab = banks[4:6]
        Gb = banks[6:8]
        Tb = banks[7][:, :].bitcast(BF16).rearrange("p (a b) -> p a b", a=8)
        NGB = 2

        eng_names = ["pe", "act", "dve", "pool"]
        eng_sems = {n: es.enter_context(nc.semaphore("sem_" + n)) for n in eng_names}
        dma_keys = ([f"xl{i}" for i in range(XR)] + [f"xs{i}" for i in range(XR)]
                    + ["p0", "p1", "stg0", "stg1", "wout", "wp", "gfin", "ident", "g1", "g2", "sink"])
        dma_sems = {k: es.enter_context(nc.semaphore("dsem_" + k)) for k in dma_keys}

        gb_rot = [0]

        def next_gb():
            b = gb_rot[0] % NGB
            gb_rot[0] += 1
            return b

        def dma(eng, key, out, in_, reads, writes):
            P.add(eng, lambda e, o=out, i=in_: e.dma_start(out=o, in_=i), reads, writes, dma=key)

        dma("pool", "ident", ident[:], ident_d, [], [("ident",)])
        dma("sp", "g1", g1[:], g1_d, [], [("g1",)])
        dma("sp", "g2", g2[:], g2_d, [], [("g2",)])
        sk32 = tgs[0][0:1, :]
        dma("sp", "sink", sk32, sink_d, [], [("sk32",), ("tgs", 0)])
        for g0 in range(NT):
            dma("sp", f"xl{g0}", xring[:, g0 * D:(g0 + 1) * D], x_d[g0 * 128:(g0 + 1) * 128, :], [], [("x", g0)])
        P.add("dve", lambda e: e.memset(ones2[:], 2.0), [], [("ones2",)])
        P.add("dve", lambda e: e.memset(onesrow[:], 1.0), [], [("onesrow",)])

        stg_i = [0]

        def staged(src_ap, ncols, consume):
            s = stg_i[0] % 2
            stg_i[0] += 1
            st_ap = xring[:, STG_OFF + s * STG: STG_OFF + s * STG + ncols]
            dma("sp", f"stg{s}", st_ap, src_ap, [], [("stg", s)])
            consume(st_ap, s)

        flip = [0]

        def scale_cast(out_ap, in_ap, scal_ap, s, extra_reads, writes):
            if flip[0] % 2 == 0:
                P.add("dve", lambda e: e.tensor_single_scalar(out=out_ap, in_=in_ap, scalar=scal_ap, op=ALU.mult),
                      [("stg", s)] + extra_reads, writes)
            else:
                P.add("act", lambda e: e.activation(out=out_ap, in_=in_ap, func=AF.Copy, scale=scal_ap),
                      [("stg", s)] + extra_reads, writes)
            flip[0] += 1

        for c in range(8):
            for hp in range(2):
                c0 = hp * (WALL // 2)
                n = WALL // 2
                staged(w_in_d[c * 128:(c + 1) * 128, c0:c0 + n], n,
                       lambda st, s, c=c, c0=c0, n=n: scale_cast(Wall[:, c, c0:c0 + n], st, g1[:, c:c + 1], s,
                                                                 [("g1",)], [("Wall", c, c0)]))
        for hh in range(2):
            def consA(st, s, hh=hh):
                for q in range(4):
                    h = hh * 4 + q
                    P.add("act", lambda e, h=h, q=q: e.activation(out=EA[:, h, :], in_=st[:, q * 256:(q + 1) * 256], func=AF.Exp),
                          [("stg", s)], [("EA", h)])
            staged(biasA_d[:, hh * 1024:(hh + 1) * 1024], 1024, consA)
        for hh in range(2):
            def consB(st, s, hh=hh):
                for q in range(4):
                    h = hh * 4 + q
                    P.add("act", lambda e, h=h, q=q: e.activation(out=EB[:, h, :], in_=st[:, q * 640:(q + 1) * 640], func=AF.Exp),
                          [("stg", s)], [("EB", h)])
            staged(biasB_d[:, hh * 2560:(hh + 1) * 2560], 2560, consB)
        for c in range(8):
            staged(w_gate_d[c * 128:(c + 1) * 128, :], D,
                   lambda st, s, c=c: scale_cast(Wg[:, c, :], st, g2[:, c:c + 1], s, [("g2",)], [("Wg", c)]))
        dma("sp", "gfin", gfin[:], gfin_d, [], [("gfin",)])
        _wk = [("Wall", c, c0) for c in range(8) for c0 in (0, WALL // 2)]
        dma("pool", "wout", Wout[:], w_out_d.rearrange("(c p) n -> p c n", p=128), _wk, [("Wout",)])
        dma("pool", "wp", Wp[:], w_pp_d.rearrange("(c p) n -> p c n", p=128), _wk, [("Wp",)])
        P.add("act", lambda e: e.activation(out=sk32, in_=sk32, func=AF.Exp), [("sk32",)], [("sk32",), ("tgs", 0)])
        P.add("dve", lambda e: e.tensor_single_scalar(out=sk32, in_=sk32, scalar=2.0, op=ALU.mult), [("sk32",)], [("sk32",), ("tgs", 0)])
        P.add("dve", lambda e: e.tensor_copy(out=skhi, in_=sk32), [("sk32",)], [("skhi",), ("tnh", 0)])
        P.add("dve", lambda e: e.tensor_tensor(out=sklo, in0=sk32, in1=skhi, op=ALU.subtract), [("sk32",), ("skhi",), ("tgs", 0)], [("sklo",), ("tnh", 1)])
        dma_sems["skc"] = es.enter_context(nc.semaphore("dsem_skc"))
        dma("sp", "skc", skhl[0:1, :], skhi, [("skhi",), ("tnh", 0)], [("skhl", 0)])
        dma("sp", "skc", skhl[1:2, :], sklo, [("sklo",), ("tnh", 1)], [("skhl", 1)])

        wall_keys = [("Wall", c, c0) for c in range(8) for c0 in (0, WALL // 2)]
        wg_keys = [("Wg", c) for c in range(8)]

        def xslot_keys(slot):
            ks = [("x", slot)]
            lo, hi = slot * D, (slot + 1) * D
            for s in range(2):
                if lo < STG_OFF + (s + 1) * STG and hi > STG_OFF + s * STG:
                    ks.append(("stg", s))
            return ks

        def load_x_tile(J, t):
            g = J * NT + t
            slot = g % XR
            dma("sp", f"xl{slot}", xring[:, slot * D:(slot + 1) * D], x_d[g * 128:(g + 1) * 128, :],
                [], xslot_keys(slot))

        def load_x(J):
            for t in range(NT):
                load_x_tile(J, t)

        EPS_AP = sb("eps_t", [128, 1], F32)
        P.add("dve", lambda e: e.memset(EPS_AP[:], EPS), [], [("eps",)])
        u2T = sb("u2T", [128, 8, TB], BF16)

        def xtile(J, t):
            g = J * NT + t
            slot = g % XR
            return xring[:, slot * D:(slot + 1) * D], slot, g

        def sq_stats(J, t, col0):
            xs, slot, g = xtile(J, t)
            P.add("act", lambda e: e.activation(out=junk[:], in_=xs, func=AF.Square, accum_out=ss[:, col0 + t:col0 + t + 1], saturate=False),
                  [("x", slot)], [("junk",), ("ss", col0 + t)])

        def sqrt_recip(col0, ncol):
            groups = sorted({(cc // NT) * NT for cc in range(col0, col0 + ncol)})
            P.add("act", lambda e: e.activation(out=sd[:, col0:col0 + ncol], in_=ss[:, col0:col0 + ncol], func=AF.Sqrt,
                                                bias=EPS_AP[:, 0:1], scale=1.0 / D),
                  [("ss", cc) for cc in range(col0, col0 + ncol)] + [("eps",)], [("sd", gq) for gq in groups])
            P.add("dve", lambda e: e.reciprocal(out=rs[:, col0:col0 + ncol], in_=sd[:, col0:col0 + ncol]),
                  [("sd", gq) for gq in groups], [("rs", gq) for gq in groups])

        def scale_u(J, t, col0):
            xs, slot, g = xtile(J, t)
            us = t % 2
            P.add("act", lambda e: e.activation(out=ubuf[us][:], in_=xs, func=AF.Copy, scale=rs[:, col0 + t:col0 + t + 1]),
                  [("x", slot), ("rs", col0)], [("u", us)])

        def transposes_u(t, dstT, dkey):
            us = t % 2
            for cc in range(8):
                P.add("pe", lambda e, cc=cc: e.transpose(Tb[:, cc, :], ubuf[us][:, cc * 128:(cc + 1) * 128], ident[:]),
                      [("u", us), ("ident",)], [("psG", 1)])
            P.add("dve", lambda e: e.tensor_copy(out=dstT[:, :, t * 128:(t + 1) * 128], in_=Tb[:, 0:8, :]),
                  [("psG", 1)], [(dkey, t)])

        def transposes_p(J, t):
            par = J % 2
            for cc in range(2):
                P.add("pe", lambda e, cc=cc: e.transpose(Tb[:, cc, :], pbf[par][:, t, cc * 128:(cc + 1) * 128], ident[:]),
                      [("pbf", par), ("ident",)], [("psG", 1)])
            P.add("dve", lambda e: e.tensor_copy(out=pT[:, :, t * 128:(t + 1) * 128], in_=Tb[:, 0:2, :]),
                  [("psG", 1)], [("pT", t)])

        def head_io(J, do_x=True, do_p=True):
            par = J % 2
            if do_x and J + 1 < nblk:
                load_x(J + 1)
            if do_p:
                dma("pool", f"p{par}", pbf[par][:], p_d[J * TB:(J + 1) * TB, :].rearrange("(t q) f -> q t f", q=128),
                    [], [("pbf", par)])

        FM = [("Qa", 0, 0), ("Qa", 2, 2), ("Ka", 0, 4), ("Ga", 0, 6), ("Ga", 2, 8), ("Qb", 0, 10), ("Qb", 2, 12),
              ("Kb", 0, 14), ("Kb", 2, 16), ("Gb", 0, 18), ("Gb", 2, 20)]
        uT_keys = [("uT", t) for t in range(NT)]

        def proj_fm(J, n_ev):
            par = J % 2
            ringcol = (J * TB) % 1024
            rt0 = (J * NT) % 8
            kind, ci, f0 = FM[n_ev]
            gb = next_gb()
            for i in range(2):
                for cc in range(8):
                    P.add("pe", lambda e, i=i, cc=cc: e.matmul(
                        Gb[gb][:, i * TB:(i + 1) * TB], Wall[:, cc, (f0 + i) * 128:(f0 + i + 1) * 128], uT[:, cc, :],
                        start=(cc == 0), stop=(cc == 7)), wall_keys + uT_keys, [("psG", gb)])
            src3 = Gb[gb][:, :].rearrange("p (a b) -> p a b", a=2)
            if kind in ("Qa", "Qb", "Ka", "Kb"):
                if kind in ("Qa", "Qb"):
                    dst = (Qa if kind == "Qa" else Qb)[par][:, ci:ci + 2, :]
                    wk = [(kind, par, ci), (kind, par, ci + 1)]
                else:
                    dst = (KaT if kind == "Ka" else KbT)[:, ci:ci + 2, ringcol:ringcol + TB]
                    wk = [(kind, ci, (rt0 + t) % 8) for t in range(NT)]
                if n_ev % 2 == 0:
                    P.add("dve", lambda e: e.tensor_copy(out=dst, in_=src3), [("psG", gb)], wk)
                else:
                    P.add("act", lambda e: e.activation(out=dst, in_=src3, func=AF.Copy), [("psG", gb)], wk)
            else:
                gc = ci + (0 if kind == "Ga" else 4)
                ts_ = n_ev % 2
                tn3 = tnh[ts_][:, :].rearrange("p (a b) -> p a b", a=2)
                P.add("act", lambda e: e.activation(out=tn3, in_=src3, func=AF.Tanh, scale=0.5), [("psG", gb)], [("tnh", ts_)])
                dst = GT[par][:, gc:gc + 2, :]
                P.add("dve", lambda e: e.scalar_tensor_tensor(out=dst, in0=tn3, scalar=1.0, in1=src3, op0=ALU.add, op1=ALU.mult),
                      [("psG", gb), ("tnh", ts_)], [("GT", par, gc), ("GT", par, gc + 1)])

        def proj_v(J, t, which):
            rt = ((J * NT) % 8 + t) % 8
            gb = next_gb()
            if which == 0:
                for cc in range(8):
                    P.add("pe", lambda e, cc=cc: e.matmul(
                        Gb[gb][:, :], uT[:, cc, t * 128:(t + 1) * 128], Wall[:, cc, NFM * 128:NFM * 128 + 512],
                        start=(cc == 0), stop=(cc == 7)), wall_keys + [("uT", t)], [("psG", gb)])
                P.add("act", lambda e: e.activation(out=Vr[:, rt, 0:512], in_=Gb[gb][:, :], func=AF.Copy), [("psG", gb)], [("Vb", rt)])
            else:
                for cc in range(8):
                    P.add("pe", lambda e, cc=cc: e.matmul(
                        Gb[gb][:, 0:128], uT[:, cc, t * 128:(t + 1) * 128], Wall[:, cc, NFM * 128 + 512:NFM * 128 + 640],
                        start=(cc == 0), stop=(cc == 7)), wall_keys + [("uT", t)], [("psG", gb)])
                P.add("dve", lambda e: e.tensor_copy(out=Vr[:, rt, 512:640], in_=Gb[gb][:, 0:128]), [("psG", gb)], [("Va", rt)])

        def wout(J, t, half):
            par = J % 2
            xs, slot, g = xtile(J, t)
            gb = next_gb()
            for cc in range(8):
                ysrc = (Qa if cc < 4 else Qb)[par][:, cc % 4, t * 128:(t + 1) * 128]
                yk = [("Qa" if cc < 4 else "Qb", par, cc % 4)]
                P.add("pe", lambda e, cc=cc, ysrc=ysrc: e.matmul(
                    Gb[gb][:, :], ysrc, Wout[:, cc, half * 512:(half + 1) * 512], start=(cc == 0), stop=(cc == 7)),
                    yk + [("Wout",)], [("psG", gb)])
            P.add("dve", lambda e: e.tensor_tensor(out=xs[:, half * 512:(half + 1) * 512], in0=Gb[gb][:, :],
                                                  in1=xs[:, half * 512:(half + 1) * 512], op=ALU.add),
                  [("psG", gb), ("x", slot)], [("x", slot)])

        def gate(J, t, half):
            xs, slot, g = xtile(J, t)
            gbg = next_gb()
            for cc in range(8):
                P.add("pe", lambda e, cc=cc: e.matmul(
                    Gb[gbg][:, :], u2T[:, cc, t * 128:(t + 1) * 128], Wg[:, cc, half * 512:(half + 1) * 512],
                    start=(cc == 0), stop=(cc == 7)), [("u2T", t)] + wg_keys, [("psG", gbg)])
            ti = 0
            P.add("act", lambda e: e.activation(out=tgs[ti][:], in_=Gb[gbg][:, :], func=AF.Tanh, scale=0.5),
                  [("psG", gbg)], [("tgs", ti)])
            gbp = next_gb()
            for cc in range(2):
                P.add("pe", lambda e, cc=cc: e.matmul(
                    Gb[gbp][:, :], pT[:, cc, t * 128:(t + 1) * 128], Wp[:, cc, half * 512:(half + 1) * 512],
                    start=(cc == 0), stop=(cc == 1)), [("pT", t), ("Wp",)], [("psG", gbp)])
            P.add("dve", lambda e: e.scalar_tensor_tensor(out=tgs[ti][:], in0=tgs[ti][:], scalar=1.0, in1=Gb[gbp][:, :],
                                                         op0=ALU.add, op1=ALU.mult),
                  [("tgs", ti), ("psG", gbp)], [("tgs", ti)])
            P.add("dve", lambda e: e.scalar_tensor_tensor(out=xs[:, half * 512:(half + 1) * 512], in0=tgs[ti][:], scalar=0.5,
                                                         in1=xs[:, half * 512:(half + 1) * 512], op0=ALU.mult, op1=ALU.add),
                  [("tgs", ti), ("x", slot)], [("x", slot)])

        def final(J, t):
            xs, slot, g = xtile(J, t)
            P.add("dve", lambda e: e.scalar_tensor_tensor(out=xs, in0=xs, scalar=rs[:, 2 * NT + t:2 * NT + t + 1], in1=gfin[:],
                                                         op0=ALU.mult, op1=ALU.mult),
                  [("x", slot), ("rs", 2 * NT), ("gfin",)], [("x", slot)])
            dma("sp", f"xs{slot}", out_d[g * 128:(g + 1) * 128, :], xs, [("x", slot)], [])

        def head(J):
            head_io(J)
            for t in range(NT):
                sq_stats(J, t, 0)
            sqrt_recip(0, NT)
            yield
            for t in range(NT):
                scale_u(J, t, 0)
                transposes_u(t, uT, "uT")
                yield
            for n_ev in range(len(FM)):
                proj_fm(J, n_ev)
                yield
            for t in range(NT):
                for which in range(2):
                    proj_v(J, t, which)
                    yield

        def tail(J):
            for t in range(NT):
                for half in range(2):
                    wout(J, t, half)
                    yield
                sq_stats(J, t, NT)
            sqrt_recip(NT, NT)
            yield
            for t in range(NT):
                scale_u(J, t, NT)
                transposes_u(t, u2T, "u2T")
                yield
                transposes_p(J, t)
                yield
            for t in range(NT):
                for half in range(2):
                    gate(J, t, half)
                    yield
                sq_stats(J, t, 2 * NT)
            sqrt_recip(2 * NT, NT)
            yield
            for t in range(NT):
                final(J, t)
                yield

        def stream2(J):
            H = J + 2
            if J + 3 < nblk:
                load_x(J + 3)
            for t in range(NT):
                for half in range(2):
                    wout(J, t, half)
                    yield
                sq_stats(J, t, NT)
            for t in range(NT):
                sq_stats(H, t, 0)
            sqrt_recip(0, 2 * NT)
            yield
            for t in range(NT):
                scale_u(H, t, 0)
            for t in range(NT):
                transposes_p(J, t)
                yield "switch"
            head_io(H, do_x=False)
            for t in range(NT):
                transposes_u(t, uT, "uT")
                yield "switch"
            for t in range(NT):
                scale_u(J, t, NT)
            for n_ev in range(0, 3):
                proj_fm(H, n_ev)
                yield
            for t in range(NT):
                transposes_u(t, u2T, "u2T")
                yield "switch"
            for n_ev in range(3, 6):
                proj_fm(H, n_ev)
                yield
            rest_h = [lambda n_ev=n_ev: proj_fm(H, n_ev) for n_ev in range(6, len(FM))] + \
                     [lambda t=t, which=which: proj_v(H, t, which) for t in range(NT) for which in range(2)]
            gates = [(t, half) for t in range(NT) for half in range(2)]
            gi = 0
            for i, fn in enumerate(rest_h):
                fn()
                yield
                if i % 2 == 1 and gi < len(gates):
                    t, half = gates[gi]
                    gate(J, t, half)
                    if half == 1:
                        sq_stats(J, t, 2 * NT)
                    gi += 1
                    yield
            while gi < len(gates):
                t, half = gates[gi]
                gate(J, t, half)
                if half == 1:
                    sq_stats(J, t, 2 * NT)
                gi += 1
                yield
            sqrt_recip(2 * NT, NT)
            yield
            for t in range(NT):
                final(J, t)
                yield

        pair_ctr = [0]
        unit_ctr = [0]

        def attn(J):
            par = J % 2
            s = J // BLK_PER_SEQ
            jl = J % BLK_PER_SEQ
            units = []
            for mixer in ("A", "B"):
                for j in range(4):
                    if mixer == "A":
                        kps, span = range(2 * jl - 1, 2 * jl + 2), 256
                    else:
                        kps, span = range(2 * jl - 4, 2 * jl + 2), 640
                    ul = []
                    for kp in kps:
                        if kp < 0:
                            continue
                        q_lo = max(128 * kp, TB * jl)
                        q_hi = min(128 * kp + span, TB * jl + TB)
                        if q_hi <= q_lo:
                            continue
                        ul.append((kp, q_lo - 128 * kp, q_hi - q_lo, q_lo - TB * jl))
                    pid = pair_ctr[0]
                    pair_ctr[0] += 1
                    for i, u in enumerate(ul):
                        uid = unit_ctr[0]
                        unit_ctr[0] += 1
                        units.append(dict(mixer=mixer, j=j, kp=u[0], r0=u[1], w=u[2], c0=u[3],
                                          first=(i == 0), last=(i == len(ul) - 1), pid=pid, uid=uid))

            sus = [units[i:i + 2] for i in range(0, len(units), 2)]

            def qk(m):
                X, Y = Sb[2 * (m % 2)], Sb[2 * (m % 2) + 1]
                for i, u in enumerate(sus[m]):
                    mixer, j, w, c0 = u["mixer"], u["j"], u["w"], u["c0"]
                    rt = (s * 16 + u["kp"]) % 8
                    if mixer == "A":
                        kv = j // 2
                        Kt = KaT[:, kv, rt * 128:(rt + 1) * 128]
                        Qt = Qa[par][:, j, c0:c0 + w]
                        rk = [("Ka", 0, rt), ("Qa", par, j)]
                    else:
                        Kt = KbT[:, j, rt * 128:(rt + 1) * 128]
                        Qt = Qb[par][:, j, c0:c0 + w]
                        rk = [("Kb", (j // 2) * 2, rt), ("Qb", par, j)]
                    for hh, bank in ((0, X), (1, Y)):
                        P.add("pe", lambda e, hh=hh, bank=bank, Kt=Kt, Qt=Qt, i=i, w=w: e.matmul(
                            bank[:, i * 256:i * 256 + w], Kt[hh * 64:(hh + 1) * 64, :], Qt[hh * 64:(hh + 1) * 64, :],
                            start=True, stop=True), rk, [("psS", 2 * (m % 2) + hh)])

            def front(m):
                su = sus[m]
                pt = m % 3
                nu = len(su)
                wmax = max(u["w"] for u in su)
                pk = [("Pt", pt, i) for i in range(nu)]
                for hh in range(2):
                    bank3 = Sb[2 * (m % 2) + hh][:, :].rearrange("p (a b) -> p a b", a=2)
                    bk = [("psS", 2 * (m % 2) + hh)]
                    P.add("act", lambda e, hh=hh, bank3=bank3: e.activation(
                        out=Pt[pt][:, 0:nu, hh, 0:wmax], in_=bank3[:, 0:nu, 0:wmax], func=AF.Exp, scale=0.125), bk, pk)
                for i, u in enumerate(su):
                    mixer, j, w, r0 = u["mixer"], u["j"], u["w"], u["r0"]
                    Et = EA if mixer == "A" else EB
                    ek = [("EA" if mixer == "A" else "EB", 2 * j), ("EA" if mixer == "A" else "EB", 2 * j + 1)]
                    P.add(EMUL_ENG, lambda e, i=i, j=j, w=w, r0=r0, Et=Et: e.tensor_tensor(
                        out=Pt[pt][:, i, :, 0:w], in0=Pt[pt][:, i, :, 0:w], in1=Et[:, 2 * j:2 * j + 2, r0:r0 + w], op=ALU.mult),
                        [("Pt", pt, i)] + ek, [("Pt", pt, i)])

            def back(m):
                pt = m % 3
                for i, u in enumerate(sus[m]):
                    back_unit(u, Pt[pt], i, pt)

            def back_unit(u, Ptile, i, pt):
                mixer, j, w, c0 = u["mixer"], u["j"], u["w"], u["c0"]
                rt = (s * 16 + u["kp"]) % 8
                ab = u["pid"] % 2
                if mixer == "A":
                    kv = j // 2
                    V0 = Vr[:, rt, 512 + kv * 64:512 + kv * 64 + 64]
                    V1 = V0
                    vk = [("Va", rt)]
                else:
                    V0 = Vr[:, rt, (2 * j) * 64:(2 * j + 1) * 64]
                    V1 = Vr[:, rt, (2 * j + 1) * 64:(2 * j + 2) * 64]
                    vk = [("Vb", rt)]
                first = u["first"]
                acc = Ab[ab]
                P.add("pe", lambda e: e.matmul(acc[0:64, c0:c0 + w], V0, Ptile[:, i, 0, 0:w], start=first, stop=True,
                                               skip_group_check=True, tile_position=(0, 0)),
                      [("Pt", pt, i)] + vk, [("psA", ab)])
                P.add("pe", lambda e: e.matmul(acc[64:128, c0:c0 + w], V1, Ptile[:, i, 1, 0:w], start=first, stop=True,
                                               skip_group_check=True, tile_position=(0, 64)),
                      [("Pt", pt, i)] + vk, [("psA", ab)])
                P.add("pe", lambda e: e.matmul(acc[0:64, 256 + c0:256 + c0 + w], ones2[:, 0:64], Ptile[:, i, 0, 0:w],
                                               start=False, stop=True, skip_group_check=True, tile_position=(0, 0)),
                      [("Pt", pt, i), ("ones2",)], [("psA", ab)])
                P.add("pe", lambda e: e.matmul(acc[64:128, 256 + c0:256 + c0 + w], ones2[:, 0:64], Ptile[:, i, 1, 0:w],
                                               start=False, stop=True, skip_group_check=True, tile_position=(0, 64)),
                      [("Pt", pt, i), ("ones2",)], [("psA", ab)])
                if not u["last"]:
                    return
                if mixer == "A":
                    P.add("pe", lambda e: e.matmul(acc[:, 256:512], skhl[0:2, j * 128:(j + 1) * 128], onesrow[0:2, :],
                                                   start=False, stop=True, skip_group_check=True),
                          [("skhl", 0), ("skhl", 1), ("onesrow",)], [("psA", ab)])
                rn = 0
                gc = j if mixer == "A" else 4 + j
                ydst = (Qa if mixer == "A" else Qb)[par][:, j, :]
                yk = [("Qa" if mixer == "A" else "Qb", par, j)]
                P.add("dve", lambda e: e.reciprocal(out=Rn[rn][:], in_=acc[:, 256:512]), [("psA", ab)], [("Rn", rn)])
                P.add("dve", lambda e: e.tensor_tensor(out=Rn[rn][:], in0=Rn[rn][:], in1=GT[par][:, gc, :], op=ALU.mult),
                      [("Rn", rn), ("GT", par, gc)], [("Rn", rn)])
                P.add("dve", lambda e: e.tensor_tensor(out=ydst, in0=acc[:, 0:256], in1=Rn[rn][:], op=ALU.mult),
                      [("psA", ab), ("Rn", rn)], yk)

            M = len(sus)
            LAG = 2
            for k in range(0, M + LAG, 2):
                for m in (k, k + 1):
                    if m < M:
                        qk(m)
                for m in (k, k + 1):
                    if LAG <= m < M + LAG:
                        back(m - LAG)
                for m in (k, k + 1):
                    if m < M:
                        front(m)
                yield

        def drain(g):
            for _ in g:
                pass

        def interleave(ga, gb, nb=1):
            da = db = False
            while not (da and db):
                if not da:
                    try:
                        next(ga)
                    except StopIteration:
                        da = True
                for _ in range(nb):
                    if not db:
                        try:
                            if next(gb) == "switch" and not da:
                                break
                        except StopIteration:
                            db = True

        def chain(*gens):
            for g in gens:
                yield from g

        def empty():
            return
            yield

        drain(head(0))
        if nblk > 1:
            interleave(attn(0), head(1))
        else:
            drain(attn(0))
        for J in range(nblk):
            s2 = stream2(J) if J + 2 < nblk else tail(J)
            if J + 1 < nblk:
                interleave(attn(J + 1), s2, nb=2)
            else:
                drain(s2)
        if dbg:
            allk = list(P.last_writer.keys())
            dumps = dict(uT=uT, Qa0=Qa[0], Qb0=Qb[0], GT0=GT[0], KaT=KaT, KbT=KbT, Vr=Vr, EA=EA, EB=EB, xring=xring,
                         Wg=Wg, Wp=Wp, Wout=Wout, skhi=skhi, sklo=sklo, rs=rs, pT=pT)
            for nm, tl in dumps.items():
                shp = list(tl[:].shape)
                dd = nc.dram_tensor("dbg_" + nm, shp, tl[:].dtype, kind="ExternalOutput").ap()
                key = "dbg_" + nm
                dma_sems[key] = es.enter_context(nc.semaphore("dsem_" + key))
                dma("sp", key, dd, tl[:], allk, [])

        if RESCHED:
            _list_schedule(P)
        P.finalize()

        with nc.Block() as block:
            @block.sync
            def _(e):
                fw = [k for k in P.dma_total if k.startswith("xs") or k.startswith("dbg_")]
                P.emit_engine("sp", e, eng_sems, dma_sems, final_dma_waits=fw)

            @block.gpsimd
            def _(e):
                P.emit_engine("pool", e, eng_sems, dma_sems)

            @block.tensor
            def _(e):
                P.emit_engine("pe", e, eng_sems, dma_sems)

            @block.scalar
            def _(e):
                P.emit_engine("act", e, eng_sems, dma_sems)

            @block.vector
            def _(e):
                P.emit_engine("dve", e, eng_sems, dma_sems)
    return nc


def _host_layout(x, p, norm_g, w_in, sink_a, rel_bias_b, w_out, ple_norm_g, w_ple_proj, w_ple_gate, final_norm_g):
    f = np.float32
    w = np.asarray(w_in, f)[0]
    qa, ka, va, ga = w[:, 0:512], w[:, 512:640], w[:, 640:768], w[:, 768:1280]
    qb, kb, vb, gb = w[:, 1280:1792], w[:, 1792:2304], w[:, 2304:2816], w[:, 2816:3328]
    ka_dup = np.concatenate([ka[:, 0:64], ka[:, 0:64], ka[:, 64:128], ka[:, 64:128]], axis=1)
    w_in_r = np.ascontiguousarray(np.concatenate([qa, ka_dup, ga, qb, kb, gb, vb, va], axis=1))
    assert w_in_r.shape == (D, WALL)
    g1 = np.ascontiguousarray(np.asarray(norm_g, f)[0].reshape(8, 128).T)
    g2 = np.ascontiguousarray(np.asarray(ple_norm_g, f)[0].reshape(8, 128).T)
    gfin = np.ascontiguousarray(np.tile(np.asarray(final_norm_g, f)[None, :], (128, 1)))
    ki = np.arange(128)[:, None]
    rA = np.arange(256)[None, :]
    slopes = np.asarray(2.0 ** (-8.0 * np.arange(1, 9) / 8), dtype=f)
    distA = np.abs(rA - ki).astype(f)
    maskA = ((ki >= 64) & (rA < 64)) | ((ki < 64) & (rA >= 192))
    biasA = np.empty((128, 8, 256), f)
    for h in range(8):
        biasA[:, h, :] = np.where(maskA, f(MASK_NEG), -slopes[h] * distA)
    rB = np.arange(640)[None, :]
    idx = np.clip(rB - ki, -128, 128) + 128
    maskB = ((ki >= 64) & (rB < 64)) | ((ki < 64) & (rB >= 576))
    tab = np.asarray(rel_bias_b, f)[0]
    biasB = np.empty((128, 8, 640), f)
    for h in range(8):
        biasB[:, h, :] = np.where(maskB, f(MASK_NEG), tab[h][idx])
    sk = np.asarray(sink_a, f)[0]
    sinkrow = np.empty((1, 512), f)
    for j in range(4):
        sinkrow[0, j * 128:j * 128 + 64] = sk[2 * j]
        sinkrow[0, j * 128 + 64:(j + 1) * 128] = sk[2 * j + 1]
    shared = dict(
        w_in_r=w_in_r, g1=g1, w_out=np.ascontiguousarray(np.asarray(w_out, f)[0]),
        w_gate=np.ascontiguousarray(np.asarray(w_ple_gate, f)[0]), g2=g2,
        w_pp=np.ascontiguousarray(np.asarray(w_ple_proj, f)[0]), gfin=gfin,
        biasA=np.ascontiguousarray(biasA.reshape(128, 8 * 256)), biasB=np.ascontiguousarray(biasB.reshape(128, 8 * 640)),
        sinkrow=sinkrow, ident=np.eye(128, dtype=f),
    )
    xf = np.asarray(x, f).reshape(NCORES, TOK, D)
    pf = np.asarray(p, f)[0].reshape(NCORES, TOK, PLE)
    return [dict(shared, x=np.ascontiguousarray(xf[c]), p=np.ascontiguousarray(pf[c])) for c in range(NCORES)]


_NC_CACHE = {}


def kernel(x, p, norm_g, w_in, sink_a, rel_bias_b, w_out, ple_norm_g, w_ple_proj, w_ple_gate, final_norm_g):
    in_maps = _host_layout(x, p, norm_g, w_in, sink_a, rel_bias_b, w_out, ple_norm_g, w_ple_proj, w_ple_gate, final_norm_g)
    if "nc" not in _NC_CACHE:
        _NC_CACHE["nc"] = build_program()
    res = run_bass_kernel_spmd(_NC_CACHE["nc"], in_maps, core_ids=list(range(NCORES)))
    outs = [np.asarray(r["out"], np.float32).reshape(SEQ_PER_CORE, SEQ, D) for r in res.results]
    return np.concatenate(outs, axis=0)
```

```python
import numpy as np
from contextlib import ExitStack
import concourse.bass as bass
import concourse.mybir as mybir
from concourse.bass_utils import run_bass_kernel_spmd

F32 = mybir.dt.float32
BF16 = mybir.dt.bfloat16
ALU = mybir.AluOpType
AF = mybir.ActivationFunctionType

NCORES = 8
D = 1024
SEQ = 2048
SEQ_PER_CORE = 4
TOK = SEQ_PER_CORE * SEQ
TB = 256
NT = TB // 128
NBLK = TOK // TB
BLK_PER_SEQ = SEQ // TB
XR = 8
PLE = 256
EPS = 1e-6
NFM = 22
WALL = NFM * 128 + 640
STG = 3072
STG_OFF = 2 * D
NCB = 3
CBW = WALL // NCB
MASK_NEG = -200.0
RESCHED = True
EMUL_ENG = "pool"


class _Op:
    __slots__ = ("eng", "fn", "deps", "dma", "idx", "signal", "sigval", "dma_val", "need_eng", "need_dma", "force")


class Prog:
    def __init__(self):
        self.ops = []
        self.last_writer = {}
        self.readers = {}

    def add(self, eng, fn, reads=(), writes=(), dma=None, after=()):
        idx = len(self.ops)
        deps = {}
        for k in reads:
            w = self.last_writer.get(k)
            if w is not None:
                deps[w] = True
        for k in writes:
            w = self.last_writer.get(k)
            if w is not None:
                deps.setdefault(w, False)
            for r in self.readers.get(k, ()):
                deps.setdefault(r, False)
        op = _Op()
        op.eng, op.fn, op.deps, op.dma, op.idx = eng, fn, deps, dma, idx
        op.signal = False
        op.sigval = 0
        op.dma_val = 0
        op.force = tuple(after)
        self.ops.append(op)
        for k in reads:
            self.readers.setdefault(k, []).append(idx)
        for k in writes:
            self.last_writer[k] = idx
            self.readers[k] = []
        return idx

    def finalize(self):
        ops = self.ops
        dma_count = {}
        for op in ops:
            if op.dma is not None:
                dma_count[op.dma] = dma_count.get(op.dma, 0) + 1
                op.dma_val = 16 * dma_count[op.dma]
        for op in ops:
            need_eng, need_dma = {}, {}
            for d, raw in op.deps.items():
                p = ops[d]
                if p.dma is not None:
                    if p.dma_val > need_dma.get(p.dma, 0):
                        need_dma[p.dma] = p.dma_val
                else:
                    if p.eng == op.eng and (op.eng == "pe" or not raw):
                        continue
                    if d > need_eng.get(p.eng, -1):
                        need_eng[p.eng] = d
            for d in op.force:
                if d > need_eng.get(ops[d].eng, -1):
                    need_eng[ops[d].eng] = d
            op.need_eng, op.need_dma = need_eng, need_dma
            for d in need_eng.values():
                ops[d].signal = True
        cnt = {}
        for op in ops:
            if op.dma is None and op.signal:
                cnt[op.eng] = cnt.get(op.eng, 0) + 1
                op.sigval = cnt[op.eng]
        self.dma_total = {k: 16 * v for k, v in dma_count.items()}

    def emit_engine(self, eng_name, e, eng_sems, dma_sems, final_dma_waits=()):
        ops = self.ops
        waited = {}
        for op in ops:
            if op.eng != eng_name:
                continue
            for pe_name, d in op.need_eng.items():
                v = ops[d].sigval
                key = ("e", pe_name)
                if waited.get(key, 0) < v:
                    e.wait_ge(eng_sems[pe_name], v)
                    waited[key] = v
            for k, v in op.need_dma.items():
                key = ("d", k)
                if waited.get(key, 0) < v:
                    e.wait_ge(dma_sems[k], v)
                    waited[key] = v
            ins = op.fn(e)
            if op.dma is not None:
                ins.then_inc(dma_sems[op.dma], 16)
            elif op.signal:
                ins.then_inc(eng_sems[eng_name], 1)
        for k in final_dma_waits:
            e.wait_ge(dma_sems[k], self.dma_total[k])


def _free_elems(ap):
    n = 1
    for d in ap.shape[1:]:
        n *= d
    return n


class _Rec:
    __slots__ = ("kind", "F", "info")

    def then_inc(self, *a, **k):
        return self


class _CostEng:
    def __init__(self):
        self.last = None

    def _r(self, kind, F, **info):
        r = _Rec()
        r.kind, r.F, r.info = kind, F, info
        self.last = r
        return r

    def matmul(self, out, lhsT, rhs, start=None, stop=None, skip_group_check=False, tile_position=None, **kw):
        return self._r("mm", _free_elems(rhs), K=lhsT.shape[0], M=_free_elems(lhsT),
                       rowbase=lhsT.base_partition(), colbase=out.base_partition())

    def transpose(self, out, in_, ident):
        return self._r("tr", 128)

    def activation(self, out, in_, func, bias=0.0, scale=1.0, accum_out=None, **kw):
        return self._r("act", _free_elems(in_), func=str(func), accum=accum_out is not None)

    def tensor_tensor(self, out, in0, in1, op, **kw):
        ps = ("bank" in in0.name) or ("bank" in in1.name)
        return self._r("tt", _free_elems(out), bf16=("bfloat16" in str(out.dtype)) and not ps)

    def tensor_copy(self, out, in_, **kw):
        return self._r("cp", _free_elems(out))

    def tensor_single_scalar(self, out, in_, scalar, op, **kw):
        return self._r("ts", _free_elems(out))

    def scalar_tensor_tensor(self, out, in0, scalar, in1, op0, op1, **kw):
        return self._r("stt", _free_elems(out))

    def reciprocal(self, out, in_, **kw):
        return self._r("rcp", _free_elems(out))

    def memset(self, ap, c):
        return self._r("ms", _free_elems(ap))

    def dma_start(self, out, in_, **kw):
        n = 1
        for d in out.shape:
            n *= d
        return self._r("dma", n)


def _op_dur(eng, r, state):
    k = r.kind
    if eng == "pe":
        if k == "tr":
            cfg, d = (128, 128), 56.0
        else:
            K_, M_, N_ = r.info["K"], r.info["M"], r.F
            cfg = (64 if 2 < K_ <= 64 else (32 if K_ <= 2 else 128), 64 if M_ <= 64 else 128)
            d = max(N_, 64) * 0.455 + 8
            prev = state.get("prevmm")
            if (prev is not None and prev["cfg"] == cfg and cfg != (128, 128) and cfg[0] != 32
                    and (prev["rb"], prev["cb"]) != (r.info["rowbase"], r.info["colbase"]) and not prev.get("paired")):
                state["prevmm"] = dict(cfg=cfg, rb=r.info["rowbase"], cb=r.info["colbase"], paired=True)
                return 4.0
        sw = 100.0 if state.get("cfg") not in (None, cfg) else 0.0
        state["cfg"] = cfg
        state["prevmm"] = dict(cfg=cfg, rb=r.info.get("rowbase", 0), cb=r.info.get("colbase", 0)) if k == "mm" else None
        return d + sw
    if eng == "act":
        if k == "dma":
            return 60.0
        return 0.88 * r.F + 110 + (93 if r.info.get("accum") else 0)
    if eng == "dve":
        if k == "rcp":
            d = 5.0 * r.F + 20
        elif k == "tt" and r.info.get("bf16"):
            d = 60 + 0.6 * r.F
        elif k in ("tt", "stt", "cp"):
            d = 70 + 1.2 * r.F
        else:
            d = 60 + 0.6 * r.F
        return max(270.0, 1.1 * d)
    if eng == "pool":
        return 900.0 if k == "dma" else 2.45 * r.F
    return 60.0


def _list_schedule(P, fixed=("sp", "pool"), SEM=70.0):
    ops = P.ops
    n = len(ops)
    engs_c = {}
    recs = []
    for op in ops:
        ce = engs_c.setdefault(op.eng, _CostEng())
        op.fn(ce)
        recs.append(ce.last)
    st = {}
    dur1 = [_op_dur(op.eng, recs[i], st.setdefault(op.eng, {})) for i, op in enumerate(ops)]
    node_of = [0] * n
    nodes = []
    for i, op in enumerate(ops):
        if op.eng == "pe" and nodes and ops[nodes[-1][-1]].eng == "pe" and nodes[-1][-1] == i - 1:
            nodes[-1].append(i)
        else:
            nodes.append([i])
        node_of[i] = len(nodes) - 1
    m = len(nodes)
    neng = [ops[nd[0]].eng for nd in nodes]
    ndur = [sum(dur1[i] for i in nd) for nd in nodes]
    nlat = [(2000.0 + recs[nd[0]].F * 4 / 200.0) if ops[nd[0]].dma is not None else 0.0 for nd in nodes]
    preds = [set() for _ in range(m)]
    for i, op in enumerate(ops):
        a = node_of[i]
        for d in list(op.deps.keys()) + list(op.force):
            b = node_of[d]
            if b != a:
                preds[a].add(b)
    last = {}
    for a in range(m):
        if neng[a] in fixed:
            if neng[a] in last:
                preds[a].add(last[neng[a]])
            last[neng[a]] = a
    succs = [[] for _ in range(m)]
    for a in range(m):
        for p in preds[a]:
            succs[p].append(a)
    prio = [0.0] * m
    for a in range(m - 1, -1, -1):
        mx = 0.0
        for s in succs[a]:
            if prio[s] > mx:
                mx = prio[s]
        prio[a] = ndur[a] + nlat[a] + mx
    npred = [len(p) for p in preds]
    ready_t = [0.0] * m
    avail = {}
    for a in range(m):
        if npred[a] == 0:
            avail.setdefault(neng[a], []).append(a)
    free = {}
    engs = sorted(set(neng))
    order_nodes = []
    while len(order_nodes) < m:
        best = None
        for e in engs:
            h = avail.get(e)
            if not h:
                continue
            t = free.get(e, 0.0)
            rdy = [a for a in h if ready_t[a] <= t]
            if rdy:
                a = max(rdy, key=lambda a: (prio[a], -a))
                s = t
            else:
                a = min(h, key=lambda a: (ready_t[a], -prio[a]))
                s = ready_t[a]
            if best is None or s < best[0] or (s == best[0] and prio[a] > prio[best[2]]):
                best = (s, e, a)
        s, e, a = best
        avail[e].remove(a)
        free[e] = s + ndur[a]
        fin = s + ndur[a] + nlat[a]
        order_nodes.append(a)
        for sc in succs[a]:
            npred[sc] -= 1
            same = neng[sc] == e and nlat[a] == 0.0
            r = fin + (0.0 if same else SEM)
            if r > ready_t[sc]:
                ready_t[sc] = r
            if npred[sc] == 0:
                avail.setdefault(neng[sc], []).append(sc)
    order = [i for a in order_nodes for i in nodes[a]]
    newpos = {old: new for new, old in enumerate(order)}
    new_ops = [ops[i] for i in order]
    for new, op in enumerate(new_ops):
        op.idx = new
        op.deps = {newpos[d]: raw for d, raw in op.deps.items()}
        op.force = tuple(newpos[d] for d in op.force)
    P.ops = new_ops


def build_program(nblk=NBLK, stages=("head", "attn", "tail"), dbg=False):
    nc = bass.Bass("TRN2", target_bir_lowering=False)

    def din(name, shape):
        return nc.dram_tensor(name, shape, F32, kind="ExternalInput").ap()

    x_d = din("x", [TOK, D])
    p_d = din("p", [TOK, PLE])
    w_in_d = din("w_in_r", [D, WALL])
    g1_d = din("g1", [128, 8])
    w_out_d = din("w_out", [D, D])
    w_gate_d = din("w_gate", [D, D])
    g2_d = din("g2", [128, 8])
    w_pp_d = din("w_pp", [PLE, D])
    gfin_d = din("gfin", [128, D])
    biasA_d = din("biasA", [128, 8 * 256])
    biasB_d = din("biasB", [128, 8 * 640])
    sink_d = din("sinkrow", [1, 512])
    ident_d = din("ident", [128, 128])
    out_d = nc.dram_tensor("out", [TOK, D], F32, kind="ExternalOutput").ap()

    P = Prog()
    es = ExitStack()
    with es:
        def sb(name, shape, dt):
            return es.enter_context(nc.sbuf_tensor(name, shape, dt))

        def ps(name, shape, dt):
            return es.enter_context(nc.psum_tensor(name, shape, dt))

        Wall = sb("Wall", [128, 8, WALL], BF16)
        Wout = sb("Wout", [128, 8, D], BF16)
        Wg = sb("Wg", [128, 8, D], BF16)
        Wp = sb("Wp", [128, 2, D], BF16)
        gfin = sb("gfin_t", [128, D], F32)
        EA = sb("EA", [128, 8, 256], BF16)
        EB = sb("EB", [128, 8, 640], BF16)
        g1 = sb("g1_t", [128, 8], F32)
        g2 = sb("g2_t", [128, 8], F32)
        ident = sb("ident_t", [128, 128], BF16)
        ones2 = sb("ones2", [128, 64], BF16)
        onesrow = sb("onesrow", [2, 256], BF16)
        skhl = sb("skhl", [2, 512], BF16)
        xring = sb("xring", [128, XR * D], F32)
        junk = sb("junk", [128, D], mybir.dt.float8e4)
        ubuf = [sb(f"ubuf{i}", [128, D], BF16) for i in range(2)]
        uT = sb("uT", [128, 8, TB], BF16)
        Qa = [sb(f"Qa{i}", [128, 4, TB], BF16) for i in range(2)]
        Qb = [sb(f"Qb{i}", [128, 4, TB], BF16) for i in range(2)]
        GT = [sb(f"GT{i}", [128, 8, TB], BF16) for i in range(2)]
        KaT = sb("KaT", [128, 2, 1024], BF16)
        KbT = sb("KbT", [128, 4, 1024], BF16)
        Vr = sb("Vr", [128, 8, 640], BF16)
        pbf = [sb(f"pbf{i}", [128, NT, PLE], BF16) for i in range(2)]
        pT = sb("pT", [128, 2, TB], BF16)
        Pt = [sb(f"Pt{i}", [128, 2, 2, 256], BF16) for i in range(3)]
        Rn = [sb("Rn0", [128, 256], F32)] * 2
        tgs = [sb("tgs0", [128, 512], F32)] * 2
        tnh = [sb(f"tnh{i}", [128, 512], BF16) for i in range(2)]
        skhi, sklo = tnh[0][0:1, :], tnh[1][0:1, :]
        ss = sb("ss", [128, 3 * NT], F32)
        sd = sb("sd", [128, 3 * NT], F32)
        rs = sb("rs", [128, 3 * NT], F32)

        banks = [ps(f"bank{i}", [128, 512], F32) for i in range(8)]
        Sb = banks[0:4]
        Ab = banks[4:6]
        Gb = banks[6:8]
        Tb = banks[7][:, :].bitcast(BF16).rearrange("p (a b) -> p a b", a=8)
        NGB = 2

        eng_names = ["pe", "act", "dve", "pool"]
        eng_sems = {n: es.enter_context(nc.semaphore("sem_" + n)) for n in eng_names}
        dma_keys = ([f"xl{i}" for i in range(XR)] + [f"xs{i}" for i in range(XR)]
                    + ["p0", "p1", "stg0", "stg1", "wout", "wp", "gfin", "ident", "g1", "g2", "sink"])
        dma_sems = {k: es.enter_context(nc.semaphore("dsem_" + k)) for k in dma_keys}

        gb_rot = [0]

        def next_gb():
            b = gb_rot[0] % NGB
            gb_rot[0] += 1
            return b

        def dma(eng, key, out, in_, reads, writes):
            P.add(eng, lambda e, o=out, i=in_: e.dma_start(out=o, in_=i), reads, writes, dma=key)

        dma("pool", "ident", ident[:], ident_d, [], [("ident",)])
        dma("sp", "g1", g1[:], g1_d, [], [("g1",)])
        dma("sp", "g2", g2[:], g2_d, [], [("g2",)])
        sk32 = tgs[0][0:1, :]
        dma("sp", "sink", sk32, sink_d, [], [("sk32",), ("tgs", 0)])
        for g0 in range(NT):
            dma("sp", f"xl{g0}", xring[:, g0 * D:(g0 + 1) * D], x_d[g0 * 128:(g0 + 1) * 128, :], [], [("x", g0)])
        P.add("dve", lambda e: e.memset(ones2[:], 2.0), [], [("ones2",)])
        P.add("dve", lambda e: e.memset(onesrow[:], 1.0), [], [("onesrow",)])

        stg_i = [0]

        def staged(src_ap, ncols, consume):
            s = stg_i[0] % 2
            stg_i[0] += 1
            st_ap = xring[:, STG_OFF + s * STG: STG_OFF + s * STG + ncols]
            dma("sp", f"stg{s}", st_ap, src_ap, [], [("stg", s)])
            consume(st_ap, s)

        flip = [0]

        def scale_cast(out_ap, in_ap, scal_ap, s, extra_reads, writes):
            if flip[0] % 2 == 0:
                P.add("dve", lambda e: e.tensor_single_scalar(out=out_ap, in_=in_ap, scalar=scal_ap, op=ALU.mult),
                      [("stg", s)] + extra_reads, writes)
            else:
                P.add("act", lambda e: e.activation(out=out_ap, in_=in_ap, func=AF.Copy, scale=scal_ap),
                      [("stg", s)] + extra_reads, writes)
            flip[0] += 1

        for cb in range(NCB):
            for c in range(8):
                c0 = cb * CBW
                staged(w_in_d[c * 128:(c + 1) * 128, c0:c0 + CBW], CBW,
                       lambda st, s, c=c, c0=c0: scale_cast(Wall[:, c, c0:c0 + CBW], st, g1[:, c:c + 1], s,
                                                            [("g1",)], [("Wall", c, c0)]))
        for c in range(8):
            staged(w_gate_d[c * 128:(c + 1) * 128, :], D,
                   lambda st, s, c=c: scale_cast(Wg[:, c, :], st, g2[:, c:c + 1], s, [("g2",)], [("Wg", c)]))
        for hh in range(2):
            def consA(st, s, hh=hh):
                for q in range(4):
                    h = hh * 4 + q
                    P.add("act", lambda e, h=h, q=q: e.activation(out=EA[:, h, :], in_=st[:, q * 256:(q + 1) * 256], func=AF.Exp),
                          [("stg", s)], [("EA", h)])
            staged(biasA_d[:, hh * 1024:(hh + 1) * 1024], 1024, consA)
        for hh in range(2):
            def consB(st, s, hh=hh):
                for q in range(4):
                    h = hh * 4 + q
                    P.add("act", lambda e, h=h, q=q: e.activation(out=EB[:, h, :], in_=st[:, q * 640:(q + 1) * 640], func=AF.Exp),
                          [("stg", s)], [("EB", h)])
            staged(biasB_d[:, hh * 2560:(hh + 1) * 2560], 2560, consB)
        dma("sp", "gfin", gfin[:], gfin_d, [], [("gfin",)])
        dma("pool", "wout", Wout[:], w_out_d.rearrange("(c p) n -> p c n", p=128), [], [("Wout",)])
        dma("pool", "wp", Wp[:], w_pp_d.rearrange("(c p) n -> p c n", p=128), [], [("Wp",)])
        P.add("act", lambda e: e.activation(out=sk32, in_=sk32, func=AF.Exp), [("sk32",)], [("sk32",), ("tgs", 0)])
        P.add("dve", lambda e: e.tensor_single_scalar(out=sk32, in_=sk32, scalar=2.0, op=ALU.mult), [("sk32",)], [("sk32",), ("tgs", 0)])
        P.add("dve", lambda e: e.tensor_copy(out=skhi, in_=sk32), [("sk32",)], [("skhi",), ("tnh", 0)])
        P.add("dve", lambda e: e.tensor_tensor(out=sklo, in0=sk32, in1=skhi, op=ALU.subtract), [("sk32",), ("skhi",), ("tgs", 0)], [("sklo",), ("tnh", 1)])
        dma_sems["skc"] = es.enter_context(nc.semaphore("dsem_skc"))
        dma("sp", "skc", skhl[0:1, :], skhi, [("skhi",), ("tnh", 0)], [("skhl", 0)])
        dma("sp", "skc", skhl[1:2, :], sklo, [("sklo",), ("tnh", 1)], [("skhl", 1)])

        wall_keys = [("Wall", c, cb * CBW) for c in range(8) for cb in range(NCB)]

        def wall_keys_for(lo, hi):
            return [("Wall", c, cb * CBW) for c in range(8) for cb in range(NCB) if lo < (cb + 1) * CBW and hi > cb * CBW]
        wg_keys = [("Wg", c) for c in range(8)]

        def xslot_keys(slot):
            ks = [("x", slot)]
            lo, hi = slot * D, (slot + 1) * D
            for s in range(2):
                if lo < STG_OFF + (s + 1) * STG and hi > STG_OFF + s * STG:
                    ks.append(("stg", s))
            return ks

        def load_x_tile(J, t):
            g = J * NT + t
            slot = g % XR
            dma("sp", f"xl{slot}", xring[:, slot * D:(slot + 1) * D], x_d[g * 128:(g + 1) * 128, :],
                [], xslot_keys(slot))

        def load_x(J):
            for t in range(NT):
                load_x_tile(J, t)

        EPS_AP = sb("eps_t", [128, 1], F32)
        P.add("dve", lambda e: e.memset(EPS_AP[:], EPS), [], [("eps",)])
        u2T = sb("u2T", [128, 8, TB], BF16)

        def xtile(J, t):
            g = J * NT + t
            slot = g % XR
            return xring[:, slot * D:(slot + 1) * D], slot, g

        def sq_stats(J, t, col0):
            xs, slot, g = xtile(J, t)
            P.add("act", lambda e: e.activation(out=junk[:], in_=xs, func=AF.Square, accum_out=ss[:, col0 + t:col0 + t + 1], saturate=False),
                  [("x", slot)], [("junk",), ("ss", col0 + t)])

        def sqrt_recip(col0, ncol):
            groups = sorted({(cc // NT) * NT for cc in range(col0, col0 + ncol)})
            P.add("act", lambda e: e.activation(out=sd[:, col0:col0 + ncol], in_=ss[:, col0:col0 + ncol], func=AF.Sqrt,
                                                bias=EPS_AP[:, 0:1], scale=1.0 / D),
                  [("ss", cc) for cc in range(col0, col0 + ncol)] + [("eps",)], [("sd", gq) for gq in groups])
            P.add("dve", lambda e: e.reciprocal(out=rs[:, col0:col0 + ncol], in_=sd[:, col0:col0 + ncol]),
                  [("sd", gq) for gq in groups], [("rs", gq) for gq in groups])

        def scale_u(J, t, col0):
            xs, slot, g = xtile(J, t)
            us = t % 2
            P.add("act", lambda e: e.activation(out=ubuf[us][:], in_=xs, func=AF.Copy, scale=rs[:, col0 + t:col0 + t + 1]),
                  [("x", slot), ("rs", col0)], [("u", us)])

        def transposes_u(t, dstT, dkey):
            us = t % 2
            for cc in range(8):
                P.add("pe", lambda e, cc=cc: e.transpose(Tb[:, cc, :], ubuf[us][:, cc * 128:(cc + 1) * 128], ident[:]),
                      [("u", us), ("ident",)], [("psG", 1)])
            P.add("dve", lambda e: e.tensor_copy(out=dstT[:, :, t * 128:(t + 1) * 128], in_=Tb[:, 0:8, :]),
                  [("psG", 1)], [(dkey, t)])

        def transposes_p(J, t):
            par = J % 2
            for cc in range(2):
                P.add("pe", lambda e, cc=cc: e.transpose(Tb[:, cc, :], pbf[par][:, t, cc * 128:(cc + 1) * 128], ident[:]),
                      [("pbf", par), ("ident",)], [("psG", 1)])
            P.add("dve", lambda e: e.tensor_copy(out=pT[:, :, t * 128:(t + 1) * 128], in_=Tb[:, 0:2, :]),
                  [("psG", 1)], [("pT", t)])

        def head_io(J, do_x=True, do_p=True):
            par = J % 2
            if do_x and J + 1 < nblk:
                load_x(J + 1)
            if do_p:
                dma("pool", f"p{par}", pbf[par][:], p_d[J * TB:(J + 1) * TB, :].rearrange("(t q) f -> q t f", q=128),
                    [], [("pbf", par)])

        FM = [("Qa", 0, 0), ("Qa", 2, 2), ("Ka", 0, 4), ("Ga", 0, 6), ("Ga", 2, 8), ("Qb", 0, 10), ("Qb", 2, 12),
              ("Kb", 0, 14), ("Kb", 2, 16), ("Gb", 0, 18), ("Gb", 2, 20)]
        uT_keys = [("uT", t) for t in range(NT)]

        def proj_fm(J, n_ev):
            par = J % 2
            ringcol = (J * TB) % 1024
            rt0 = (J * NT) % 8
            kind, ci, f0 = FM[n_ev]
            gb = next_gb()
            for i in range(2):
                for cc in range(8):
                    P.add("pe", lambda e, i=i, cc=cc: e.matmul(
                        Gb[gb][:, i * TB:(i + 1) * TB], Wall[:, cc, (f0 + i) * 128:(f0 + i + 1) * 128], uT[:, cc, :],
                        start=(cc == 0), stop=(cc == 7)), wall_keys_for(f0 * 128, (f0 + 2) * 128) + uT_keys, [("psG", gb)])
            src3 = Gb[gb][:, :].rearrange("p (a b) -> p a b", a=2)
            if kind in ("Qa", "Qb", "Ka", "Kb"):
                if kind in ("Qa", "Qb"):
                    dst = (Qa if kind == "Qa" else Qb)[par][:, ci:ci + 2, :]
                    wk = [(kind, par, ci), (kind, par, ci + 1)]
                else:
                    dst = (KaT if kind == "Ka" else KbT)[:, ci:ci + 2, ringcol:ringcol + TB]
                    wk = [(kind, ci, (rt0 + t) % 8) for t in range(NT)]
                if n_ev % 2 == 0:
                    P.add("dve", lambda e: e.tensor_copy(out=dst, in_=src3), [("psG", gb)], wk)
                else:
                    P.add("act", lambda e: e.activation(out=dst, in_=src3, func=AF.Copy), [("psG", gb)], wk)
            else:
                gc = ci + (0 if kind == "Ga" else 4)
                ts_ = n_ev % 2
                tn3 = tnh[ts_][:, :].rearrange("p (a b) -> p a b", a=2)
                P.add("act", lambda e: e.activation(out=tn3, in_=src3, func=AF.Tanh, scale=0.5), [("psG", gb)], [("tnh", ts_)])
                dst = GT[par][:, gc:gc + 2, :]
                P.add("dve", lambda e: e.scalar_tensor_tensor(out=dst, in0=tn3, scalar=1.0, in1=src3, op0=ALU.add, op1=ALU.mult),
                      [("psG", gb), ("tnh", ts_)], [("GT", par, gc), ("GT", par, gc + 1)])

        def proj_v(J, t, which):
            rt = ((J * NT) % 8 + t) % 8
            gb = next_gb()
            if which == 0:
                for cc in range(8):
                    P.add("pe", lambda e, cc=cc: e.matmul(
                        Gb[gb][:, :], uT[:, cc, t * 128:(t + 1) * 128], Wall[:, cc, NFM * 128:NFM * 128 + 512],
                        start=(cc == 0), stop=(cc == 7)), wall_keys_for(NFM * 128, WALL) + [("uT", t)], [("psG", gb)])
                P.add("act", lambda e: e.activation(out=Vr[:, rt, 0:512], in_=Gb[gb][:, :], func=AF.Copy), [("psG", gb)], [("Vb", rt)])
            else:
                for cc in range(8):
                    P.add("pe", lambda e, cc=cc: e.matmul(
                        Gb[gb][:, 0:128], uT[:, cc, t * 128:(t + 1) * 128], Wall[:, cc, NFM * 128 + 512:NFM * 128 + 640],
                        start=(cc == 0), stop=(cc == 7)), wall_keys_for(NFM * 128, WALL) + [("uT", t)], [("psG", gb)])
                P.add("dve", lambda e: e.tensor_copy(out=Vr[:, rt, 512:640], in_=Gb[gb][:, 0:128]), [("psG", gb)], [("Va", rt)])

        def wout(J, t, half):
            par = J % 2
            xs, slot, g = xtile(J, t)
            gb = next_gb()
            for cc in range(8):
                ysrc = (Qa if cc < 4 else Qb)[par][:, cc % 4, t * 128:(t + 1) * 128]
                yk = [("Qa" if cc < 4 else "Qb", par, cc % 4)]
                P.add("pe", lambda e, cc=cc, ysrc=ysrc: e.matmul(
                    Gb[gb][:, :], ysrc, Wout[:, cc, half * 512:(half + 1) * 512], start=(cc == 0), stop=(cc == 7)),
                    yk + [("Wout",)], [("psG", gb)])
            P.add("dve", lambda e: e.tensor_tensor(out=xs[:, half * 512:(half + 1) * 512], in0=Gb[gb][:, :],
                                                  in1=xs[:, half * 512:(half + 1) * 512], op=ALU.add),
                  [("psG", gb), ("x", slot)], [("x", slot)])

        def gate(J, t, half):
            xs, slot, g = xtile(J, t)
            gbg = next_gb()
            for cc in range(8):
                P.add("pe", lambda e, cc=cc: e.matmul(
                    Gb[gbg][:, :], u2T[:, cc, t * 128:(t + 1) * 128], Wg[:, cc, half * 512:(half + 1) * 512],
                    start=(cc == 0), stop=(cc == 7)), [("u2T", t)] + wg_keys, [("psG", gbg)])
            ti = 0
            P.add("act", lambda e: e.activation(out=tgs[ti][:], in_=Gb[gbg][:, :], func=AF.Tanh, scale=0.5),
                  [("psG", gbg)], [("tgs", ti)])
            gbp = next_gb()
            for cc in range(2):
                P.add("pe", lambda e, cc=cc: e.matmul(
                    Gb[gbp][:, :], pT[:, cc, t * 128:(t + 1) * 128], Wp[:, cc, half * 512:(half + 1) * 512],
                    start=(cc == 0), stop=(cc == 1)), [("pT", t), ("Wp",)], [("psG", gbp)])
            P.add("dve", lambda e: e.scalar_tensor_tensor(out=tgs[ti][:], in0=tgs[ti][:], scalar=1.0, in1=Gb[gbp][:, :],
                                                         op0=ALU.add, op1=ALU.mult),
                  [("tgs", ti), ("psG", gbp)], [("tgs", ti)])
            P.add("dve", lambda e: e.scalar_tensor_tensor(out=xs[:, half * 512:(half + 1) * 512], in0=tgs[ti][:], scalar=0.5,
                                                         in1=xs[:, half * 512:(half + 1) * 512], op0=ALU.mult, op1=ALU.add),
                  [("tgs", ti), ("x", slot)], [("x", slot)])

        def final(J, t):
            xs, slot, g = xtile(J, t)
            P.add("dve", lambda e: e.scalar_tensor_tensor(out=xs, in0=xs, scalar=rs[:, 2 * NT + t:2 * NT + t + 1], in1=gfin[:],
                                                         op0=ALU.mult, op1=ALU.mult),
                  [("x", slot), ("rs", 2 * NT), ("gfin",)], [("x", slot)])
            dma("sp", f"xs{slot}", out_d[g * 128:(g + 1) * 128, :], xs, [("x", slot)], [])

        def head(J):
            head_io(J)
            for t in range(NT):
                sq_stats(J, t, 0)
            sqrt_recip(0, NT)
            yield
            for t in range(NT):
                scale_u(J, t, 0)
                transposes_u(t, uT, "uT")
                yield
            for n_ev in range(len(FM)):
                proj_fm(J, n_ev)
                yield
            for t in range(NT):
                for which in range(2):
                    proj_v(J, t, which)
                    yield

        def tail(J):
            for t in range(NT):
                for half in range(2):
                    wout(J, t, half)
                    yield
                sq_stats(J, t, NT)
            sqrt_recip(NT, NT)
            yield
            for t in range(NT):
                scale_u(J, t, NT)
                transposes_u(t, u2T, "u2T")
                yield
                transposes_p(J, t)
                yield
            for t in range(NT):
                for half in range(2):
                    gate(J, t, half)
                    yield
                sq_stats(J, t, 2 * NT)
            sqrt_recip(2 * NT, NT)
            yield
            for t in range(NT):
                final(J, t)
                yield

        def stream2(J):
            H = J + 2
            if J + 3 < nblk:
                load_x(J + 3)
            for t in range(NT):
                for half in range(2):
                    wout(J, t, half)
                    yield
                sq_stats(J, t, NT)
            for t in range(NT):
                sq_stats(H, t, 0)
            sqrt_recip(0, 2 * NT)
            yield
            for t in range(NT):
                scale_u(H, t, 0)
            for t in range(NT):
                transposes_p(J, t)
                yield "switch"
            head_io(H, do_x=False)
            for t in range(NT):
                transposes_u(t, uT, "uT")
                yield "switch"
            for t in range(NT):
                scale_u(J, t, NT)
            for n_ev in range(0, 3):
                proj_fm(H, n_ev)
                yield
            for t in range(NT):
                transposes_u(t, u2T, "u2T")
                yield "switch"
            for n_ev in range(3, 6):
                proj_fm(H, n_ev)
                yield
            rest_h = [lambda n_ev=n_ev: proj_fm(H, n_ev) for n_ev in range(6, len(FM))] + \
                     [lambda t=t, which=which: proj_v(H, t, which) for t in range(NT) for which in range(2)]
            gates = [(t, half) for t in range(NT) for half in range(2)]
            gi = 0
            for i, fn in enumerate(rest_h):
                fn()
                yield
                if i % 2 == 1 and gi < len(gates):
                    t, half = gates[gi]
                    gate(J, t, half)
                    if half == 1:
                        sq_stats(J, t, 2 * NT)
                    gi += 1
                    yield
            while gi < len(gates):
                t, half = gates[gi]
                gate(J, t, half)
                if half == 1:
                    sq_stats(J, t, 2 * NT)
                gi += 1
                yield
            sqrt_recip(2 * NT, NT)
            yield
            for t in range(NT):
                final(J, t)
                yield

        pair_ctr = [0]
        unit_ctr = [0]

        def attn(J):
            par = J % 2
            s = J // BLK_PER_SEQ
            jl = J % BLK_PER_SEQ
            units = []
            for mixer in ("A", "B"):
                for j in range(4):
                    if mixer == "A":
                        kps, span = range(2 * jl - 1, 2 * jl + 2), 256
                    else:
                        kps, span = range(2 * jl - 4, 2 * jl + 2), 640
                    ul = []
                    for kp in kps:
                        if kp < 0:
                            continue
                        q_lo = max(128 * kp, TB * jl)
                        q_hi = min(128 * kp + span, TB * jl + TB)
                        if q_hi <= q_lo:
                            continue
                        ul.append((kp, q_lo - 128 * kp, q_hi - q_lo, q_lo - TB * jl))
                    pid = pair_ctr[0]
                    pair_ctr[0] += 1
                    for i, u in enumerate(ul):
                        uid = unit_ctr[0]
                        unit_ctr[0] += 1
                        units.append(dict(mixer=mixer, j=j, kp=u[0], r0=u[1], w=u[2], c0=u[3],
                                          first=(i == 0), last=(i == len(ul) - 1), pid=pid, uid=uid))

            sus = [units[i:i + 2] for i in range(0, len(units), 2)]

            def qk(m):
                X, Y = Sb[2 * (m % 2)], Sb[2 * (m % 2) + 1]
                for i, u in enumerate(sus[m]):
                    mixer, j, w, c0 = u["mixer"], u["j"], u["w"], u["c0"]
                    rt = (s * 16 + u["kp"]) % 8
                    if mixer == "A":
                        kv = j // 2
                        Kt = KaT[:, kv, rt * 128:(rt + 1) * 128]
                        Qt = Qa[par][:, j, c0:c0 + w]
                        rk = [("Ka", 0, rt), ("Qa", par, j)]
                    else:
                        Kt = KbT[:, j, rt * 128:(rt + 1) * 128]
                        Qt = Qb[par][:, j, c0:c0 + w]
                        rk = [("Kb", (j // 2) * 2, rt), ("Qb", par, j)]
                    for hh, bank in ((0, X), (1, Y)):
                        P.add("pe", lambda e, hh=hh, bank=bank, Kt=Kt, Qt=Qt, i=i, w=w: e.matmul(
                            bank[:, i * 256:i * 256 + w], Kt[hh * 64:(hh + 1) * 64, :], Qt[hh * 64:(hh + 1) * 64, :],
                            start=True, stop=True), rk, [("psS", 2 * (m % 2) + hh)])

            def front(m):
                su = sus[m]
                pt = m % 3
                nu = len(su)
                wmax = max(u["w"] for u in su)
                pk = [("Pt", pt, i) for i in range(nu)]
                for hh in range(2):
                    bank3 = Sb[2 * (m % 2) + hh][:, :].rearrange("p (a b) -> p a b", a=2)
                    bk = [("psS", 2 * (m % 2) + hh)]
                    P.add("act", lambda e, hh=hh, bank3=bank3: e.activation(
                        out=Pt[pt][:, 0:nu, hh, 0:wmax], in_=bank3[:, 0:nu, 0:wmax], func=AF.Exp, scale=0.125), bk, pk)
                for i, u in enumerate(su):
                    mixer, j, w, r0 = u["mixer"], u["j"], u["w"], u["r0"]
                    Et = EA if mixer == "A" else EB
                    ek = [("EA" if mixer == "A" else "EB", 2 * j), ("EA" if mixer == "A" else "EB", 2 * j + 1)]
                    P.add(EMUL_ENG, lambda e, i=i, j=j, w=w, r0=r0, Et=Et: e.tensor_tensor(
                        out=Pt[pt][:, i, :, 0:w], in0=Pt[pt][:, i, :, 0:w], in1=Et[:, 2 * j:2 * j + 2, r0:r0 + w], op=ALU.mult),
                        [("Pt", pt, i)] + ek, [("Pt", pt, i)])

            def back(m):
                pt = m % 3
                for i, u in enumerate(sus[m]):
                    back_unit(u, Pt[pt], i, pt)

            def back_unit(u, Ptile, i, pt):
                mixer, j, w, c0 = u["mixer"], u["j"], u["w"], u["c0"]
                rt = (s * 16 + u["kp"]) % 8
                ab = u["pid"] % 2
                if mixer == "A":
                    kv = j // 2
                    V0 = Vr[:, rt, 512 + kv * 64:512 + kv * 64 + 64]
                    V1 = V0
                    vk = [("Va", rt)]
                else:
                    V0 = Vr[:, rt, (2 * j) * 64:(2 * j + 1) * 64]
                    V1 = Vr[:, rt, (2 * j + 1) * 64:(2 * j + 2) * 64]
                    vk = [("Vb", rt)]
                first = u["first"]
                acc = Ab[ab]
                P.add("pe", lambda e: e.matmul(acc[0:64, c0:c0 + w], V0, Ptile[:, i, 0, 0:w], start=first, stop=True,
                                               skip_group_check=True, tile_position=(0, 0)),
                      [("Pt", pt, i)] + vk, [("psA", ab)])
                P.add("pe", lambda e: e.matmul(acc[64:128, c0:c0 + w], V1, Ptile[:, i, 1, 0:w], start=first, stop=True,
                                               skip_group_check=True, tile_position=(0, 64)),
                      [("Pt", pt, i)] + vk, [("psA", ab)])
                P.add("pe", lambda e: e.matmul(acc[0:64, 256 + c0:256 + c0 + w], ones2[:, 0:64], Ptile[:, i, 0, 0:w],
                                               start=False, stop=True, skip_group_check=True, tile_position=(0, 0)),
                      [("Pt", pt, i), ("ones2",)], [("psA", ab)])
                P.add("pe", lambda e: e.matmul(acc[64:128, 256 + c0:256 + c0 + w], ones2[:, 0:64], Ptile[:, i, 1, 0:w],
                                               start=False, stop=True, skip_group_check=True, tile_position=(0, 64)),
                      [("Pt", pt, i), ("ones2",)], [("psA", ab)])
                if not u["last"]:
                    return
                if mixer == "A":
                    P.add("pe", lambda e: e.matmul(acc[:, 256:512], skhl[0:2, j * 128:(j + 1) * 128], onesrow[0:2, :],
                                                   start=False, stop=True, skip_group_check=True),
                          [("skhl", 0), ("skhl", 1), ("onesrow",)], [("psA", ab)])
                rn = 0
                gc = j if mixer == "A" else 4 + j
                ydst = (Qa if mixer == "A" else Qb)[par][:, j, :]
                yk = [("Qa" if mixer == "A" else "Qb", par, j)]
                P.add("dve", lambda e: e.reciprocal(out=Rn[rn][:], in_=acc[:, 256:512]), [("psA", ab)], [("Rn", rn)])
                P.add("dve", lambda e: e.tensor_tensor(out=Rn[rn][:], in0=Rn[rn][:], in1=GT[par][:, gc, :], op=ALU.mult),
                      [("Rn", rn), ("GT", par, gc)], [("Rn", rn)])
                P.add("dve", lambda e: e.tensor_tensor(out=ydst, in0=acc[:, 0:256], in1=Rn[rn][:], op=ALU.mult),
                      [("psA", ab), ("Rn", rn)], yk)

            M = len(sus)
            LAG = 2
            for k in range(0, M + LAG, 2):
                for m in (k, k + 1):
                    if m < M:
                        qk(m)
                for m in (k, k + 1):
                    if LAG <= m < M + LAG:
                        back(m - LAG)
                for m in (k, k + 1):
                    if m < M:
                        front(m)
                yield

        def drain(g):
            for _ in g:
                pass

        def interleave(ga, gb, nb=1):
            da = db = False
            while not (da and db):
                if not da:
                    try:
                        next(ga)
                    except StopIteration:
                        da = True
                for _ in range(nb):
                    if not db:
                        try:
                            if next(gb) == "switch" and not da:
                                break
                        except StopIteration:
                            db = True

        def chain(*gens):
            for g in gens:
                yield from g

        def empty():
            return
            yield

        drain(head(0))
        if nblk > 1:
            interleave(attn(0), head(1))
        else:
            drain(attn(0))
        for J in range(nblk):
            s2 = stream2(J) if J + 2 < nblk else tail(J)
            if J + 1 < nblk:
                interleave(attn(J + 1), s2, nb=2)
            else:
                drain(s2)
        if dbg:
            allk = list(P.last_writer.keys())
            dumps = dict(uT=uT, Qa0=Qa[0], Qb0=Qb[0], GT0=GT[0], KaT=KaT, KbT=KbT, Vr=Vr, EA=EA, EB=EB, xring=xring,
                         Wg=Wg, Wp=Wp, Wout=Wout, skhi=skhi, sklo=sklo, rs=rs, pT=pT)
            for nm, tl in dumps.items():
                shp = list(tl[:].shape)
                dd = nc.dram_tensor("dbg_" + nm, shp, tl[:].dtype, kind="ExternalOutput").ap()
                key = "dbg_" + nm
                dma_sems[key] = es.enter_context(nc.semaphore("dsem_" + key))
                dma("sp", key, dd, tl[:], allk, [])

        if RESCHED:
            _list_schedule(P)
        P.finalize()

        with nc.Block() as block:
            @block.sync
            def _(e):
                fw = [k for k in P.dma_total if k.startswith("xs") or k.startswith("dbg_")]
                P.emit_engine("sp", e, eng_sems, dma_sems, final_dma_waits=fw)

            @block.gpsimd
            def _(e):
                P.emit_engine("pool", e, eng_sems, dma_sems)

            @block.tensor
            def _(e):
                P.emit_engine("pe", e, eng_sems, dma_sems)

            @block.scalar
            def _(e):
                P.emit_engine("act", e, eng_sems, dma_sems)

            @block.vector
            def _(e):
                P.emit_engine("dve", e, eng_sems, dma_sems)
    return nc


def _host_layout(x, p, norm_g, w_in, sink_a, rel_bias_b, w_out, ple_norm_g, w_ple_proj, w_ple_gate, final_norm_g):
    f = np.float32
    w = np.asarray(w_in, f)[0]
    qa, ka, va, ga = w[:, 0:512], w[:, 512:640], w[:, 640:768], w[:, 768:1280]
    qb, kb, vb, gb = w[:, 1280:1792], w[:, 1792:2304], w[:, 2304:2816], w[:, 2816:3328]
    ka_dup = np.concatenate([ka[:, 0:64], ka[:, 0:64], ka[:, 64:128], ka[:, 64:128]], axis=1)
    w_in_r = np.ascontiguousarray(np.concatenate([qa, ka_dup, ga, qb, kb, gb, vb, va], axis=1))
    assert w_in_r.shape == (D, WALL)
    g1 = np.ascontiguousarray(np.asarray(norm_g, f)[0].reshape(8, 128).T)
    g2 = np.ascontiguousarray(np.asarray(ple_norm_g, f)[0].reshape(8, 128).T)
    gfin = np.ascontiguousarray(np.tile(np.asarray(final_norm_g, f)[None, :], (128, 1)))
    ki = np.arange(128)[:, None]
    rA = np.arange(256)[None, :]
    slopes = np.asarray(2.0 ** (-8.0 * np.arange(1, 9) / 8), dtype=f)
    distA = np.abs(rA - ki).astype(f)
    maskA = ((ki >= 64) & (rA < 64)) | ((ki < 64) & (rA >= 192))
    biasA = np.empty((128, 8, 256), f)
    for h in range(8):
        biasA[:, h, :] = np.where(maskA, f(MASK_NEG), -slopes[h] * distA)
    rB = np.arange(640)[None, :]
    idx = np.clip(rB - ki, -128, 128) + 128
    maskB = ((ki >= 64) & (rB < 64)) | ((ki < 64) & (rB >= 576))
    tab = np.asarray(rel_bias_b, f)[0]
    biasB = np.empty((128, 8, 640), f)
    for h in range(8):
        biasB[:, h, :] = np.where(maskB, f(MASK_NEG), tab[h][idx])
    sk = np.asarray(sink_a, f)[0]
    sinkrow = np.empty((1, 512), f)
    for j in range(4):
        sinkrow[0, j * 128:j * 128 + 64] = sk[2 * j]
        sinkrow[0, j * 128 + 64:(j + 1) * 128] = sk[2 * j + 1]
    shared = dict(
        w_in_r=w_in_r, g1=g1, w_out=np.ascontiguousarray(np.asarray(w_out, f)[0]),
        w_gate=np.ascontiguousarray(np.asarray(w_ple_gate, f)[0]), g2=g2,
        w_pp=np.ascontiguousarray(np.asarray(w_ple_proj, f)[0]), gfin=gfin,
        biasA=np.ascontiguousarray(biasA.reshape(128, 8 * 256)), biasB=np.ascontiguousarray(biasB.reshape(128, 8 * 640)),
        sinkrow=sinkrow, ident=np.eye(128, dtype=f),
    )
    xf = np.asarray(x, f).reshape(NCORES, TOK, D)
    pf = np.asarray(p, f)[0].reshape(NCORES, TOK, PLE)
    return [dict(shared, x=np.ascontiguousarray(xf[c]), p=np.ascontiguousarray(pf[c])) for c in range(NCORES)]


_NC_CACHE = {}


def kernel(x, p, norm_g, w_in, sink_a, rel_bias_b, w_out, ple_norm_g, w_ple_proj, w_ple_gate, final_norm_g):
    in_maps = _host_layout(x, p, norm_g, w_in, sink_a, rel_bias_b, w_out, ple_norm_g, w_ple_proj, w_ple_gate, final_norm_g)
    if "nc" not in _NC_CACHE:
        _NC_CACHE["nc"] = build_program()
    res = run_bass_kernel_spmd(_NC_CACHE["nc"], in_maps, core_ids=list(range(NCORES)))
    outs = [np.asarray(r["out"], np.float32).reshape(SEQ_PER_CORE, SEQ, D) for r in res.results]
    return np.concatenate(outs, axis=0)
```

```python
import numpy as np
from contextlib import ExitStack
import concourse.bass as bass
import concourse.mybir as mybir
from concourse.bass_utils import run_bass_kernel_spmd

F32 = mybir.dt.float32
BF16 = mybir.dt.bfloat16
ALU = mybir.AluOpType
AF = mybir.ActivationFunctionType

NCORES = 8
D = 1024
SEQ = 2048
SEQ_PER_CORE = 4
TOK = SEQ_PER_CORE * SEQ
TB = 256
NT = TB // 128
NBLK = TOK // TB
BLK_PER_SEQ = SEQ // TB
XR = 8
PLE = 256
EPS = 1e-6
NFM = 22
WALL = NFM * 128 + 640
STG = 3072
MASK_NEG = -200.0
RESCHED = True
STRICT_SAME_ENGINE = True
EMUL_ENG = "pool"


class _Op:
    __slots__ = ("eng", "fn", "deps", "dma", "idx", "signal", "sigval", "dma_val", "need_eng", "need_dma", "force")


class Prog:
    def __init__(self):
        self.ops = []
        self.last_writer = {}
        self.readers = {}

    def add(self, eng, fn, reads=(), writes=(), dma=None, after=()):
        idx = len(self.ops)
        deps = {}
        for k in reads:
            w = self.last_writer.get(k)
            if w is not None:
                deps[w] = True
        for k in writes:
            w = self.last_writer.get(k)
            if w is not None:
                deps.setdefault(w, False)
            for r in self.readers.get(k, ()):
                deps.setdefault(r, False)
        op = _Op()
        op.eng, op.fn, op.deps, op.dma, op.idx = eng, fn, deps, dma, idx
        op.signal = False
        op.sigval = 0
        op.dma_val = 0
        op.force = tuple(after)
        self.ops.append(op)
        for k in reads:
            self.readers.setdefault(k, []).append(idx)
        for k in writes:
            self.last_writer[k] = idx
            self.readers[k] = []
        return idx

    def finalize(self):
        ops = self.ops
        dma_count = {}
        for op in ops:
            if op.dma is not None:
                dma_count[op.dma] = dma_count.get(op.dma, 0) + 1
                op.dma_val = 16 * dma_count[op.dma]
        for op in ops:
            need_eng, need_dma = {}, {}
            for d, raw in op.deps.items():
                p = ops[d]
                if p.dma is not None:
                    if p.dma_val > need_dma.get(p.dma, 0):
                        need_dma[p.dma] = p.dma_val
                else:
                    if p.eng == op.eng and (op.eng == "pe" or (not raw and not STRICT_SAME_ENGINE)):
                        continue
                    if d > need_eng.get(p.eng, -1):
                        need_eng[p.eng] = d
            for d in op.force:
                if d > need_eng.get(ops[d].eng, -1):
                    need_eng[ops[d].eng] = d
            op.need_eng, op.need_dma = need_eng, need_dma
            for d in need_eng.values():
                ops[d].signal = True
        cnt = {}
        for op in ops:
            if op.dma is None and op.signal:
                cnt[op.eng] = cnt.get(op.eng, 0) + 1
                op.sigval = cnt[op.eng]
        self.dma_total = {k: 16 * v for k, v in dma_count.items()}

    def emit_engine(self, eng_name, e, eng_sems, dma_sems, final_dma_waits=()):
        ops = self.ops
        waited = {}
        for op in ops:
            if op.eng != eng_name:
                continue
            for pe_name, d in op.need_eng.items():
                v = ops[d].sigval
                key = ("e", pe_name)
                if waited.get(key, 0) < v:
                    e.wait_ge(eng_sems[pe_name], v)
                    waited[key] = v
            for k, v in op.need_dma.items():
                key = ("d", k)
                if waited.get(key, 0) < v:
                    e.wait_ge(dma_sems[k], v)
                    waited[key] = v
            ins = op.fn(e)
            if op.dma is not None:
                ins.then_inc(dma_sems[op.dma], 16)
            elif op.signal:
                ins.then_inc(eng_sems[eng_name], 1)
        for k in final_dma_waits:
            e.wait_ge(dma_sems[k], self.dma_total[k])


def _free_elems(ap):
    n = 1
    for d in ap.shape[1:]:
        n *= d
    return n


class _Rec:
    __slots__ = ("kind", "F", "info")

    def then_inc(self, *a, **k):
        return self


class _CostEng:
    def __init__(self):
        self.last = None

    def _r(self, kind, F, **info):
        r = _Rec()
        r.kind, r.F, r.info = kind, F, info
        self.last = r
        return r

    def matmul(self, out, lhsT, rhs, start=None, stop=None, skip_group_check=False, tile_position=None, **kw):
        return self._r("mm", _free_elems(rhs), K=lhsT.shape[0], M=_free_elems(lhsT),
                       rowbase=lhsT.base_partition(), colbase=out.base_partition())

    def transpose(self, out, in_, ident):
        return self._r("tr", 128)

    def activation(self, out, in_, func, bias=0.0, scale=1.0, accum_out=None, **kw):
        return self._r("act", _free_elems(in_), func=str(func), accum=accum_out is not None)

    def tensor_tensor(self, out, in0, in1, op, **kw):
        ps = ("bank" in in0.name) or ("bank" in in1.name)
        return self._r("tt", _free_elems(out), bf16=("bfloat16" in str(out.dtype)) and not ps)

    def tensor_copy(self, out, in_, **kw):
        return self._r("cp", _free_elems(out))

    def tensor_single_scalar(self, out, in_, scalar, op, **kw):
        return self._r("ts", _free_elems(out))

    def scalar_tensor_tensor(self, out, in0, scalar, in1, op0, op1, **kw):
        return self._r("stt", _free_elems(out))

    def reciprocal(self, out, in_, **kw):
        return self._r("rcp", _free_elems(out))

    def memset(self, ap, c):
        return self._r("ms", _free_elems(ap))

    def dma_start(self, out, in_, **kw):
        n = 1
        for d in out.shape:
            n *= d
        return self._r("dma", n)


def _op_dur(eng, r, state):
    k = r.kind
    if eng == "pe":
        if k == "tr":
            cfg, d = (128, 128), 56.0
        else:
            K_, M_, N_ = r.info["K"], r.info["M"], r.F
            cfg = (64 if 2 < K_ <= 64 else (32 if K_ <= 2 else 128), 64 if M_ <= 64 else 128)
            d = max(N_, 64) * 0.455 + 8
            prev = state.get("prevmm")
            if (prev is not None and prev["cfg"] == cfg and cfg != (128, 128) and cfg[0] != 32
                    and (prev["rb"], prev["cb"]) != (r.info["rowbase"], r.info["colbase"]) and not prev.get("paired")):
                state["prevmm"] = dict(cfg=cfg, rb=r.info["rowbase"], cb=r.info["colbase"], paired=True)
                return 4.0
        sw = 100.0 if state.get("cfg") not in (None, cfg) else 0.0
        state["cfg"] = cfg
        state["prevmm"] = dict(cfg=cfg, rb=r.info.get("rowbase", 0), cb=r.info.get("colbase", 0)) if k == "mm" else None
        return d + sw
    if eng == "act":
        if k == "dma":
            return 60.0
        return 0.88 * r.F + 110 + (93 if r.info.get("accum") else 0)
    if eng == "dve":
        if k == "rcp":
            d = 5.0 * r.F + 20
        elif k == "tt" and r.info.get("bf16"):
            d = 60 + 0.6 * r.F
        elif k in ("tt", "stt", "cp"):
            d = 70 + 1.2 * r.F
        else:
            d = 60 + 0.6 * r.F
        return max(270.0, 1.1 * d)
    if eng == "pool":
        return 900.0 if k == "dma" else 2.45 * r.F
    return 60.0


def _list_schedule(P, fixed=("sp", "pool"), SEM=70.0):
    ops = P.ops
    n = len(ops)
    engs_c = {}
    recs = []
    for op in ops:
        ce = engs_c.setdefault(op.eng, _CostEng())
        op.fn(ce)
        recs.append(ce.last)
    st = {}
    dur1 = [_op_dur(op.eng, recs[i], st.setdefault(op.eng, {})) for i, op in enumerate(ops)]
    node_of = [0] * n
    nodes = []
    for i, op in enumerate(ops):
        if op.eng == "pe" and nodes and ops[nodes[-1][-1]].eng == "pe" and nodes[-1][-1] == i - 1:
            nodes[-1].append(i)
        else:
            nodes.append([i])
        node_of[i] = len(nodes) - 1
    m = len(nodes)
    neng = [ops[nd[0]].eng for nd in nodes]
    ndur = [sum(dur1[i] for i in nd) for nd in nodes]
    nlat = [(2000.0 + recs[nd[0]].F * 4 / 200.0) if ops[nd[0]].dma is not None else 0.0 for nd in nodes]
    preds = [set() for _ in range(m)]
    for i, op in enumerate(ops):
        a = node_of[i]
        for d in list(op.deps.keys()) + list(op.force):
            b = node_of[d]
            if b != a:
                preds[a].add(b)
    last = {}
    for a in range(m):
        if neng[a] in fixed:
            if neng[a] in last:
                preds[a].add(last[neng[a]])
            last[neng[a]] = a
    succs = [[] for _ in range(m)]
    for a in range(m):
        for p in preds[a]:
            succs[p].append(a)
    prio = [0.0] * m
    for a in range(m - 1, -1, -1):
        mx = 0.0
        for s in succs[a]:
            if prio[s] > mx:
                mx = prio[s]
        prio[a] = ndur[a] + nlat[a] + mx
    npred = [len(p) for p in preds]
    ready_t = [0.0] * m
    avail = {}
    for a in range(m):
        if npred[a] == 0:
            avail.setdefault(neng[a], []).append(a)
    free = {}
    engs = sorted(set(neng))
    order_nodes = []
    while len(order_nodes) < m:
        best = None
        for e in engs:
            h = avail.get(e)
            if not h:
                continue
            t = free.get(e, 0.0)
            rdy = [a for a in h if ready_t[a] <= t]
            if rdy:
                a = max(rdy, key=lambda a: (prio[a], -a))
                s = t
            else:
                a = min(h, key=lambda a: (ready_t[a], -prio[a]))
                s = ready_t[a]
            if best is None or s < best[0] or (s == best[0] and prio[a] > prio[best[2]]):
                best = (s, e, a)
        s, e, a = best
        avail[e].remove(a)
        free[e] = s + ndur[a]
        fin = s + ndur[a] + nlat[a]
        order_nodes.append(a)
        for sc in succs[a]:
            npred[sc] -= 1
            same = neng[sc] == e and nlat[a] == 0.0
            r = fin + (0.0 if same else SEM)
            if r > ready_t[sc]:
                ready_t[sc] = r
            if npred[sc] == 0:
                avail.setdefault(neng[sc], []).append(sc)
    order = [i for a in order_nodes for i in nodes[a]]
    newpos = {old: new for new, old in enumerate(order)}
    new_ops = [ops[i] for i in order]
    for new, op in enumerate(new_ops):
        op.idx = new
        op.deps = {newpos[d]: raw for d, raw in op.deps.items()}
        op.force = tuple(newpos[d] for d in op.force)
    P.ops = new_ops


def build_program(nblk=NBLK, stages=("head", "attn", "tail"), dbg=False):
    nc = bass.Bass("TRN2", target_bir_lowering=False)

    def din(name, shape):
        return nc.dram_tensor(name, shape, F32, kind="ExternalInput").ap()

    x_d = din("x", [TOK, D])
    p_d = din("p", [TOK, PLE])
    w_in_d = din("w_in_r", [D, WALL])
    g1_d = din("g1", [128, 8])
    w_out_d = din("w_out", [D, D])
    w_gate_d = din("w_gate", [D, D])
    g2_d = din("g2", [128, 8])
    w_pp_d = din("w_pp", [PLE, D])
    gfin_d = din("gfin", [128, D])
    biasA_d = din("biasA", [128, 8 * 256])
    biasB_d = din("biasB", [128, 8 * 640])
    sink_d = din("sinkrow", [1, 512])
    ident_d = din("ident", [128, 128])
    out_d = nc.dram_tensor("out", [TOK, D], F32, kind="ExternalOutput").ap()

    P = Prog()
    es = ExitStack()
    with es:
        def sb(name, shape, dt):
            return es.enter_context(nc.sbuf_tensor(name, shape, dt))

        def ps(name, shape, dt):
            return es.enter_context(nc.psum_tensor(name, shape, dt))

        Wall = sb("Wall", [128, 8, WALL], BF16)
        Wout = sb("Wout", [128, 8, D], BF16)
        Wg = sb("Wg", [128, 8, D], BF16)
        Wp = sb("Wp", [128, 2, D], BF16)
        gfin = sb("gfin_t", [128, D], F32)
        EA = sb("EA", [128, 8, 256], BF16)
        EB = sb("EB", [128, 8, 640], BF16)
        g1 = sb("g1_t", [128, 8], F32)
        g2 = sb("g2_t", [128, 8], F32)
        ident = sb("ident_t", [128, 128], BF16)
        ones2 = sb("ones2", [128, 64], BF16)
        onesrow = sb("onesrow", [2, 256], BF16)
        skhl = sb("skhl", [2, 512], BF16)
        xring = sb("xring", [128, XR * D], F32)
        junk = sb("junk", [128, D], mybir.dt.float8e4)
        ubuf = [sb(f"ubuf{i}", [128, D], BF16) for i in range(2)]
        uT = sb("uT", [128, 8, TB], BF16)
        Qa = [sb(f"Qa{i}", [128, 4, TB], BF16) for i in range(2)]
        Qb = [sb(f"Qb{i}", [128, 4, TB], BF16) for i in range(2)]
        GT = [sb(f"GT{i}", [128, 8, TB], BF16) for i in range(2)]
        KaT = sb("KaT", [128, 2, 1024], BF16)
        KbT = sb("KbT", [128, 4, 1024], BF16)
        Vr = sb("Vr", [128, 8, 640], BF16)
        pbf = [sb(f"pbf{i}", [128, NT, PLE], BF16) for i in range(2)]
        pT = sb("pT", [128, 2, TB], BF16)
        Pt = [sb(f"Pt{i}", [128, 2, 2, 256], BF16) for i in range(3)]
        Rn = [sb("Rn0", [128, 256], F32)] * 2
        tgs = [sb("tgs0", [128, 512], F32)] * 2
        tnh = [sb(f"tnh{i}", [128, 512], BF16) for i in range(2)]
        skhi, sklo = tnh[0][0:1, :], tnh[1][0:1, :]
        ss = sb("ss", [128, 3 * NT], F32)
        sd = sb("sd", [128, 3 * NT], F32)
        rs = sb("rs", [128, 3 * NT], F32)

        banks = [ps(f"bank{i}", [128, 512], F32) for i in range(8)]
        Sb = banks[0:4]
        Ab = banks[4:6]
        Gb = banks[6:8]
        Tb = banks[7][:, :].bitcast(BF16).rearrange("p (a b) -> p a b", a=8)
        NGB = 2

        eng_names = ["pe", "act", "dve", "pool"]
        eng_sems = {n: es.enter_context(nc.semaphore("sem_" + n)) for n in eng_names}
        dma_keys = ([f"xl{i}" for i in range(XR)] + [f"xs{i}" for i in range(XR)]
                    + ["p0", "p1", "stg0", "stg1", "wout", "wp", "gfin", "ident", "g1", "g2", "sink"])
        dma_sems = {k: es.enter_context(nc.semaphore("dsem_" + k)) for k in dma_keys}

        gb_rot = [0]

        def next_gb():
            b = gb_rot[0] % NGB
            gb_rot[0] += 1
            return b

        def dma(eng, key, out, in_, reads, writes):
            P.add(eng, lambda e, o=out, i=in_: e.dma_start(out=o, in_=i), reads, writes, dma=key)

        dma("pool", "ident", ident[:], ident_d, [], [("ident",)])
        dma("sp", "g1", g1[:], g1_d, [], [("g1",)])
        dma("sp", "g2", g2[:], g2_d, [], [("g2",)])
        sk32 = tgs[0][0:1, :]
        dma("sp", "sink", sk32, sink_d, [], [("sk32",), ("tgs", 0)])
        dma("sp", "gfin", gfin[:], gfin_d, [], [("gfin",)])
        P.add("dve", lambda e: e.memset(ones2[:], 2.0), [], [("ones2",)])
        P.add("dve", lambda e: e.memset(onesrow[:], 1.0), [], [("onesrow",)])

        stg_i = [0]

        def staged(src_ap, ncols, consume):
            s = stg_i[0] % 2
            stg_i[0] += 1
            st_ap = xring[:, s * STG: s * STG + ncols]
            dma("sp", f"stg{s}", st_ap, src_ap, [], [("stg", s)])
            consume(st_ap, s)

        flip = [0]

        def scale_cast(out_ap, in_ap, scal_ap, s, extra_reads, writes):
            if flip[0] % 2 == 0:
                P.add("dve", lambda e: e.tensor_single_scalar(out=out_ap, in_=in_ap, scalar=scal_ap, op=ALU.mult),
                      [("stg", s)] + extra_reads, writes)
            else:
                P.add("act", lambda e: e.activation(out=out_ap, in_=in_ap, func=AF.Copy, scale=scal_ap),
                      [("stg", s)] + extra_reads, writes)
            flip[0] += 1

        for c in range(8):
            for hp in range(2):
                c0 = hp * (WALL // 2)
                n = WALL // 2
                staged(w_in_d[c * 128:(c + 1) * 128, c0:c0 + n], n,
                       lambda st, s, c=c, c0=c0, n=n: scale_cast(Wall[:, c, c0:c0 + n], st, g1[:, c:c + 1], s,
                                                                 [("g1",)], [("Wall", c, c0)]))
        for c in range(8):
            staged(w_gate_d[c * 128:(c + 1) * 128, :], D,
                   lambda st, s, c=c: scale_cast(Wg[:, c, :], st, g2[:, c:c + 1], s, [("g2",)], [("Wg", c)]))
        for hh in range(2):
            def consA(st, s, hh=hh):
                for q in range(4):
                    h = hh * 4 + q
                    P.add("act", lambda e, h=h, q=q: e.activation(out=EA[:, h, :], in_=st[:, q * 256:(q + 1) * 256], func=AF.Exp),
                          [("stg", s)], [("EA", h)])
            staged(biasA_d[:, hh * 1024:(hh + 1) * 1024], 1024, consA)
        for hh in range(2):
            def consB(st, s, hh=hh):
                for q in range(4):
                    h = hh * 4 + q
                    P.add("act", lambda e, h=h, q=q: e.activation(out=EB[:, h, :], in_=st[:, q * 640:(q + 1) * 640], func=AF.Exp),
                          [("stg", s)], [("EB", h)])
            staged(biasB_d[:, hh * 2560:(hh + 1) * 2560], 2560, consB)
        dma("pool", "wout", Wout[:], w_out_d.rearrange("(c p) n -> p c n", p=128), [], [("Wout",)])
        dma("pool", "wp", Wp[:], w_pp_d.rearrange("(c p) n -> p c n", p=128), [], [("Wp",)])
        P.add("act", lambda e: e.activation(out=sk32, in_=sk32, func=AF.Exp), [("sk32",)], [("sk32",), ("tgs", 0)])
        P.add("dve", lambda e: e.tensor_single_scalar(out=sk32, in_=sk32, scalar=2.0, op=ALU.mult), [("sk32",)], [("sk32",), ("tgs", 0)])
        P.add("dve", lambda e: e.tensor_copy(out=skhi, in_=sk32), [("sk32",)], [("skhi",), ("tnh", 0)])
        P.add("dve", lambda e: e.tensor_tensor(out=sklo, in0=sk32, in1=skhi, op=ALU.subtract), [("sk32",), ("skhi",), ("tgs", 0)], [("sklo",), ("tnh", 1)])
        dma_sems["skc"] = es.enter_context(nc.semaphore("dsem_skc"))
        dma("sp", "skc", skhl[0:1, :], skhi, [("skhi",), ("tnh", 0)], [("skhl", 0)])
        dma("sp", "skc", skhl[1:2, :], sklo, [("sklo",), ("tnh", 1)], [("skhl", 1)])

        wall_keys = [("Wall", c, c0) for c in range(8) for c0 in (0, WALL // 2)]
        wg_keys = [("Wg", c) for c in range(8)]

        def xslot_keys(slot):
            ks = [("x", slot)]
            lo, hi = slot * D, (slot + 1) * D
            for s in range(2):
                if lo < (s + 1) * STG and hi > s * STG:
                    ks.append(("stg", s))
            return ks

        def load_x_tile(J, t):
            g = J * NT + t
            slot = g % XR
            dma("sp", f"xl{slot}", xring[:, slot * D:(slot + 1) * D], x_d[g * 128:(g + 1) * 128, :],
                [], xslot_keys(slot))

        def load_x(J):
            for t in range(NT):
                load_x_tile(J, t)

        EPS_AP = sb("eps_t", [128, 1], F32)
        P.add("dve", lambda e: e.memset(EPS_AP[:], EPS), [], [("eps",)])
        u2T = sb("u2T", [128, 8, TB], BF16)

        def xtile(J, t):
            g = J * NT + t
            slot = g % XR
            return xring[:, slot * D:(slot + 1) * D], slot, g

        def sq_stats(J, t, col0):
            xs, slot, g = xtile(J, t)
            P.add("act", lambda e: e.activation(out=junk[:], in_=xs, func=AF.Square, accum_out=ss[:, col0 + t:col0 + t + 1], saturate=False),
                  [("x", slot)], [("junk",), ("ss", col0 + t)])

        def sqrt_recip(col0, ncol):
            groups = sorted({(cc // NT) * NT for cc in range(col0, col0 + ncol)})
            P.add("act", lambda e: e.activation(out=sd[:, col0:col0 + ncol], in_=ss[:, col0:col0 + ncol], func=AF.Sqrt,
                                                bias=EPS_AP[:, 0:1], scale=1.0 / D),
                  [("ss", cc) for cc in range(col0, col0 + ncol)] + [("eps",)], [("sd", gq) for gq in groups])
            P.add("dve", lambda e: e.reciprocal(out=rs[:, col0:col0 + ncol], in_=sd[:, col0:col0 + ncol]),
                  [("sd", gq) for gq in groups], [("rs", gq) for gq in groups])

        def scale_u(J, t, col0):
            xs, slot, g = xtile(J, t)
            us = t % 2
            P.add("act", lambda e: e.activation(out=ubuf[us][:], in_=xs, func=AF.Copy, scale=rs[:, col0 + t:col0 + t + 1]),
                  [("x", slot), ("rs", col0)], [("u", us)])

        def transposes_u(t, dstT, dkey):
            us = t % 2
            for cc in range(8):
                P.add("pe", lambda e, cc=cc: e.transpose(Tb[:, cc, :], ubuf[us][:, cc * 128:(cc + 1) * 128], ident[:]),
                      [("u", us), ("ident",)], [("psG", 1)])
            P.add("dve", lambda e: e.tensor_copy(out=dstT[:, :, t * 128:(t + 1) * 128], in_=Tb[:, 0:8, :]),
                  [("psG", 1)], [(dkey, t)])

        def transposes_p(J, t):
            par = J % 2
            for cc in range(2):
                P.add("pe", lambda e, cc=cc: e.transpose(Tb[:, cc, :], pbf[par][:, t, cc * 128:(cc + 1) * 128], ident[:]),
                      [("pbf", par), ("ident",)], [("psG", 1)])
            P.add("dve", lambda e: e.tensor_copy(out=pT[:, :, t * 128:(t + 1) * 128], in_=Tb[:, 0:2, :]),
                  [("psG", 1)], [("pT", t)])

        def head_io(J, do_x=True, do_p=True):
            par = J % 2
            if do_x and J + 1 < nblk:
                load_x(J + 1)
            if do_p:
                dma("pool", f"p{par}", pbf[par][:], p_d[J * TB:(J + 1) * TB, :].rearrange("(t q) f -> q t f", q=128),
                    [], [("pbf", par)])

        FM = [("Qa", 0, 0), ("Qa", 2, 2), ("Ka", 0, 4), ("Ga", 0, 6), ("Ga", 2, 8), ("Qb", 0, 10), ("Qb", 2, 12),
              ("Kb", 0, 14), ("Kb", 2, 16), ("Gb", 0, 18), ("Gb", 2, 20)]
        uT_keys = [("uT", t) for t in range(NT)]

        def proj_fm(J, n_ev):
            par = J % 2
            ringcol = (J * TB) % 1024
            rt0 = (J * NT) % 8
            kind, ci, f0 = FM[n_ev]
            gb = next_gb()
            for i in range(2):
                for cc in range(8):
                    P.add("pe", lambda e, i=i, cc=cc: e.matmul(
                        Gb[gb][:, i * TB:(i + 1) * TB], Wall[:, cc, (f0 + i) * 128:(f0 + i + 1) * 128], uT[:, cc, :],
                        start=(cc == 0), stop=(cc == 7)), wall_keys + uT_keys, [("psG", gb)])
            src3 = Gb[gb][:, :].rearrange("p (a b) -> p a b", a=2)
            if kind in ("Qa", "Qb", "Ka", "Kb"):
                if kind in ("Qa", "Qb"):
                    dst = (Qa if kind == "Qa" else Qb)[par][:, ci:ci + 2, :]
                    wk = [(kind, par, ci), (kind, par, ci + 1)]
                else:
                    dst = (KaT if kind == "Ka" else KbT)[:, ci:ci + 2, ringcol:ringcol + TB]
                    wk = [(kind, ci, (rt0 + t) % 8) for t in range(NT)]
                if n_ev % 2 == 0:
                    P.add("dve", lambda e: e.tensor_copy(out=dst, in_=src3), [("psG", gb)], wk)
                else:
                    P.add("act", lambda e: e.activation(out=dst, in_=src3, func=AF.Copy), [("psG", gb)], wk)
            else:
                gc = ci + (0 if kind == "Ga" else 4)
                ts_ = n_ev % 2
                tn3 = tnh[ts_][:, :].rearrange("p (a b) -> p a b", a=2)
                P.add("act", lambda e: e.activation(out=tn3, in_=src3, func=AF.Tanh, scale=0.5), [("psG", gb)], [("tnh", ts_)])
                dst = GT[par][:, gc:gc + 2, :]
                P.add("dve", lambda e: e.scalar_tensor_tensor(out=dst, in0=tn3, scalar=1.0, in1=src3, op0=ALU.add, op1=ALU.mult),
                      [("psG", gb), ("tnh", ts_)], [("GT", par, gc), ("GT", par, gc + 1)])

        def proj_v(J, t, which):
            rt = ((J * NT) % 8 + t) % 8
            gb = next_gb()
            if which == 0:
                for cc in range(8):
                    P.add("pe", lambda e, cc=cc: e.matmul(
                        Gb[gb][:, :], uT[:, cc, t * 128:(t + 1) * 128], Wall[:, cc, NFM * 128:NFM * 128 + 512],
                        start=(cc == 0), stop=(cc == 7)), wall_keys + [("uT", t)], [("psG", gb)])
                P.add("act", lambda e: e.activation(out=Vr[:, rt, 0:512], in_=Gb[gb][:, :], func=AF.Copy), [("psG", gb)], [("Vb", rt)])
            else:
                for cc in range(8):
                    P.add("pe", lambda e, cc=cc: e.matmul(
                        Gb[gb][:, 0:128], uT[:, cc, t * 128:(t + 1) * 128], Wall[:, cc, NFM * 128 + 512:NFM * 128 + 640],
                        start=(cc == 0), stop=(cc == 7)), wall_keys + [("uT", t)], [("psG", gb)])
                P.add("dve", lambda e: e.tensor_copy(out=Vr[:, rt, 512:640], in_=Gb[gb][:, 0:128]), [("psG", gb)], [("Va", rt)])

        def wout(J, t, half):
            par = J % 2
            xs, slot, g = xtile(J, t)
            gb = next_gb()
            for cc in range(8):
                ysrc = (Qa if cc < 4 else Qb)[par][:, cc % 4, t * 128:(t + 1) * 128]
                yk = [("Qa" if cc < 4 else "Qb", par, cc % 4)]
                P.add("pe", lambda e, cc=cc, ysrc=ysrc: e.matmul(
                    Gb[gb][:, :], ysrc, Wout[:, cc, half * 512:(half + 1) * 512], start=(cc == 0), stop=(cc == 7)),
                    yk + [("Wout",)], [("psG", gb)])
            P.add("dve", lambda e: e.tensor_tensor(out=xs[:, half * 512:(half + 1) * 512], in0=Gb[gb][:, :],
                                                  in1=xs[:, half * 512:(half + 1) * 512], op=ALU.add),
                  [("psG", gb), ("x", slot)], [("x", slot)])

        def gate(J, t, half):
            xs, slot, g = xtile(J, t)
            gbg = next_gb()
            for cc in range(8):
                P.add("pe", lambda e, cc=cc: e.matmul(
                    Gb[gbg][:, :], u2T[:, cc, t * 128:(t + 1) * 128], Wg[:, cc, half * 512:(half + 1) * 512],
                    start=(cc == 0), stop=(cc == 7)), [("u2T", t)] + wg_keys, [("psG", gbg)])
            ti = 0
            P.add("act", lambda e: e.activation(out=tgs[ti][:], in_=Gb[gbg][:, :], func=AF.Tanh, scale=0.5),
                  [("psG", gbg)], [("tgs", ti)])
            gbp = next_gb()
            for cc in range(2):
                P.add("pe", lambda e, cc=cc: e.matmul(
                    Gb[gbp][:, :], pT[:, cc, t * 128:(t + 1) * 128], Wp[:, cc, half * 512:(half + 1) * 512],
                    start=(cc == 0), stop=(cc == 1)), [("pT", t), ("Wp",)], [("psG", gbp)])
            P.add("dve", lambda e: e.scalar_tensor_tensor(out=tgs[ti][:], in0=tgs[ti][:], scalar=1.0, in1=Gb[gbp][:, :],
                                                         op0=ALU.add, op1=ALU.mult),
                  [("tgs", ti), ("psG", gbp)], [("tgs", ti)])
            P.add("dve", lambda e: e.scalar_tensor_tensor(out=xs[:, half * 512:(half + 1) * 512], in0=tgs[ti][:], scalar=0.5,
                                                         in1=xs[:, half * 512:(half + 1) * 512], op0=ALU.mult, op1=ALU.add),
                  [("tgs", ti), ("x", slot)], [("x", slot)])

        def final(J, t):
            xs, slot, g = xtile(J, t)
            P.add("dve", lambda e: e.scalar_tensor_tensor(out=xs, in0=xs, scalar=rs[:, 2 * NT + t:2 * NT + t + 1], in1=gfin[:],
                                                         op0=ALU.mult, op1=ALU.mult),
                  [("x", slot), ("rs", 2 * NT), ("gfin",)], [("x", slot)])
            dma("sp", f"xs{slot}", out_d[g * 128:(g + 1) * 128, :], xs, [("x", slot)], [])

        def head(J):
            head_io(J)
            for t in range(NT):
                sq_stats(J, t, 0)
            sqrt_recip(0, NT)
            yield
            for t in range(NT):
                scale_u(J, t, 0)
                transposes_u(t, uT, "uT")
                yield
            for n_ev in range(len(FM)):
                proj_fm(J, n_ev)
                yield
            for t in range(NT):
                for which in range(2):
                    proj_v(J, t, which)
                    yield

        def tail(J):
            for t in range(NT):
                for half in range(2):
                    wout(J, t, half)
                    yield
                sq_stats(J, t, NT)
            sqrt_recip(NT, NT)
            yield
            for t in range(NT):
                scale_u(J, t, NT)
                transposes_u(t, u2T, "u2T")
                yield
                transposes_p(J, t)
                yield
            for t in range(NT):
                for half in range(2):
                    gate(J, t, half)
                    yield
                sq_stats(J, t, 2 * NT)
            sqrt_recip(2 * NT, NT)
            yield
            for t in range(NT):
                final(J, t)
                yield

        def stream2(J):
            H = J + 2
            if J + 3 < nblk:
                load_x(J + 3)
            for t in range(NT):
                for half in range(2):
                    wout(J, t, half)
                    yield
                sq_stats(J, t, NT)
            for t in range(NT):
                sq_stats(H, t, 0)
            sqrt_recip(0, 2 * NT)
            yield
            for t in range(NT):
                scale_u(H, t, 0)
            for t in range(NT):
                transposes_p(J, t)
                yield "switch"
            head_io(H, do_x=False)
            for t in range(NT):
                transposes_u(t, uT, "uT")
                yield "switch"
            for t in range(NT):
                scale_u(J, t, NT)
            for n_ev in range(0, 3):
                proj_fm(H, n_ev)
                yield
            for t in range(NT):
                transposes_u(t, u2T, "u2T")
                yield "switch"
            for n_ev in range(3, 6):
                proj_fm(H, n_ev)
                yield
            rest_h = [lambda n_ev=n_ev: proj_fm(H, n_ev) for n_ev in range(6, len(FM))] + \
                     [lambda t=t, which=which: proj_v(H, t, which) for t in range(NT) for which in range(2)]
            gates = [(t, half) for t in range(NT) for half in range(2)]
            gi = 0
            for i, fn in enumerate(rest_h):
                fn()
                yield
                if i % 2 == 1 and gi < len(gates):
                    t, half = gates[gi]
                    gate(J, t, half)
                    if half == 1:
                        sq_stats(J, t, 2 * NT)
                    gi += 1
                    yield
            while gi < len(gates):
                t, half = gates[gi]
                gate(J, t, half)
                if half == 1:
                    sq_stats(J, t, 2 * NT)
                gi += 1
                yield
            sqrt_recip(2 * NT, NT)
            yield
            for t in range(NT):
                final(J, t)
                yield

        pair_ctr = [0]
        unit_ctr = [0]

        def attn(J):
            par = J % 2
            s = J // BLK_PER_SEQ
            jl = J % BLK_PER_SEQ
            units = []
            for mixer in ("A", "B"):
                for j in range(4):
                    if mixer == "A":
                        kps, span = range(2 * jl - 1, 2 * jl + 2), 256
                    else:
                        kps, span = range(2 * jl - 4, 2 * jl + 2), 640
                    ul = []
                    for kp in kps:
                        if kp < 0:
                            continue
                        q_lo = max(128 * kp, TB * jl)
                        q_hi = min(128 * kp + span, TB * jl + TB)
                        if q_hi <= q_lo:
                            continue
                        ul.append((kp, q_lo - 128 * kp, q_hi - q_lo, q_lo - TB * jl))
                    pid = pair_ctr[0]
                    pair_ctr[0] += 1
                    for i, u in enumerate(ul):
                        uid = unit_ctr[0]
                        unit_ctr[0] += 1
                        units.append(dict(mixer=mixer, j=j, kp=u[0], r0=u[1], w=u[2], c0=u[3],
                                          first=(i == 0), last=(i == len(ul) - 1), pid=pid, uid=uid))

            sus = [units[i:i + 2] for i in range(0, len(units), 2)]

            def qk(m):
                X, Y = Sb[2 * (m % 2)], Sb[2 * (m % 2) + 1]
                for i, u in enumerate(sus[m]):
                    mixer, j, w, c0 = u["mixer"], u["j"], u["w"], u["c0"]
                    rt = (s * 16 + u["kp"]) % 8
                    if mixer == "A":
                        kv = j // 2
                        Kt = KaT[:, kv, rt * 128:(rt + 1) * 128]
                        Qt = Qa[par][:, j, c0:c0 + w]
                        rk = [("Ka", 0, rt), ("Qa", par, j)]
                    else:
                        Kt = KbT[:, j, rt * 128:(rt + 1) * 128]
                        Qt = Qb[par][:, j, c0:c0 + w]
                        rk = [("Kb", (j // 2) * 2, rt), ("Qb", par, j)]
                    for hh, bank in ((0, X), (1, Y)):
                        P.add("pe", lambda e, hh=hh, bank=bank, Kt=Kt, Qt=Qt, i=i, w=w: e.matmul(
                            bank[:, i * 256:i * 256 + w], Kt[hh * 64:(hh + 1) * 64, :], Qt[hh * 64:(hh + 1) * 64, :],
                            start=True, stop=True), rk, [("psS", 2 * (m % 2) + hh)])

            def front(m):
                su = sus[m]
                pt = m % 3
                nu = len(su)
                wmax = max(u["w"] for u in su)
                pk = [("Pt", pt, i) for i in range(nu)]
                same_w = all(u["w"] == wmax for u in su)
                for hh in range(2):
                    bank3 = Sb[2 * (m % 2) + hh][:, :].rearrange("p (a b) -> p a b", a=2)
                    bk = [("psS", 2 * (m % 2) + hh)]
                    if same_w:
                        P.add("act", lambda e, hh=hh, bank3=bank3: e.activation(
                            out=Pt[pt][:, 0:nu, hh, 0:wmax], in_=bank3[:, 0:nu, 0:wmax], func=AF.Exp, scale=0.125), bk, pk)
                    else:
                        for i, u in enumerate(su):
                            w = u["w"]
                            P.add("act", lambda e, hh=hh, bank3=bank3, i=i, w=w: e.activation(
                                out=Pt[pt][:, i, hh, 0:w], in_=bank3[:, i, 0:w], func=AF.Exp, scale=0.125), bk, [("Pt", pt, i)])
                for i, u in enumerate(su):
                    mixer, j, w, r0 = u["mixer"], u["j"], u["w"], u["r0"]
                    Et = EA if mixer == "A" else EB
                    ek = [("EA" if mixer == "A" else "EB", 2 * j), ("EA" if mixer == "A" else "EB", 2 * j + 1)]
                    P.add(EMUL_ENG, lambda e, i=i, j=j, w=w, r0=r0, Et=Et: e.tensor_tensor(
                        out=Pt[pt][:, i, :, 0:w], in0=Pt[pt][:, i, :, 0:w], in1=Et[:, 2 * j:2 * j + 2, r0:r0 + w], op=ALU.mult),
                        [("Pt", pt, i)] + ek, [("Pt", pt, i)])

            def back(m):
                pt = m % 3
                for i, u in enumerate(sus[m]):
                    back_unit(u, Pt[pt], i, pt)

            def back_unit(u, Ptile, i, pt):
                mixer, j, w, c0 = u["mixer"], u["j"], u["w"], u["c0"]
                rt = (s * 16 + u["kp"]) % 8
                ab = u["pid"] % 2
                if mixer == "A":
                    kv = j // 2
                    V0 = Vr[:, rt, 512 + kv * 64:512 + kv * 64 + 64]
                    V1 = V0
                    vk = [("Va", rt)]
                else:
                    V0 = Vr[:, rt, (2 * j) * 64:(2 * j + 1) * 64]
                    V1 = Vr[:, rt, (2 * j + 1) * 64:(2 * j + 2) * 64]
                    vk = [("Vb", rt)]
                first = u["first"]
                acc = Ab[ab]
                P.add("pe", lambda e: e.matmul(acc[0:64, c0:c0 + w], V0, Ptile[:, i, 0, 0:w], start=first, stop=True,
                                               skip_group_check=True, tile_position=(0, 0)),
                      [("Pt", pt, i)] + vk, [("psA", ab)])
                P.add("pe", lambda e: e.matmul(acc[64:128, c0:c0 + w], V1, Ptile[:, i, 1, 0:w], start=first, stop=True,
                                               skip_group_check=True, tile_position=(0, 64)),
                      [("Pt", pt, i)] + vk, [("psA", ab)])
                P.add("pe", lambda e: e.matmul(acc[0:64, 256 + c0:256 + c0 + w], ones2[:, 0:64], Ptile[:, i, 0, 0:w],
                                               start=False, stop=True, skip_group_check=True, tile_position=(0, 0)),
                      [("Pt", pt, i), ("ones2",)], [("psA", ab)])
                P.add("pe", lambda e: e.matmul(acc[64:128, 256 + c0:256 + c0 + w], ones2[:, 0:64], Ptile[:, i, 1, 0:w],
                                               start=False, stop=True, skip_group_check=True, tile_position=(0, 64)),
                      [("Pt", pt, i), ("ones2",)], [("psA", ab)])
                if not u["last"]:
                    return
                if mixer == "A":
                    P.add("pe", lambda e: e.matmul(acc[:, 256:512], skhl[0:2, j * 128:(j + 1) * 128], onesrow[0:2, :],
                                                   start=False, stop=True, skip_group_check=True),
                          [("skhl", 0), ("skhl", 1), ("onesrow",)], [("psA", ab)])
                rn = 0
                gc = j if mixer == "A" else 4 + j
                ydst = (Qa if mixer == "A" else Qb)[par][:, j, :]
                yk = [("Qa" if mixer == "A" else "Qb", par, j)]
                P.add("dve", lambda e: e.reciprocal(out=Rn[rn][:], in_=acc[:, 256:512]), [("psA", ab)], [("Rn", rn)])
                P.add("dve", lambda e: e.tensor_tensor(out=Rn[rn][:], in0=Rn[rn][:], in1=GT[par][:, gc, :], op=ALU.mult),
                      [("Rn", rn), ("GT", par, gc)], [("Rn", rn)])
                P.add("dve", lambda e: e.tensor_tensor(out=ydst, in0=acc[:, 0:256], in1=Rn[rn][:], op=ALU.mult),
                      [("psA", ab), ("Rn", rn)], yk)

            M = len(sus)
            LAG = 2
            for k in range(0, M + LAG, 2):
                for m in (k, k + 1):
                    if m < M:
                        qk(m)
                for m in (k, k + 1):
                    if LAG <= m < M + LAG:
                        back(m - LAG)
                for m in (k, k + 1):
                    if m < M:
                        front(m)
                yield

        def drain(g):
            for _ in g:
                pass

        def interleave(ga, gb, nb=1):
            da = db = False
            while not (da and db):
                if not da:
                    try:
                        next(ga)
                    except StopIteration:
                        da = True
                for _ in range(nb):
                    if not db:
                        try:
                            if next(gb) == "switch" and not da:
                                break
                        except StopIteration:
                            db = True

        def chain(*gens):
            for g in gens:
                yield from g

        def empty():
            return
            yield

        load_x(0)
        drain(head(0))
        if nblk > 1:
            interleave(attn(0), head(1))
        else:
            drain(attn(0))
        for J in range(nblk):
            s2 = stream2(J) if J + 2 < nblk else tail(J)
            if J + 1 < nblk:
                interleave(attn(J + 1), s2, nb=2)
            else:
                drain(s2)
        if dbg:
            allk = list(P.last_writer.keys())
            dumps = dict(uT=uT, Qa0=Qa[0], Qb0=Qb[0], GT0=GT[0], KaT=KaT, KbT=KbT, Vr=Vr, EA=EA, EB=EB, xring=xring,
                         Wg=Wg, Wp=Wp, Wout=Wout, skhi=skhi, sklo=sklo, rs=rs, pT=pT)
            for nm, tl in dumps.items():
                shp = list(tl[:].shape)
                dd = nc.dram_tensor("dbg_" + nm, shp, tl[:].dtype, kind="ExternalOutput").ap()
                key = "dbg_" + nm
                dma_sems[key] = es.enter_context(nc.semaphore("dsem_" + key))
                dma("sp", key, dd, tl[:], allk, [])

        if RESCHED:
            _list_schedule(P)
        P.finalize()

        with nc.Block() as block:
            @block.sync
            def _(e):
                fw = [k for k in P.dma_total if k.startswith("xs") or k.startswith("dbg_")]
                P.emit_engine("sp", e, eng_sems, dma_sems, final_dma_waits=fw)

            @block.gpsimd
            def _(e):
                P.emit_engine("pool", e, eng_sems, dma_sems)

            @block.tensor
            def _(e):
                P.emit_engine("pe", e, eng_sems, dma_sems)

            @block.scalar
            def _(e):
                P.emit_engine("act", e, eng_sems, dma_sems)

            @block.vector
            def _(e):
                P.emit_engine("dve", e, eng_sems, dma_sems)
    return nc


def _host_layout(x, p, norm_g, w_in, sink_a, rel_bias_b, w_out, ple_norm_g, w_ple_proj, w_ple_gate, final_norm_g):
    f = np.float32
    w = np.asarray(w_in, f)[0]
    qa, ka, va, ga = w[:, 0:512], w[:, 512:640], w[:, 640:768], w[:, 768:1280]
    qb, kb, vb, gb = w[:, 1280:1792], w[:, 1792:2304], w[:, 2304:2816], w[:, 2816:3328]
    ka_dup = np.concatenate([ka[:, 0:64], ka[:, 0:64], ka[:, 64:128], ka[:, 64:128]], axis=1)
    w_in_r = np.ascontiguousarray(np.concatenate([qa, ka_dup, ga, qb, kb, gb, vb, va], axis=1))
    assert w_in_r.shape == (D, WALL)
    g1 = np.ascontiguousarray(np.asarray(norm_g, f)[0].reshape(8, 128).T)
    g2 = np.ascontiguousarray(np.asarray(ple_norm_g, f)[0].reshape(8, 128).T)
    gfin = np.ascontiguousarray(np.tile(np.asarray(final_norm_g, f)[None, :], (128, 1)))
    ki = np.arange(128)[:, None]
    rA = np.arange(256)[None, :]
    slopes = np.asarray(2.0 ** (-8.0 * np.arange(1, 9) / 8), dtype=f)
    distA = np.abs(rA - ki).astype(f)
    maskA = ((ki >= 64) & (rA < 64)) | ((ki < 64) & (rA >= 192))
    biasA = np.empty((128, 8, 256), f)
    for h in range(8):
        biasA[:, h, :] = np.where(maskA, f(MASK_NEG), -slopes[h] * distA)
    rB = np.arange(640)[None, :]
    idx = np.clip(rB - ki, -128, 128) + 128
    maskB = ((ki >= 64) & (rB < 64)) | ((ki < 64) & (rB >= 576))
    tab = np.asarray(rel_bias_b, f)[0]
    biasB = np.empty((128, 8, 640), f)
    for h in range(8):
        biasB[:, h, :] = np.where(maskB, f(MASK_NEG), tab[h][idx])
    sk = np.asarray(sink_a, f)[0]
    sinkrow = np.empty((1, 512), f)
    for j in range(4):
        sinkrow[0, j * 128:j * 128 + 64] = sk[2 * j]
        sinkrow[0, j * 128 + 64:(j + 1) * 128] = sk[2 * j + 1]
    shared = dict(
        w_in_r=w_in_r, g1=g1, w_out=np.ascontiguousarray(np.asarray(w_out, f)[0]),
        w_gate=np.ascontiguousarray(np.asarray(w_ple_gate, f)[0]), g2=g2,
        w_pp=np.ascontiguousarray(np.asarray(w_ple_proj, f)[0]), gfin=gfin,
        biasA=np.ascontiguousarray(biasA.reshape(128, 8 * 256)), biasB=np.ascontiguousarray(biasB.reshape(128, 8 * 640)),
        sinkrow=sinkrow, ident=np.eye(128, dtype=f),
    )
    xf = np.asarray(x, f).reshape(NCORES, TOK, D)
    pf = np.asarray(p, f)[0].reshape(NCORES, TOK, PLE)
    return [dict(shared, x=np.ascontiguousarray(xf[c]), p=np.ascontiguousarray(pf[c])) for c in range(NCORES)]


_NC_CACHE = {}


def kernel(x, p, norm_g, w_in, sink_a, rel_bias_b, w_out, ple_norm_g, w_ple_proj, w_ple_gate, final_norm_g):
    in_maps = _host_layout(x, p, norm_g, w_in, sink_a, rel_bias_b, w_out, ple_norm_g, w_ple_proj, w_ple_gate, final_norm_g)
    if "nc" not in _NC_CACHE:
        _NC_CACHE["nc"] = build_program()
    res = run_bass_kernel_spmd(_NC_CACHE["nc"], in_maps, core_ids=list(range(NCORES)))
    outs = [np.asarray(r["out"], np.float32).reshape(SEQ_PER_CORE, SEQ, D) for r in res.results]
    return np.concatenate(outs, axis=0)
```
